# Optimizing a Trainium2 kernel written in Bass

```python
import math
import jax, jax.numpy as jnp
from jax import lax
import numpy as np

D_MODEL = 1024
BATCH = 2
SEQ = 8192
DEPTH = 2

ATTN_HEADS = 8
ATTN_KV_HEADS = 2
ATTN_HEAD_DIM = D_MODEL // ATTN_HEADS
ATTN_WIDTH = ATTN_HEADS * ATTN_HEAD_DIM
ATTN_KV_WIDTH = ATTN_KV_HEADS * ATTN_HEAD_DIM
ATTN_ROT_DIM = ATTN_HEAD_DIM // 4
IDX_HEADS = 8
IDX_DIM = 64
IDX_ROT_DIM = IDX_DIM // 4
INDEX_TOPK = 256
Q_BLOCK = 128
ROPE_THETA = 500000.0
GDN_HEADS = 8
GDN_DK = 128
GDN_DV = 128
GDN_QK_WIDTH = GDN_HEADS * GDN_DK
GDN_WIDTH = GDN_HEADS * GDN_DV
GDN_QKV_WIDTH = 2 * GDN_QK_WIDTH + GDN_WIDTH
CONV_K = 4
CHUNK = 64
N_BRANCH = 2
NORM_EPS = 1e-6

IN_SPLITS = [
    ATTN_WIDTH,
    ATTN_KV_WIDTH,
    ATTN_KV_WIDTH,
    ATTN_WIDTH,
    IDX_HEADS * IDX_DIM,
    IDX_DIM,
    IDX_HEADS,
    GDN_QKV_WIDTH,
    GDN_WIDTH,
    GDN_HEADS,
    GDN_HEADS,
    N_BRANCH * D_MODEL,
]
IN_COLS = sum(IN_SPLITS)
IN_SPLIT_IDX = [sum(IN_SPLITS[:i + 1]) for i in range(len(IN_SPLITS) - 1)]

kernel_name = "hybrid_dsa_gated_deltanet_parallel"


def rms_norm(x, gain):
    xf = x.astype(jnp.float32)
    y = xf * lax.rsqrt(jnp.mean(xf * xf, axis=-1, keepdims=True) + NORM_EPS)
    return (y * gain.astype(jnp.float32)).astype(x.dtype)


def l2_normalize(x):
    xf = x.astype(jnp.float32)
    return xf * lax.rsqrt(jnp.sum(xf * xf, axis=-1, keepdims=True) + NORM_EPS)


def rope_tables(positions, rot_dim):
    inv_freq = ROPE_THETA ** (-jnp.arange(0, rot_dim, 2, dtype=jnp.float32) / rot_dim)
    ang = positions.astype(jnp.float32)[..., None] * inv_freq
    return jnp.cos(ang)[:, :, None, :], jnp.sin(ang)[:, :, None, :]


def apply_partial_rope(x, cos, sin):
    half = cos.shape[-1]
    rot = 2 * half
    x1, x2, xp = x[..., :half], x[..., half:rot], x[..., rot:]
    c, s = cos.astype(x.dtype), sin.astype(x.dtype)
    return jnp.concatenate([x1 * c - x2 * s, x2 * c + x1 * s, xp], axis=-1)


def dsa_attention(q, k, v, q_idx, k_idx, w_idx):
    B, S = q.shape[0], q.shape[1]
    topk = min(INDEX_TOPK, S // 4)
    nb = S // Q_BLOCK
    group = ATTN_HEADS // ATTN_KV_HEADS
    scale = ATTN_HEAD_DIM ** -0.5
    key_pos = jnp.arange(S, dtype=jnp.int32)

    def blocks(t):
        return t.reshape(B, nb, Q_BLOCK, *t.shape[2:]).swapaxes(0, 1)

    def one_block(args):
        qb, qib, wb, tb = args
        raw = jnp.einsum('bqhd,bsd->bqhs', qib, k_idx)
        iscore = jnp.einsum('bqhs,bqh->bqs', jax.nn.relu(raw), wb).astype(jnp.float32)
        causal = key_pos[None, :] <= tb[:, None]
        iscore = jnp.where(causal[None], iscore, -jnp.inf)
        _, sel = lax.top_k(iscore, topk)
        valid = sel <= tb[None, :, None]
        k_sel = jax.vmap(lambda kb, ib: kb[ib])(k, sel)
        v_sel = jax.vmap(lambda vb, ib: vb[ib])(v, sel)
        qg = qb.reshape(B, Q_BLOCK, ATTN_KV_HEADS, group, ATTN_HEAD_DIM)
        logits = jnp.einsum('bqhgd,bqkhd->bqhgk', qg, k_sel).astype(jnp.float32) * scale
        logits = jnp.where(valid[:, :, None, None, :], logits, -jnp.inf)
        p = jax.nn.softmax(logits, axis=-1).astype(v.dtype)
        o = jnp.einsum('bqhgk,bqkhd->bqhgd', p, v_sel)
        return o.reshape(B, Q_BLOCK, ATTN_WIDTH)

    out = lax.map(one_block, (blocks(q), blocks(q_idx), blocks(w_idx),
                              key_pos.reshape(nb, Q_BLOCK)))
    return out.swapaxes(0, 1).reshape(B, S, ATTN_WIDTH)


def causal_depthwise_conv(x, w):
    C = x.shape[-1]
    return lax.conv_general_dilated(
        x, w[:, None, :].astype(x.dtype), window_strides=(1,),
        padding=[(CONV_K - 1, 0)], dimension_numbers=('NWC', 'WIO', 'NWC'),
        feature_group_count=C)


def chunked_gated_delta_rule(q, k, v, beta, g):
    B, S, H, dk = q.shape
    dv = v.shape[-1]
    nc = S // CHUNK

    def c4(t):
        return t.reshape(B, nc, CHUNK, H, t.shape[-1]).transpose(0, 3, 1, 2, 4)

    def c3(t):
        return t.reshape(B, nc, CHUNK, H).transpose(0, 3, 1, 2)

    q, k, v = c4(q) * (dk ** -0.5), c4(k), c4(v)
    beta, gc = c3(beta), jnp.cumsum(c3(g), axis=-1)
    incl = jnp.tril(jnp.ones((CHUNK, CHUNK), dtype=bool))
    strict = jnp.tril(jnp.ones((CHUNK, CHUNK), dtype=bool), k=-1)
    decay = jnp.exp(jnp.where(incl, gc[..., :, None] - gc[..., None, :], -jnp.inf))
    kb = k * beta[..., None]
    lower = jnp.where(strict, jnp.einsum('bhnid,bhnjd->bhnij', kb, k) * decay, 0.0)
    tmat = jnp.eye(CHUNK, dtype=jnp.float32) + lower
    u = lax.linalg.triangular_solve(tmat, v * beta[..., None], left_side=True,
                                    lower=True, unit_diagonal=True)
    w = lax.linalg.triangular_solve(tmat, kb * jnp.exp(gc)[..., None], left_side=True,
                                    lower=True, unit_diagonal=True)
    qk = jnp.einsum('bhnid,bhnjd->bhnij', q, k) * decay
    q_dec = q * jnp.exp(gc)[..., None]
    k_dec = k * jnp.exp(gc[..., -1:] - gc)[..., None]
    g_tot = jnp.exp(gc[..., -1])
    xs = (jnp.moveaxis(qk, 2, 0), jnp.moveaxis(q_dec, 2, 0), jnp.moveaxis(k_dec, 2, 0),
          jnp.moveaxis(u, 2, 0), jnp.moveaxis(w, 2, 0), jnp.moveaxis(g_tot, 2, 0))

    def step(state, xc):
        qk_c, qd_c, kd_c, u_c, w_c, gt_c = xc
        v_new = u_c - jnp.einsum('bhcd,bhde->bhce', w_c, state)
        o_c = (jnp.einsum('bhcd,bhde->bhce', qd_c, state)
               + jnp.einsum('bhij,bhje->bhie', qk_c, v_new))
        state = state * gt_c[..., None, None] + jnp.einsum('bhcd,bhce->bhde', kd_c, v_new)
        return state, o_c

    state0 = jnp.zeros((B, H, dk, dv), dtype=jnp.float32)
    _, o = lax.scan(step, state0, xs)
    return o.transpose(1, 0, 3, 2, 4).reshape(B, S, H, dv)


def setup_inputs(seed: int = 0) -> dict:
    key = jax.random.key(seed)
    ks = jax.random.split(key, 12)
    x = jax.random.normal(ks[0], (BATCH, SEQ, D_MODEL), jnp.float32)
    positions = jnp.broadcast_to(jnp.arange(SEQ, dtype=jnp.int32), (BATCH, SEQ))
    norm_gain = 1.0 + 0.02 * jax.random.normal(ks[1], (DEPTH, D_MODEL), jnp.float32)
    w_in = jax.random.normal(ks[2], (DEPTH, D_MODEL, IN_COLS), jnp.float32) * D_MODEL ** -0.5
    conv_w = jax.random.normal(ks[3], (DEPTH, CONV_K, GDN_QKV_WIDTH), jnp.float32) * CONV_K ** -0.5
    a_log = jnp.log(jax.random.uniform(ks[4], (DEPTH, GDN_HEADS), jnp.float32, 1.0, 16.0))
    dt = jnp.exp(jax.random.uniform(ks[5], (DEPTH, GDN_HEADS), jnp.float32,
                                    math.log(1e-3), math.log(1e-1)))
    dt_bias = dt + jnp.log(-jnp.expm1(-dt))
    gdn_norm_gain = 1.0 + 0.02 * jax.random.normal(ks[6], (DEPTH, GDN_DV), jnp.float32)
    idx_k_gain = 1.0 + 0.02 * jax.random.normal(ks[7], (DEPTH, IDX_DIM), jnp.float32)
    w_out = jax.random.normal(ks[8], (DEPTH, D_MODEL, D_MODEL), jnp.float32) * D_MODEL ** -0.5
    final_gain = 1.0 + 0.02 * jax.random.normal(ks[9], (D_MODEL,), jnp.float32)
    return {"x": x, "positions": positions, "norm_gain": norm_gain, "w_in": w_in,
            "conv_w": conv_w, "a_log": a_log, "dt_bias": dt_bias,
            "gdn_norm_gain": gdn_norm_gain, "idx_k_gain": idx_k_gain,
            "w_out": w_out, "final_gain": final_gain}


def reference(x, positions, norm_gain, w_in, conv_w, a_log, dt_bias, gdn_norm_gain,
              idx_k_gain, w_out, final_gain):
    B, S = x.shape[0], x.shape[1]
    cos_a, sin_a = rope_tables(positions, ATTN_ROT_DIM)
    cos_i, sin_i = rope_tables(positions, IDX_ROT_DIM)
    idx_scale = IDX_HEADS ** -0.5 * IDX_DIM ** -0.5
    for layer in range(DEPTH):
        h = rms_norm(x, norm_gain[layer])
        proj = h @ w_in[layer]
        (q_a, k_a, v_a, z_a, q_i, k_i, w_i, qkv_b, z_b, beta_l, a_l, gate_l) = jnp.split(
            proj, IN_SPLIT_IDX, axis=-1)

        q_a = apply_partial_rope(q_a.reshape(B, S, ATTN_HEADS, ATTN_HEAD_DIM), cos_a, sin_a)
        k_a = apply_partial_rope(k_a.reshape(B, S, ATTN_KV_HEADS, ATTN_HEAD_DIM), cos_a, sin_a)
        v_a = v_a.reshape(B, S, ATTN_KV_HEADS, ATTN_HEAD_DIM)
        q_i = apply_partial_rope(q_i.reshape(B, S, IDX_HEADS, IDX_DIM), cos_i, sin_i)
        k_i = apply_partial_rope(rms_norm(k_i, idx_k_gain[layer])[:, :, None, :],
                                 cos_i, sin_i)[:, :, 0, :]
        w_i = w_i * idx_scale
        y_a = dsa_attention(q_a, k_a, v_a, q_i, k_i, w_i) * jax.nn.silu(z_a)

        qkv_b = jax.nn.silu(causal_depthwise_conv(qkv_b, conv_w[layer]))
        q_b, k_b, v_b = jnp.split(qkv_b, [GDN_QK_WIDTH, 2 * GDN_QK_WIDTH], axis=-1)
        q_b = l2_normalize(q_b.reshape(B, S, GDN_HEADS, GDN_DK))
        k_b = l2_normalize(k_b.reshape(B, S, GDN_HEADS, GDN_DK))
        v_b = v_b.reshape(B, S, GDN_HEADS, GDN_DV).astype(jnp.float32)
        beta = jax.nn.sigmoid(beta_l.astype(jnp.float32))
        g = -jnp.exp(a_log[layer].astype(jnp.float32)) * jax.nn.softplus(
            a_l.astype(jnp.float32) + dt_bias[layer].astype(jnp.float32))
        o_b = chunked_gated_delta_rule(q_b, k_b, v_b, beta, g).astype(x.dtype)
        y_b = (rms_norm(o_b, gdn_norm_gain[layer])
               * jax.nn.silu(z_b.reshape(B, S, GDN_HEADS, GDN_DV))).reshape(B, S, GDN_WIDTH)

        gates = jax.nn.sigmoid(gate_l).reshape(B, S, N_BRANCH, D_MODEL)
        mixed = gates[:, :, 0, :] * y_a + gates[:, :, 1, :] * y_b
        x = x + mixed @ w_out[layer]
    return rms_norm(x, final_gain)
```

```python
import numpy as np
from contextlib import ExitStack
import concourse.bass as bass
import concourse.mybir as mybir

F32 = mybir.dt.float32
BF16 = mybir.dt.bfloat16
I32 = mybir.dt.int32
ALU = mybir.AluOpType
AF = mybir.ActivationFunctionType
AX = mybir.AxisListType


class _Op:
    __slots__ = ("idx", "eng", "fn", "deps", "dma", "dseq", "sig", "signals")

    def __init__(self, idx, eng, fn, deps, dma):
        self.idx = idx
        self.eng = eng
        self.fn = fn
        self.deps = deps
        self.dma = dma
        self.dseq = 0
        self.sig = 0
        self.signals = False


_PSUM_T = ("pb", "bi", "bs", "bo")
_PSUM_S = ("b6", "b7", "bd")


def _is_psum_key(x):
    return (isinstance(x, tuple) and x[0] in _PSUM_T) or (isinstance(x, str) and x in _PSUM_S)


class Prog:
    ENGS = ("pe", "act", "dve", "pool", "sp")

    def __init__(self, nc):
        self.nc = nc
        self.ops = []
        self.flushed = 0
        self.lastw = {}
        self.readers = {}
        self.dma_prev = {}
        self.dma_cnt = {}
        self.last_eng = {}
        self.es = ExitStack()
        self.pes = None
        self._n = 0
        self.esem = {e: self.es.enter_context(nc.semaphore("s_" + e)) for e in self.ENGS}
        self.dsem = {}
        self.cnt = {e: 0 for e in self.ENGS}
        self.waited = {e: {} for e in self.ENGS}

    def _nm(self, name):
        self._n += 1
        return "%s_%d" % (name, self._n)

    def sb(self, name, shape, dt, persist=False):
        st = self.es if (persist or self.pes is None) else self.pes
        return st.enter_context(self.nc.sbuf_tensor(self._nm(name), list(shape), dt))

    def ps(self, name, shape, dt):
        return self.es.enter_context(self.nc.psum_tensor(self._nm(name), list(shape), dt))

    def begin_phase(self):
        self.pes = ExitStack()

    def end_phase(self):
        self.barrier()
        self.flush()
        self.pes.close()
        self.pes = None

    def op(self, eng, fn, r=(), w=(), dma=None, deps=None):
        idx = len(self.ops)
        deps = set(deps) if deps is not None else set()
        pr = [x for x in r if _is_psum_key(x)]
        if pr:
            r = [x for x in r if not _is_psum_key(x)]
            w = list(w) + pr
        for k in r:
            if k in self.lastw:
                deps.add(self.lastw[k])
        for k in w:
            if k in self.lastw:
                deps.add(self.lastw[k])
            for q in self.readers.get(k, ()):
                deps.add(q)
        if dma is not None:
            if dma in self.dma_prev:
                deps.add(self.dma_prev[dma])
            self.dma_prev[dma] = idx
        o = _Op(idx, eng, fn, deps, dma)
        if dma is not None:
            self.dma_cnt[dma] = self.dma_cnt.get(dma, 0) + 1
            o.dseq = self.dma_cnt[dma]
        else:
            self.last_eng[eng] = idx
        self.ops.append(o)
        for k in w:
            self.lastw[k] = idx
            self.readers[k] = []
        for k in r:
            lst = self.readers.setdefault(k, [])
            if dma is None:
                lst[:] = [q for q in lst if not (self.ops[q].dma is None and self.ops[q].eng == eng)]
            lst.append(idx)
        return idx

    def pe(self, fn, r=(), w=()):
        return self.op("pe", fn, r, w)

    def act(self, fn, r=(), w=()):
        return self.op("act", fn, r, w)

    def dve(self, fn, r=(), w=()):
        return self.op("dve", fn, r, w)

    def pool(self, fn, r=(), w=()):
        return self.op("pool", fn, r, w)

    def I(self, eng, meth, *args, r=(), w=(), **kw):
        return self.op(eng, lambda e: getattr(e, meth)(*args, **kw), r, w)

    def dma(self, q, key, out, in_, r=(), w=(), **kw):
        return self.op(q, lambda e: e.dma_start(out=out, in_=in_, **kw), r, w, dma=key)

    def barrier(self):
        deps = set(i for i in self.last_eng.values() if i >= self.flushed) | set(self.dma_prev.values())
        for e in self.ENGS:
            self.op(e, lambda eng: eng.nop(), deps=deps)
        self.lastw.clear()
        self.readers.clear()

    def flush(self):
        nc = self.nc
        ops = self.ops
        batch = ops[self.flushed:]
        for o in batch:
            for d in o.deps:
                p = ops[d]
                if p.dma is None:
                    if p.eng == "pe" and o.eng == "pe" and o.dma is None:
                        continue
                    assert d >= self.flushed, "dependency on already-flushed compute op"
                    p.signals = True
        for o in batch:
            if o.dma is None and o.signals:
                self.cnt[o.eng] += 1
                o.sig = self.cnt[o.eng]
            if o.dma is not None and o.dma not in self.dsem:
                self.dsem[o.dma] = self.es.enter_context(nc.semaphore(self._nm("d")))
        esem, dsem = self.esem, self.dsem
        per = {e: [o for o in batch if o.eng == e] for e in self.ENGS}

        def run(ename, eng):
            waited = self.waited[ename]
            for o in per[ename]:
                need = {}
                for d in o.deps:
                    p = ops[d]
                    if p.dma is not None:
                        s, v = dsem[p.dma], 16 * p.dseq
                    else:
                        if p.eng == "pe" and ename == "pe" and o.dma is None:
                            continue
                        s, v = esem[p.eng], p.sig
                    if waited.get(s.num, 0) >= v:
                        continue
                    if need.get(s.num, (None, 0))[1] < v:
                        need[s.num] = (s, v)
                for s, v in need.values():
                    eng.wait_ge(s, v)
                    waited[s.num] = v
                ins = o.fn(eng)
                if o.dma is not None:
                    ins.then_inc(dsem[o.dma], 16)
                elif o.signals:
                    ins.then_inc(esem[ename], 1)

        with nc.Block() as block:
            @block.tensor
            def _(e):
                run("pe", e)

            @block.scalar
            def _(e):
                run("act", e)

            @block.vector
            def _(e):
                run("dve", e)

            @block.gpsimd
            def _(e):
                run("pool", e)

            @block.sync
            def _(e):
                run("sp", e)
        self.flushed = len(ops)

    def finish(self):
        if self.flushed < len(self.ops):
            self.barrier()
            self.flush()
        self.es.close()

import math
import ml_dtypes
from concourse.bass_utils import run_bass_kernel_spmd
import math
import numpy as np
import ml_dtypes

D = 1024
EPS = 1e-6
THETA = 500000.0
NBIS = 20
TOPK = 256
PI = math.pi
NEG = -1.0e30


def fm_blocks():
    b = []
    for h in range(8):
        b.append(("qa", h, 128, 32))
    for g in range(2):
        b.append(("ka", g, 128, 32))
    for h in range(8):
        b.append(("qi", h, 64, 16))
    b.append(("ki", 0, 64, 16))
    for i in range(24):
        b.append(("gd", i, 128, 0))
    return b


FM = fm_blocks()
FM_GROUPS = [FM[0:8], FM[8:19], FM[19:27], FM[27:35], FM[35:43]]
NFM = sum(b[2] + b[3] for b in FM)
TM = ([("v", 1280, 256), ("za", 1536, 512), ("za", 2048, 512), ("wi", 3136, 8)] +
      [("gt", 7256 + 512 * i, 512) for i in range(4)] +
      [("zb", 6216, 512), ("zb", 6728, 512), ("ba", 7240, 16)])
TM_GROUPS = [TM[0:3], TM[3:6], TM[6:8], TM[8:11]]
NTM = sum(c[2] for c in TM)
WG_MAX = 1280


def host_w_fm(w_in, l):
    W = w_in[l]
    cols = []
    for kind, i, nm, ns in FM:
        if kind == "qa":
            c0 = i * 128
            cols += [W[:, c0:c0 + 128], W[:, c0 + 16:c0 + 32], W[:, c0:c0 + 16]]
        elif kind == "ka":
            c0 = 1024 + i * 128
            cols += [W[:, c0:c0 + 128], W[:, c0 + 16:c0 + 32], W[:, c0:c0 + 16]]
        elif kind == "qi":
            c0 = 2560 + i * 64
            cols += [W[:, c0:c0 + 64], W[:, c0 + 8:c0 + 16], W[:, c0:c0 + 8]]
        elif kind == "ki":
            c0 = 3072
            cols += [W[:, c0:c0 + 64], W[:, c0 + 8:c0 + 16], W[:, c0:c0 + 8]]
        else:
            c0 = 3144 + i * 128
            cols += [W[:, c0:c0 + 128]]
    return np.ascontiguousarray(np.concatenate(cols, axis=1))


def host_w_tm(w_in, l):
    W = w_in[l]
    return np.ascontiguousarray(np.concatenate([W[:, c0:c0 + n] for _, c0, n in TM], axis=1))


def host_consts():
    c = {}
    c["ident_bf"] = np.eye(128, dtype=ml_dtypes.bfloat16)
    c["ident_f"] = np.eye(128, dtype=np.float32)
    c["i4"] = np.concatenate([np.eye(128)] * 4, axis=1).astype(ml_dtypes.bfloat16)
    q = np.arange(128)[:, None]
    k = np.arange(128)[None, :]
    c["tri_bias"] = np.where(k <= q, 0.0, NEG).astype(np.float32)
    j = np.arange(64)[:, None]
    i = np.arange(64)[None, :]
    m = np.zeros((64, 5, 64), np.float32)
    m[:, 0, :] = (j <= i)
    m[:, 1, :] = (i < j)
    m[:, 2, :] = np.where(i >= j, 0.0, NEG)
    m[:, 3, :] = np.where(i < j, 0.0, NEG)
    m[:, 4, :] = np.eye(64)
    c["m64"] = m
    cs = np.zeros((128, 8), np.float32)
    r = np.arange(32)
    cs[:32, 0] = THETA ** (-(2.0 * (r % 16)) / 32.0)
    r16 = np.arange(16)
    cs[:16, 1] = THETA ** (-(2.0 * (r16 % 8)) / 16.0)
    cs[:32, 2] = np.where(r < 16, -1.0, 1.0)
    cs[:16, 3] = np.where(r16 < 8, -1.0, 1.0)
    cs[:, 4] = EPS
    cs[:, 5] = 1.0
    c["cst"] = cs
    return c


class K:
    pass


def build(SEQ, NL, dbg=(), phases=('tab', 'A1', 'A2', 'gdn', 'attn', 'C')):
    nc = bass.Bass("TRN2", target_bir_lowering=False)
    k = K()
    k.nc, k.SEQ, k.NL = nc, SEQ, NL
    NT = SEQ // 128
    NB = SEQ // 512
    k.NT, k.NB = NT, NB

    def inp(name, shape, dt=F32):
        return nc.dram_tensor(name, list(shape), dt, kind="ExternalInput").ap()

    def scr(name, shape, dt=F32):
        kind = "ExternalOutput" if name in dbg else "Internal"
        return nc.dram_tensor(name, list(shape), dt, kind=kind).ap()

    k.x = inp("x", [SEQ, D])
    k.pos = inp("pos", [1, SEQ], I32)
    k.fgain = inp("fgain", [1, D])
    k.wfm = [inp("wfm%d" % l, [D, NFM]) for l in range(NL)]
    k.wtm = [inp("wtm%d" % l, [D, NTM]) for l in range(NL)]
    k.wout = [inp("wout%d" % l, [D, D]) for l in range(NL)]
    k.ng = [inp("ng%d" % l, [1, D]) for l in range(NL)]
    k.convw = [inp("convw%d" % l, [128, 24, 4]) for l in range(NL)]
    k.gdng = [inp("gdng%d" % l, [1, 128]) for l in range(NL)]
    k.alog = [inp("alog%d" % l, [1, 8]) for l in range(NL)]
    k.dtb = [inp("dtb%d" % l, [1, 8]) for l in range(NL)]
    k.ikg = [inp("ikg%d" % l, [64, 2]) for l in range(NL)]
    k.c_ident_bf = inp("ident_bf", [128, 128], BF16)
    k.c_ident_f = inp("ident_f", [128, 128])
    k.c_i4 = inp("i4", [128, 512], BF16)
    k.c_tri = inp("tri_bias", [128, 128])
    k.c_m64 = inp("m64", [64, 5, 64])
    k.c_cst = inp("cst", [128, 8])
    k.out = nc.dram_tensor("out", [SEQ, D], F32, kind="ExternalOutput").ap()

    k.tab = scr("tab", [4, 32, SEQ])
    k.hT = scr("hT", [128, 8, SEQ], BF16)
    k.qT = scr("qT", [8, 128, SEQ], BF16)
    k.kT = scr("kT", [2, 128, SEQ], BF16)
    k.qiT = scr("qiT", [8, 64, SEQ], BF16)
    k.kiT = scr("kiT", [64, SEQ], BF16)
    k.gT = scr("gT", [24, 128, SEQ])
    k.v = scr("v", [SEQ, 256], BF16)
    k.za = scr("za", [SEQ, D])
    k.wi = scr("wi", [SEQ, 8])
    k.gt = scr("gt", [SEQ, 2 * D])
    k.zb = scr("zb", [SEQ, D])
    k.ba = scr("ba", [SEQ, 16])
    k.yb = scr("yb", [SEQ, D])
    k.oa = scr("oa", [SEQ, D])
    k.x1 = scr("x1", [SEQ, D])

    P = Prog(nc)
    k.P = P
    k.banks = [P.ps("bank", [128, 512], F32) for _ in range(8)]
    k.bi = 0

    k.ident_bf = P.sb("ident_bf", [128, 128], BF16, persist=True)
    k.ident_f = P.sb("ident_f", [128, 128], F32, persist=True)
    k.i4 = P.sb("i4", [128, 512], BF16, persist=True)
    k.tri = P.sb("tri", [128, 128], F32, persist=True)
    k.m64 = P.sb("m64", [64, 5, 64], F32, persist=True)
    k.cst = P.sb("cst", [128, 8], F32, persist=True)
    k.ones_f = P.sb("ones_f", [128, 128], F32, persist=True)
    k.ones_bf = P.sb("ones_bf", [128, 8], BF16, persist=True)
    k.zeros_bf = P.sb("zeros_bf", [128, 512], BF16, persist=True)
    P.begin_phase()
    for t, src in ((k.ident_bf, k.c_ident_bf), (k.ident_f, k.c_ident_f), (k.i4, k.c_i4), (k.tri, k.c_tri),
                   (k.m64, k.c_m64), (k.cst, k.c_cst)):
        P.dma("sp", ("cload", id(t)), t[:], src)
    P.pool(lambda e: e.memset(k.ones_f[:], 1.0))
    P.pool(lambda e: e.memset(k.ones_bf[:], 1.0))
    P.pool(lambda e: e.memset(k.zeros_bf[:], 0.0))
    P.end_phase()

    if 'tab' in phases:
        phase_tables(k)
    for l in range(NL):
        src = k.x if l == 0 else k.x1
        if 'A1' in phases:
            phase_A1(k, l, src)
        if 'A2' in phases:
            phase_A2(k, l)
        if 'gdn' in phases:
            phase_gdn(k, l)
        if 'attn' in phases:
            phase_attn(k, l)
        if 'C' in phases:
            phase_C(k, l, src, last=(l == NL - 1))
    P.finish()
    return nc


def next_bank(k):
    i = k.bi % 8
    k.bi += 1
    return i, k.banks[i]


def phase_tables(k):
    P, SEQ = k.P, k.SEQ
    P.begin_phase()
    CH = min(SEQ, 2048)
    posi = P.sb("posi", [32, CH], I32)
    posf = P.sb("posf", [32, CH], F32)
    ang = P.sb("ang", [32, CH], F32)
    u = P.sb("u", [32, CH], F32)
    ki = P.sb("ki", [32, CH], I32)
    kf = P.sb("kf", [32, CH], F32)
    r = P.sb("r", [32, CH], F32)
    w = P.sb("w", [32, CH], F32)
    res = P.sb("res", [32, CH], F32)
    C1 = float(np.float32(2 * PI))
    C2 = float(2 * PI - C1)
    LIM = 3.14159
    for c0 in range(0, SEQ, CH):
        P.dma("sp", "posi", posi[:], k.pos[:, c0:c0 + CH].partition_broadcast(32), w=["posi"])
        P.dve(lambda e: e.tensor_copy(out=posf[:], in_=posi[:]), r=["posi"], w=["posf"])
        for ti, (rows, fcol, scol, shift) in enumerate(((32, 0, None, PI / 2), (32, 0, 2, 0.0),
                                                        (16, 1, None, PI / 2), (16, 1, 3, 0.0))):
            R = slice(0, rows)
            P.dve(lambda e, R=R, fcol=fcol, shift=shift: e.tensor_scalar(
                out=ang[R, :], in0=posf[R, :], scalar1=k.cst[R, fcol:fcol + 1], scalar2=shift,
                op0=ALU.mult, op1=ALU.add), r=["posf"], w=["ang"])
            P.dve(lambda e, R=R: e.tensor_scalar(out=u[R, :], in0=ang[R, :], scalar1=1.0 / (2 * PI), scalar2=None,
                                                 op0=ALU.mult), r=["ang"], w=["u"])
            P.dve(lambda e, R=R: e.tensor_copy(out=ki[R, :], in_=u[R, :]), r=["u"], w=["ki"])
            P.dve(lambda e, R=R: e.tensor_copy(out=kf[R, :], in_=ki[R, :]), r=["ki"], w=["kf"])
            P.dve(lambda e, R=R: e.scalar_tensor_tensor(out=r[R, :], in0=kf[R, :], scalar=-C1, in1=ang[R, :],
                                                        op0=ALU.mult, op1=ALU.add), r=["kf", "ang"], w=["r"])
            P.dve(lambda e, R=R: e.scalar_tensor_tensor(out=r[R, :], in0=kf[R, :], scalar=-C2, in1=r[R, :],
                                                        op0=ALU.mult, op1=ALU.add), r=["kf", "r"], w=["r"])
            P.dve(lambda e, R=R: e.tensor_scalar(out=w[R, :], in0=r[R, :], scalar1=PI, scalar2=-2 * PI,
                                                 op0=ALU.is_gt, op1=ALU.mult), r=["r"], w=["w"])
            P.dve(lambda e, R=R: e.tensor_tensor(out=r[R, :], in0=r[R, :], in1=w[R, :], op=ALU.add),
                  r=["r", "w"], w=["r"])
            P.dve(lambda e, R=R: e.tensor_scalar(out=w[R, :], in0=r[R, :], scalar1=-PI, scalar2=2 * PI,
                                                 op0=ALU.is_lt, op1=ALU.mult), r=["r"], w=["w"])
            P.dve(lambda e, R=R: e.tensor_tensor(out=r[R, :], in0=r[R, :], in1=w[R, :], op=ALU.add),
                  r=["r", "w"], w=["r"])
            P.dve(lambda e, R=R: e.tensor_scalar(out=r[R, :], in0=r[R, :], scalar1=-LIM, scalar2=LIM,
                                                 op0=ALU.max, op1=ALU.min), r=["r"], w=["r"])
            if scol is None:
                P.act(lambda e, R=R: e.activation(out=res[R, :], in_=r[R, :], func=AF.Sin), r=["r"], w=["res"])
            else:
                P.act(lambda e, R=R, scol=scol: e.activation(out=res[R, :], in_=r[R, :], func=AF.Sin,
                                                             scale=k.cst[R, scol:scol + 1]), r=["r"], w=["res"])
            P.dma("sp", "tabst", k.tab[ti, 0:rows, c0:c0 + CH], res[R, :], r=["res"])
    P.end_phase()


def phase_A1(k, l, src):
    P, NT = k.P, k.NT
    P.begin_phase()
    gain_bc = P.sb("gain_bc", [128, D], F32)
    P.dma("sp", "gain", gain_bc[:], k.ng[l].partition_broadcast(128), w=["gain"])
    xt = [P.sb("xt", [128, D], F32) for _ in range(2)]
    junk = P.sb("junk", [128, D], BF16)
    ss = P.sb("ss", [128, NT], F32)
    rstd = P.sb("rstd", [128, NT], F32)
    hb = [P.sb("hb", [128, D], BF16) for _ in range(2)]
    ht = [P.sb("ht", [128, 8, 128], BF16) for _ in range(2)]
    for i in range(NT):
        s = i % 2
        P.dma("sp", ("xt", s), xt[s][:], src[i * 128:(i + 1) * 128, :], w=[("xt", s)])
        P.act(lambda e, s=s, i=i: e.activation(out=junk[:], in_=xt[s][:], func=AF.Square,
                                               accum_out=ss[:, i:i + 1]),
              r=[("xt", s)], w=["junk", ("ss", i)])
        P.act(lambda e, i=i: e.activation(out=rstd[:, i:i + 1], in_=ss[:, i:i + 1], func=AF.Sqrt,
                                          bias=k.cst[:, 4:5], scale=1.0 / D),
              r=[("ss", i)], w=[("rstd", i)])
        P.dve(lambda e, i=i: e.reciprocal(out=rstd[:, i:i + 1], in_=rstd[:, i:i + 1]),
              r=[("rstd", i)], w=[("rstd", i)])
        P.dve(lambda e, s=s, i=i: e.scalar_tensor_tensor(out=hb[s][:], in0=xt[s][:], scalar=rstd[:, i:i + 1],
                                                         in1=gain_bc[:], op0=ALU.mult, op1=ALU.mult),
              r=[("xt", s), ("rstd", i), "gain"], w=[("hb", s)])
        bi, bank = next_bank(k)
        pT = bank[:].bitcast(BF16).rearrange("p (c t) -> p c t", t=128)
        for kc in range(8):
            P.pe(lambda e, s=s, kc=kc, pT=pT: e.transpose(out=pT[:, kc, :], in_=hb[s][:, kc * 128:(kc + 1) * 128],
                                                          identity=k.ident_bf[:]),
                 r=[("hb", s)], w=[("pb", bi)])
        P.act(lambda e, s=s, pT=pT: e.copy(out=ht[s][:], in_=pT), r=[("pb", bi)], w=[("ht", s)])
        P.dma("pool", ("hts", s), k.hT[:, :, i * 128:(i + 1) * 128], ht[s][:], r=[("ht", s)])
    P.end_phase()


def phase_A2(k, l):
    P, NB, SEQ = k.P, k.NB, k.SEQ
    IDX_SCALE = (8 ** -0.5) * (64 ** -0.5)
    P.begin_phase()
    wb = [P.sb("wb", [128, 8, WG_MAX], BF16) for _ in range(2)]
    wf = [P.sb("wf", [128, WG_MAX], F32) for _ in range(2)]
    hb = [P.sb("hblk", [128, 8, 512], BF16) for _ in range(2)]
    ca = [P.sb("ca", [32, 512], F32) for _ in range(2)]
    sa = [P.sb("sa", [32, 512], F32) for _ in range(2)]
    cas = [P.sb("cas", [32, 512], F32) for _ in range(2)]
    sas = [P.sb("sas", [32, 512], F32) for _ in range(2)]
    ci = [P.sb("ci", [16, 512], F32) for _ in range(2)]
    si = [P.sb("si", [16, 512], F32) for _ in range(2)]
    t1 = P.sb("t1", [32, 512], F32)
    t2 = P.sb("t2", [32, 512], F32)
    obf = [P.sb("obf", [128, 512], BF16) for _ in range(3)]
    off = [P.sb("off", [128, 512], F32) for _ in range(3)]
    sq = P.sb("sq", [64, 512], F32)
    rinv = P.sb("rinv", [64, 512], F32)
    kn = P.sb("kn", [64, 512], F32)
    ksw = P.sb("ksw", [16, 512], F32)
    ikg = P.sb("ikg", [64, 2], F32)
    P.dma("sp", "ikg", ikg[:], k.ikg[l], w=["ikg"])
    wfv = k.wfm[l].rearrange("(kc p) c -> p kc c", p=128)
    wtv = k.wtm[l].rearrange("(kc p) c -> p kc c", p=128)
    st = {"w": 0, "h": 0, "o": 0, "q": 0}

    def load_w(view, c0, n):
        s = st["w"] % 2
        st["w"] += 1
        for kc in range(8):
            f = kc % 2
            P.dma("sp", ("wf", f), wf[f][:, :n], view[:, kc, c0:c0 + n], w=[("wf", f)])
            P.I("pool", "tensor_copy", out=wb[s][:, kc, :n], in_=wf[f][:, :n], r=[("wf", f)], w=[("wb", s, kc)])
        return s

    def load_h(tb, need_tab):
        s = st["h"] % 2
        st["h"] += 1
        T = slice(tb * 512, (tb + 1) * 512)
        P.dma("sp", ("hblk", s), hb[s][:], k.hT[:, :, T], w=[("hblk", s)])
        if need_tab:
            P.dma("sp", ("ca", s), ca[s][:], k.tab[0, :, T], w=[("ca", s)])
            P.dma("sp", ("sa", s), sa[s][:], k.tab[1, :, T], w=[("sa", s)])
            P.dma("sp", ("ci", s), ci[s][:], k.tab[2, 0:16, T], w=[("ci", s)])
            P.dma("sp", ("si", s), si[s][:], k.tab[3, 0:16, T], w=[("si", s)])
            P.I("dve", "tensor_scalar", out=cas[s][:], in0=ca[s][:], scalar1=128 ** -0.5, scalar2=None, op0=ALU.mult,
                r=[("ca", s)], w=[("cas", s)])
            P.I("dve", "tensor_scalar", out=sas[s][:], in0=sa[s][:], scalar1=128 ** -0.5, scalar2=None, op0=ALU.mult,
                r=[("sa", s)], w=[("sas", s)])
        return s

    def qname():
        st["q"] += 1
        return "sp" if st["q"] % 2 == 0 else "pool"

    def rope(nrot, bankm, bm, banksw, bs, cos_t, ckey, sin_t, skey, scale, o):
        P.I("dve", "tensor_tensor", out=t1[0:nrot, :], in0=bankm[0:nrot, :], in1=cos_t[0:nrot, :], op=ALU.mult,
            r=[("pb", bm), ckey], w=["t1"])
        P.I("dve", "tensor_tensor", out=t2[0:nrot, :], in0=banksw[0:nrot, :], in1=sin_t[0:nrot, :], op=ALU.mult,
            r=[("pb", bs), skey], w=["t2"])
        P.I("dve", "tensor_tensor", out=obf[o][0:nrot, :], in0=t1[0:nrot, :], in1=t2[0:nrot, :], op=ALU.add,
            r=["t1", "t2"], w=[("obf", o)])

    import os
    for gi, grp in enumerate(FM_GROUPS):
        if str(gi) not in os.environ.get('MK_FM', '01234'):
            continue
        gc0 = sum(b[2] + b[3] for g in FM_GROUPS[:gi] for b in g)
        gn = sum(b[2] + b[3] for b in grp)
        ws = load_w(wfv, gc0, gn)
        wk = [("wb", ws, kc) for kc in range(8)]
        for tb in range(NB):
            hs = load_h(tb, gi <= 1)
            T = slice(tb * 512, (tb + 1) * 512)
            c0 = 0
            for (kind, idx, nm, ns) in grp:
                bm, bankm = next_bank(k)
                for kc in range(8):
                    P.pe(lambda e, ws=ws, hs=hs, kc=kc, c0=c0, nm=nm, bankm=bankm: e.matmul(
                        bankm[0:nm, :], lhsT=wb[ws][:, kc, c0:c0 + nm], rhs=hb[hs][:, kc, :],
                        start=(kc == 0), stop=(kc == 7)), r=[wk[kc], ("hblk", hs)], w=[("pb", bm)])
                bs, banksw = None, None
                if ns and os.environ.get("MK_QA", "full") != "copy":
                    bs, banksw = next_bank(k)
                    for kc in range(8):
                        P.pe(lambda e, ws=ws, hs=hs, kc=kc, c0=c0, nm=nm, ns=ns, banksw=banksw: e.matmul(
                            banksw[0:ns, :], lhsT=wb[ws][:, kc, c0 + nm:c0 + nm + ns], rhs=hb[hs][:, kc, :],
                            start=(kc == 0), stop=(kc == 7)), r=[wk[kc], ("hblk", hs)], w=[("pb", bs)])
                c0 += nm + ns
                o = st["o"] % 3
                st["o"] += 1
                if kind in ("qa", "ka"):
                    scale = 128 ** -0.5 if kind == "qa" else 1.0
                    P.act(lambda e, o=o, bankm=bankm, scale=scale: e.activation(
                        out=obf[o][:, :], in_=bankm[:, :], func=AF.Copy, scale=scale),
                        r=[("pb", bm)], w=[("obf", o)])
                    if os.environ.get("MK_QA", "full") == "full":
                        if kind == "qa":
                            rope(32, bankm, bm, banksw, bs, cas[hs], ("cas", hs), sas[hs], ("sas", hs), scale, o)
                        else:
                            rope(32, bankm, bm, banksw, bs, ca[hs], ("ca", hs), sa[hs], ("sa", hs), scale, o)
                    dst = (k.qT if kind == "qa" else k.kT)[idx, :, T]
                    P.dma(qname(), ("obf", o), dst, obf[o][:, :], r=[("obf", o)])
                elif kind == "qi":
                    P.act(lambda e, o=o, bankm=bankm: e.copy(out=obf[o][0:64, :], in_=bankm[0:64, :]),
                          r=[("pb", bm)], w=[("obf", o)])
                    rope(16, bankm, bm, banksw, bs, ci[hs], ("ci", hs), si[hs], ("si", hs), 1.0, o)
                    P.dma(qname(), ("obf", o), k.qiT[idx, :, T], obf[o][0:64, :], r=[("obf", o)])
                elif kind == "ki":
                    P.act(lambda e, bankm=bankm: e.activation(out=sq[:, :], in_=bankm[0:64, :], func=AF.Square),
                          r=[("pb", bm)], w=["sq"])
                    bq, bankq = next_bank(k)
                    P.pe(lambda e, bankq=bankq: e.matmul(bankq[0:64, :], lhsT=k.ones_f[0:64, 0:64], rhs=sq[:, :],
                                                         start=True, stop=True), r=["sq"], w=[("pb", bq)])
                    P.act(lambda e, bankq=bankq: e.activation(out=rinv[:, :], in_=bankq[0:64, :], func=AF.Sqrt,
                                                              bias=k.cst[0:64, 4:5], scale=1.0 / 64),
                          r=[("pb", bq)], w=["rinv"])
                    P.dve(lambda e: e.reciprocal(out=rinv[:, :], in_=rinv[:, :]), r=["rinv"], w=["rinv"])
                    P.dve(lambda e, bankm=bankm: e.scalar_tensor_tensor(
                        out=kn[:, :], in0=bankm[0:64, :], scalar=ikg[:, 0:1], in1=rinv[:, :],
                        op0=ALU.mult, op1=ALU.mult), r=[("pb", bm), "rinv", "ikg"], w=["kn"])
                    P.dve(lambda e, banksw=banksw: e.scalar_tensor_tensor(
                        out=ksw[:, :], in0=banksw[0:16, :], scalar=ikg[0:16, 1:2], in1=rinv[0:16, :],
                        op0=ALU.mult, op1=ALU.mult), r=[("pb", bs), "rinv", "ikg"], w=["ksw"])
                    P.act(lambda e, o=o: e.copy(out=obf[o][0:64, :], in_=kn[:, :]), r=["kn"], w=[("obf", o)])
                    P.dve(lambda e, hs=hs: e.tensor_tensor(out=t1[0:16, :], in0=kn[0:16, :], in1=ci[hs][:, :],
                                                           op=ALU.mult), r=["kn", ("ci", hs)], w=["t1"])
                    P.dve(lambda e, hs=hs: e.tensor_tensor(out=t2[0:16, :], in0=ksw[:, :], in1=si[hs][:, :],
                                                           op=ALU.mult), r=["ksw", ("si", hs)], w=["t2"])
                    P.dve(lambda e, o=o: e.tensor_tensor(out=obf[o][0:16, :], in0=t1[0:16, :], in1=t2[0:16, :],
                                                         op=ALU.add), r=["t1", "t2"], w=[("obf", o)])
                    P.dma(qname(), ("obf", o), k.kiT[:, T], obf[o][0:64, :], r=[("obf", o)])
                else:
                    if o % 2 == 0:
                        P.act(lambda e, o=o, bankm=bankm: e.copy(out=off[o][:, :], in_=bankm[:, :]),
                              r=[("pb", bm)], w=[("off", o)])
                    else:
                        P.dve(lambda e, o=o, bankm=bankm: e.tensor_copy(out=off[o][:, :], in_=bankm[:, :]),
                              r=[("pb", bm)], w=[("off", o)])
                    P.dma(qname(), ("off", o), k.gT[idx, :, T], off[o][:, :], r=[("off", o)])

    for gi, grp in enumerate(TM_GROUPS):
        if str(gi) not in os.environ.get('MK_TM', '0123'):
            continue
        gc0 = sum(c[2] for g in TM_GROUPS[:gi] for c in g)
        gn = sum(c[2] for c in grp)
        ws = load_w(wtv, gc0, gn)
        wk = [("wb", ws, kc) for kc in range(8)]
        for tb in range(NB):
            hs = load_h(tb, False)
            for tt in range(4):
                R = slice(tb * 512 + tt * 128, tb * 512 + (tt + 1) * 128)
                c0 = 0
                for (kind, csrc, n) in grp:
                    bm, bankm = next_bank(k)
                    for kc in range(8):
                        P.pe(lambda e, ws=ws, hs=hs, kc=kc, c0=c0, n=n, tt=tt, bankm=bankm: e.matmul(
                            bankm[:, 0:n], lhsT=hb[hs][:, kc, tt * 128:(tt + 1) * 128], rhs=wb[ws][:, kc, c0:c0 + n],
                            start=(kc == 0), stop=(kc == 7)), r=[wk[kc], ("hblk", hs)], w=[("pb", bm)])
                    c0 += n
                    o = st["o"] % 3
                    st["o"] += 1
                    if kind == "v":
                        P.act(lambda e, o=o, bankm=bankm, n=n: e.copy(out=obf[o][:, 0:n], in_=bankm[:, 0:n]),
                              r=[("pb", bm)], w=[("obf", o)])
                        P.dma(qname(), ("obf", o), k.v[R, :], obf[o][:, 0:n], r=[("obf", o)])
                        continue
                    if kind in ("za", "zb"):
                        P.act(lambda e, o=o, bankm=bankm, n=n: e.activation(out=off[o][:, 0:n], in_=bankm[:, 0:n],
                                                                            func=AF.Silu),
                              r=[("pb", bm)], w=[("off", o)])
                        dst = (k.za[R, csrc - 1536:csrc - 1536 + n] if kind == "za"
                               else k.zb[R, csrc - 6216:csrc - 6216 + n])
                    elif kind == "gt":
                        P.act(lambda e, o=o, bankm=bankm, n=n: e.activation(out=off[o][:, 0:n], in_=bankm[:, 0:n],
                                                                            func=AF.Sigmoid),
                              r=[("pb", bm)], w=[("off", o)])
                        dst = k.gt[R, csrc - 7256:csrc - 7256 + n]
                    elif kind == "wi":
                        P.dve(lambda e, o=o, bankm=bankm, n=n: e.tensor_scalar(
                            out=off[o][:, 0:n], in0=bankm[:, 0:n], scalar1=IDX_SCALE, scalar2=None, op0=ALU.mult),
                            r=[("pb", bm)], w=[("off", o)])
                        dst = k.wi[R, :]
                    else:
                        P.dve(lambda e, o=o, bankm=bankm, n=n: e.tensor_copy(out=off[o][:, 0:n], in_=bankm[:, 0:n]),
                              r=[("pb", bm)], w=[("off", o)])
                        dst = k.ba[R, :]
                    P.dma(qname(), ("off", o), dst, off[o][:, 0:n], r=[("off", o)])
    P.end_phase()


def phase_gdn(k, l):
    P, SEQ = k.P, k.SEQ
    NS = SEQ // 512
    DKS = 128 ** -0.5
    P.begin_phase()
    U, SU, MBu, MBl, I64 = (k.m64[:, i, :] for i in range(5))

    def bcn(a):
        return a.unsqueeze(1).to_broadcast([64, 8, 64])

    def bci(a, n=64):
        return a.unsqueeze(2).to_broadcast([64, a.shape[1], n])

    def v3(a, n):
        return a.rearrange("p (c n) -> p c n", n=n)

    rb = {"i": 0}

    def nb6():
        i = rb["i"] % 6
        rb["i"] += 1
        return i, k.banks[i]

    B6, B7 = k.banks[6], k.banks[7]

    def T(name, shape, dt=F32):
        return P.sb(name, shape, dt)

    convw = T("convw", [128, 24, 4])
    gg = T("gg", [64, 128])
    alog = T("alog", [64, 8])
    dtb = T("dtb", [64, 8])
    negA = T("negA", [64, 8])
    P.dma("sp", "g_convw", convw[:], k.convw[l], w=["convw"])
    P.dma("sp", "g_gg", gg[:], k.gdng[l].partition_broadcast(64), w=["gg"])
    P.dma("sp", "g_alog", alog[:], k.alog[l].partition_broadcast(64), w=["alog"])
    P.dma("sp", "g_dtb", dtb[:], k.dtb[l].partition_broadcast(64), w=["dtb"])
    P.I("act", "activation", out=negA[:], in_=alog[:], func=AF.Exp, r=["alog"], w=["negA"])
    P.I("dve", "tensor_scalar", out=negA[:], in0=negA[:], scalar1=-1.0, scalar2=None, op0=ALU.mult,
        r=["negA"], w=["negA"])
    S = [[T("S", [128, 128]) for _ in range(2)] for _ in range(8)]
    for h in range(8):
        P.I("pool", "memset", S[h][0][:], 0.0, w=[("S", h, 0)])
    xin = [T("xin", [128, 515]) for _ in range(3)]
    y = [T("y", [128, 512]) for _ in range(3)]
    sqt = T("sqt", [128, 512])
    ctmp = T("ctmp", [128, 512])
    rin = [T("rin", [128, 512]) for _ in range(2)]
    qn = T("qn", [128, 512])
    kn = T("kn", [128, 512])
    egbc = T("egbc", [128, 512])
    qd = T("qd", [128, 512])
    bat = T("bat", [64, 8, 16])
    beta = T("beta", [64, 8, 8])
    xa = T("xa", [64, 8, 8])
    gall = T("gall", [64, 8, 8])
    gh = T("gh", [64, 8])
    bh = T("bh", [64, 8])
    nbh = T("nbh", [64, 8])
    gcs = T("gcs", [64, 8])
    dgl = T("dgl", [64, 8])
    egl = T("egl", [64, 8])
    eg = T("eg", [64, 8])
    beg = T("beg", [64, 8])
    gtot = T("gtot", [128, 8])
    Gm = T("Gm", [64, 8, 64])
    d3 = T("d3", [64, 8, 64])
    DTu = T("DTu", [64, 8, 64])
    DTl = T("DTl", [64, 8, 64])
    Bm = T("Bm", [64, 8, 64])
    Mfac = T("Mfac", [64, 8, 64])
    Nm = [T("Nm", [64, 8, 64]) for _ in range(2)]
    NTm = [T("NTm", [64, 8, 64]) for _ in range(2)]
    Rm = [T("Rm", [64, 8, 64]) for _ in range(2)]
    qkT = T("qkT", [64, 8, 64])
    kbg = T("kbg", [64, 8, 128])
    kdec = T("kdec", [64, 8, 128])
    vb = T("vb", [64, 8, 128])
    us = T("us", [64, 8, 128])
    os_ = T("os", [64, 8, 128])
    sq3 = T("sq3", [64, 8, 128])
    zbt = T("zbt", [64, 8, 128])
    y1 = T("y1", [64, 8, 128])
    wTs = T("wTs", [128, 512])
    vnew = T("vnew", [64, 128])
    ssq = T("ssq", [64, 8])
    rinv8 = T("rinv8", [64, 8])
    eps64 = k.cst[0:64, 4:5]
    one64 = k.cst[0:64, 5:6]

    for s in range(NS):
        t0 = s * 512
        P.dma("sp", "g_bat", bat[:], k.ba[t0:t0 + 512, :].rearrange("(n p) c -> p n c", p=64), w=["bat"])
        P.I("act", "activation", out=beta[:], in_=bat[:, :, 0:8], func=AF.Sigmoid, r=["bat"], w=["beta"])
        P.I("dve", "tensor_tensor", out=xa[:], in0=bat[:, :, 8:16], in1=dtb[:].unsqueeze(1).to_broadcast([64, 8, 8]),
            op=ALU.add, r=["bat", "dtb"], w=["xa"])
        P.I("act", "activation", out=xa[:], in_=xa[:], func=AF.Exp, r=["xa"], w=["xa"])
        P.I("act", "activation", out=xa[:], in_=xa[:], func=AF.Ln, bias=one64, r=["xa"], w=["xa"])
        P.I("dve", "tensor_tensor", out=gall[:], in0=xa[:], in1=negA[:].unsqueeze(1).to_broadcast([64, 8, 8]),
            op=ALU.mult, r=["xa", "negA"], w=["gall"])
        for h in range(8):
            for i, blk in enumerate((h, 8 + h, 16 + h)):
                if s == 0:
                    P.I("pool", "memset", xin[i][:, 0:3], 0.0, w=[("xin", i)])
                    P.dma("sp", ("g_xin", i), xin[i][:, 3:515], k.gT[blk, :, 0:512], w=[("xin", i)])
                else:
                    P.dma("sp", ("g_xin", i), xin[i][:, :], k.gT[blk, :, t0 - 3:t0 + 512], w=[("xin", i)])
                P.I("pool", "tensor_scalar", out=y[i][:], in0=xin[i][:, 0:512], scalar1=convw[:, blk, 0:1],
                    scalar2=None, op0=ALU.mult, r=[("xin", i), "convw"], w=[("y", i)])
                for j in range(1, 4):
                    P.I("pool", "tensor_scalar", out=ctmp[:], in0=xin[i][:, j:j + 512],
                        scalar1=convw[:, blk, j:j + 1], scalar2=None, op0=ALU.mult,
                        r=[("xin", i), "convw"], w=["ctmp"])
                    P.I("pool", "tensor_tensor", out=y[i][:], in0=y[i][:], in1=ctmp[:], op=ALU.add,
                        r=["ctmp", ("y", i)], w=[("y", i)])
                P.I("act", "activation", out=y[i][:], in_=y[i][:], func=AF.Silu, r=[("y", i)], w=[("y", i)])
            for i in range(2):
                P.I("act", "activation", out=sqt[:], in_=y[i][:], func=AF.Square, r=[("y", i)], w=["sqt"])
                bi, bk = nb6()
                P.I("pe", "matmul", bk[:, :], lhsT=k.ones_f[:, :], rhs=sqt[:], start=True, stop=True,
                    r=["sqt"], w=[("pb", bi)])
                P.I("act", "activation", out=rin[i][:], in_=bk[:, :], func=AF.Sqrt, bias=k.cst[:, 4:5],
                    r=[("pb", bi)], w=[("rin", i)])
                P.I("dve", "reciprocal", out=rin[i][:], in_=rin[i][:], r=[("rin", i)], w=[("rin", i)])
            P.I("dve", "scalar_tensor_tensor", out=qn[:], in0=y[0][:], scalar=DKS, in1=rin[0][:], op0=ALU.mult,
                op1=ALU.mult, r=[("y", 0), ("rin", 0)], w=["qn"])
            P.I("dve", "tensor_tensor", out=kn[:], in0=y[1][:], in1=rin[1][:], op=ALU.mult,
                r=[("y", 1), ("rin", 1)], w=["kn"])
            P.I("dve", "tensor_copy", out=gh[:], in_=gall[:, :, h], r=["gall"], w=["gh"])
            P.I("dve", "tensor_copy", out=bh[:], in_=beta[:, :, h], r=["beta"], w=["bh"])
            P.I("dve", "tensor_scalar", out=nbh[:], in0=beta[:, :, h], scalar1=-1.0, scalar2=None, op0=ALU.mult,
                r=["beta"], w=["nbh"])
            P.I("pe", "matmul", B6[0:64, 0:8], lhsT=U, rhs=gh[:], start=True, stop=True, r=["gh"], w=["b6"])
            P.I("pe", "matmul", B6[:, 8:16], lhsT=k.ones_f[0:64, :], rhs=gh[:], start=True, stop=True,
                r=["gh"], w=["b6"])
            P.I("dve", "tensor_copy", out=gcs[:], in_=B6[0:64, 0:8], r=["b6"], w=["gcs"])
            P.I("dve", "tensor_tensor", out=dgl[:], in0=B6[0:64, 8:16], in1=gcs[:], op=ALU.subtract,
                r=["b6", "gcs"], w=["dgl"])
            P.I("act", "activation", out=egl[:], in_=dgl[:], func=AF.Exp, r=["dgl"], w=["egl"])
            P.I("act", "activation", out=eg[:], in_=gcs[:], func=AF.Exp, r=["gcs"], w=["eg"])
            P.I("act", "activation", out=gtot[:], in_=B6[:, 8:16], func=AF.Exp, r=["b6"], w=["gtot"])
            P.I("dve", "tensor_tensor", out=beg[:], in0=bh[:], in1=eg[:], op=ALU.mult, r=["bh", "eg"], w=["beg"])
            P.I("dve", "tensor_tensor", out=Gm[:], in0=bcn(U), in1=bci(gh[:]), op=ALU.mult, r=["gh"], w=["Gm"])
            bB, bkB = nb6()
            P.I("pe", "matmul", bkB[:, :], lhsT=k.ones_f[0:64, :], rhs=Gm[:].rearrange("p c n -> p (c n)"),
                start=True, stop=True, r=["Gm"], w=[("pb", bB)])
            P.I("act", "activation", out=egbc[:], in_=bkB[:, :], func=AF.Exp, r=[("pb", bB)], w=["egbc"])
            P.I("dve", "tensor_tensor", out=qd[:], in0=qn[:], in1=egbc[:], op=ALU.mult, r=["qn", "egbc"], w=["qd"])
            P.I("dve", "tensor_tensor", out=d3[:], in0=v3(bkB[0:64, :], 64), in1=bci(gcs[:]), op=ALU.subtract,
                r=[("pb", bB), "gcs"], w=["d3"])
            P.I("dve", "tensor_tensor", out=DTu[:], in0=d3[:], in1=bcn(MBu), op=ALU.add, r=["d3"], w=["DTu"])
            P.I("act", "activation", out=DTu[:], in_=DTu[:], func=AF.Exp, r=["DTu"], w=["DTu"])
            P.I("dve", "scalar_tensor_tensor", out=DTl[:], in0=d3[:], scalar=-1.0, in1=bcn(MBl), op0=ALU.mult,
                op1=ALU.add, r=["d3"], w=["DTl"])
            P.I("act", "activation", out=DTl[:], in_=DTl[:], func=AF.Exp, r=["DTl"], w=["DTl"])
            P.I("dve", "tensor_tensor", out=DTl[:], in0=DTl[:], in1=bci(nbh[:]), op=ALU.mult, r=["DTl", "nbh"],
                w=["DTl"])
            P.I("dve", "tensor_tensor", out=Bm[:], in0=bcn(I64), in1=bci(nbh[:]), op=ALU.mult, r=["nbh"], w=["Bm"])
            bN, bkN = nb6()
            P.I("pe", "matmul", bkN[0:64, :], lhsT=SU, rhs=Bm[:].rearrange("p c n -> p (c n)"), start=True, stop=True,
                r=["Bm"], w=[("pb", bN)])
            P.I("dve", "tensor_tensor", out=Mfac[:], in0=DTu[:], in1=v3(bkN[0:64, :], 64), op=ALU.mult,
                r=["DTu", ("pb", bN)], w=["Mfac"])
            for half in range(2):
                bi, bk = nb6()
                for m in range(4):
                    c = half * 4 + m
                    P.I("pe", "transpose", out=bk[0:64, m * 128:(m + 1) * 128], in_=kn[:, c * 64:(c + 1) * 64],
                        identity=k.ident_f[:, :], r=["kn"], w=[("pb", bi)])
                hs = slice(half * 4, half * 4 + 4)
                P.I("dve", "tensor_tensor", out=kbg[:, hs, :], in0=v3(bk[0:64, :], 128),
                    in1=beg[:, hs].unsqueeze(2).to_broadcast([64, 4, 128]), op=ALU.mult,
                    r=[("pb", bi), "beg"], w=[("kbg", half)])
                P.I("dve", "tensor_tensor", out=kdec[:, hs, :], in0=v3(bk[0:64, :], 128),
                    in1=egl[:, hs].unsqueeze(2).to_broadcast([64, 4, 128]), op=ALU.mult,
                    r=[("pb", bi), "egl"], w=[("kdec", half)])
                bi, bk = nb6()
                for m in range(4):
                    c = half * 4 + m
                    P.I("pe", "transpose", out=bk[0:64, m * 128:(m + 1) * 128], in_=y[2][:, c * 64:(c + 1) * 64],
                        identity=k.ident_f[:, :], r=[("y", 2)], w=[("pb", bi)])
                P.I("dve", "tensor_tensor", out=vb[:, hs, :], in0=v3(bk[0:64, :], 128),
                    in1=bh[:, hs].unsqueeze(2).to_broadcast([64, 4, 128]), op=ALU.mult,
                    r=[("pb", bi), "bh"], w=[("vb", half)])
            bKK, bkKK = nb6()
            for m in range(8):
                cs = slice(m * 64, (m + 1) * 64)
                P.I("pe", "matmul", bkKK[0:64, cs], lhsT=kn[:, cs], rhs=kn[:, cs], start=True, stop=True,
                    r=["kn"], w=[("pb", bKK)])
            bKQ, bkKQ = nb6()
            for m in range(8):
                cs = slice(m * 64, (m + 1) * 64)
                P.I("pe", "matmul", bkKQ[0:64, cs], lhsT=kn[:, cs], rhs=qn[:, cs], start=True, stop=True,
                    r=["kn", "qn"], w=[("pb", bKQ)])
            P.I("dve", "tensor_tensor", out=Nm[0][:], in0=v3(bkKK[0:64, :], 64), in1=Mfac[:], op=ALU.mult,
                r=[("pb", bKK), "Mfac"], w=[("Nm", 0)])
            P.I("dve", "tensor_tensor", out=NTm[0][:], in0=v3(bkKK[0:64, :], 64), in1=DTl[:], op=ALU.mult,
                r=[("pb", bKK), "DTl"], w=[("NTm", 0)])
            P.I("dve", "tensor_tensor", out=qkT[:], in0=v3(bkKQ[0:64, :], 64), in1=DTu[:], op=ALU.mult,
                r=[("pb", bKQ), "DTu"], w=["qkT"])
            P.I("dve", "tensor_tensor", out=Rm[0][:], in0=Nm[0][:], in1=bcn(I64), op=ALU.add,
                r=[("Nm", 0)], w=[("Rm", 0)])
            cn, cr = 0, 0
            for it in range(5):
                nn = 1 - cn
                if it < 4:
                    bA, bkA = nb6()
                    for m in range(8):
                        cs = slice(m * 64, (m + 1) * 64)
                        P.I("pe", "matmul", bkA[0:64, cs], lhsT=NTm[cn][:, m, :], rhs=Nm[cn][:, m, :], start=True,
                            stop=True, r=[("NTm", cn), ("Nm", cn)], w=[("pb", bA)])
                bBt, bkBt = nb6()
                for m in range(8):
                    cs = slice(m * 64, (m + 1) * 64)
                    P.I("pe", "matmul", bkBt[0:64, cs], lhsT=Nm[cn][:, m, :], rhs=NTm[cn][:, m, :], start=True,
                        stop=True, r=[("NTm", cn), ("Nm", cn)], w=[("pb", bBt)])
                P.I("act", "copy", out=NTm[nn][:], in_=v3(bkBt[0:64, :], 64), r=[("pb", bBt)], w=[("NTm", nn)])
                bC, bkC = nb6()
                for m in range(8):
                    cs = slice(m * 64, (m + 1) * 64)
                    P.I("pe", "matmul", bkC[0:64, cs], lhsT=NTm[nn][:, m, :], rhs=Rm[cr][:, m, :], start=True,
                        stop=True, r=[("NTm", nn), ("Rm", cr)], w=[("pb", bC)])
                P.I("dve", "tensor_tensor", out=Rm[1 - cr][:], in0=Rm[cr][:], in1=v3(bkC[0:64, :], 64), op=ALU.add,
                    r=[("Rm", cr), ("pb", bC)], w=[("Rm", 1 - cr)])
                cr = 1 - cr
                if it < 4:
                    P.I("act", "copy", out=Nm[nn][:], in_=v3(bkA[0:64, :], 64), r=[("pb", bA)], w=[("Nm", nn)])
                cn = nn
            R = Rm[cr]
            rk = ("Rm", cr)
            for half in range(2):
                bi, bk = nb6()
                for m in range(4):
                    c = half * 4 + m
                    P.I("pe", "matmul", bk[0:64, m * 128:(m + 1) * 128], lhsT=R[:, c, :], rhs=vb[:, c, :], start=True,
                        stop=True, r=[rk, ("vb", half)], w=[("pb", bi)])
                P.I("act", "copy", out=us[:, half * 4:half * 4 + 4, :], in_=v3(bk[0:64, :], 128),
                    r=[("pb", bi)], w=[("us", half)])
            bW, bkW = nb6()
            for m in range(8):
                P.I("pe", "matmul", bkW[:, m * 64:(m + 1) * 64], lhsT=kbg[:, m, :], rhs=R[:, m, :], start=True,
                    stop=True, r=[rk, ("kbg", m // 4)], w=[("pb", bW)])
            P.I("dve", "tensor_copy", out=wTs[:], in_=bkW[:, :], r=[("pb", bW)], w=["wTs"])
            for m in range(8):
                cur = (s * 8 + m) % 2
                Sc, Sn = S[h][cur], S[h][1 - cur]
                cs = slice(m * 64, (m + 1) * 64)
                P.I("pe", "matmul", B6[0:64, 128:256], lhsT=wTs[:, cs], rhs=Sc[:], start=True, stop=True,
                    r=["wTs", ("S", h, cur)], w=["b6"])
                P.I("dve", "tensor_tensor", out=vnew[:], in0=us[:, m, :], in1=B6[0:64, 128:256], op=ALU.subtract,
                    r=[("us", m // 4), "b6"], w=["vnew"])
                oc = slice((m % 4) * 128, (m % 4 + 1) * 128)
                P.I("pe", "matmul", B7[0:64, oc], lhsT=qd[:, cs], rhs=Sc[:], start=True, stop=False,
                    r=["qd", ("S", h, cur)], w=["b7"])
                P.I("pe", "matmul", B7[0:64, oc], lhsT=qkT[:, m, :], rhs=vnew[:], start=False, stop=True,
                    r=["qkT", "vnew"], w=["b7"])
                P.I("pe", "matmul", B6[:, 256:384], lhsT=kdec[:, m, :], rhs=vnew[:], start=True, stop=True,
                    r=[("kdec", m // 4), "vnew"], w=["b6"])
                P.I("dve", "scalar_tensor_tensor", out=Sn[:], in0=Sc[:], scalar=gtot[:, m:m + 1], in1=B6[:, 256:384],
                    op0=ALU.mult, op1=ALU.add, r=[("S", h, cur), "gtot", "b6"], w=[("S", h, 1 - cur)])
                if m % 4 == 3:
                    P.I("act", "copy", out=os_[:, m - 3:m + 1, :], in_=v3(B7[0:64, :], 128), r=["b7"],
                        w=[("os", m // 4)])
            P.I("dve", "tensor_tensor", out=sq3[:], in0=os_[:], in1=os_[:], op=ALU.mult,
                r=[("os", 0), ("os", 1)], w=["sq3"])
            P.I("dve", "tensor_reduce", out=ssq[:], in_=sq3[:], axis=AX.X, op=ALU.add, r=["sq3"], w=["ssq"])
            P.I("act", "activation", out=rinv8[:], in_=ssq[:], func=AF.Sqrt, bias=eps64, scale=1.0 / 128,
                r=["ssq"], w=["rinv8"])
            P.I("dve", "reciprocal", out=rinv8[:], in_=rinv8[:], r=["rinv8"], w=["rinv8"])
            P.dma("sp", "g_zbt", zbt[:], k.zb[t0:t0 + 512, h * 128:(h + 1) * 128].rearrange("(n p) c -> p n c", p=64),
                  w=["zbt"])
            P.I("dve", "tensor_tensor", out=y1[:], in0=os_[:], in1=bci(rinv8[:], 128), op=ALU.mult,
                r=[("os", 0), ("os", 1), "rinv8"], w=["y1"])
            P.I("pool", "tensor_tensor", out=y1[:], in0=y1[:], in1=gg[:].unsqueeze(1).to_broadcast([64, 8, 128]),
                op=ALU.mult, r=["y1", "gg"], w=["y1"])
            P.I("pool", "tensor_tensor", out=y1[:], in0=y1[:], in1=zbt[:], op=ALU.mult, r=["y1", "zbt"], w=["y1"])
            P.dma("pool", "g_yb", k.yb[t0:t0 + 512, h * 128:(h + 1) * 128].rearrange("(n p) c -> p n c", p=64),
                  y1[:], r=["y1"])
    P.end_phase()


def phase_attn(k, l):
    P, SEQ, NT = k.P, k.SEQ, k.NT
    P.begin_phase()

    def T(name, shape, dt=F32):
        return P.sb(name, shape, dt)

    kT = T("kT", [128, 2, SEQ], BF16)
    V = T("V", [128, NT, 256], BF16)
    kiT = T("kiT", [64, SEQ], BF16)
    Isc = T("Isc", [128, SEQ])
    junk = T("junk", [128, SEQ], BF16)
    mb = T("mb", [128, SEQ], BF16)
    qt = [T("qt", [128, 8, 128], BF16) for _ in range(2)]
    qit = [T("qit", [64, 8, 128], BF16) for _ in range(2)]
    wit = [T("wit", [128, 8]) for _ in range(2)]
    rl = [T("rl", [128, 512]) for _ in range(3)]
    pt = [T("pt", [128, 512], BF16) for _ in range(3)]
    osb = T("osb", [128, 8, 128])
    st8 = T("st8", [128, 8])
    rden = T("rden", [128, 8])
    for g in range(2):
        P.dma("sp", ("a_kT", g), kT[:, g, :], k.kT[g, :, :], w=["kT"])
    vv = k.v.rearrange("(t p) c -> p t c", p=128)
    for t0 in range(0, NT, 8):
        P.dma("sp", ("a_V", t0), V[:, t0:t0 + 8, :], vv[:, t0:t0 + 8, :], w=[("V", t0)])
    P.dma("sp", "a_kiT", kiT[:], k.kiT, w=["kiT"])
    BI = [k.banks[0], k.banks[1]]
    BS = [k.banks[2], k.banks[3]]
    BO = [k.banks[4], k.banks[5]]
    BD = k.banks[6]
    ci = {"i": 0, "s": 0, "r": 0, "p": 0}
    topk = float(min(TOPK, SEQ // 4))
    for j in range(NT):
        s = j % 2
        nk = j + 1
        L = nk * 128
        Tq = slice(j * 128, (j + 1) * 128)
        P.dma("sp", ("a_qt", s), qt[s][:], k.qT[:, :, Tq].rearrange("h d t -> d h t"), w=[("qt", s)])
        P.dma("sp", ("a_qit", s), qit[s][:], k.qiT[:, :, Tq].rearrange("h d t -> d h t"), w=[("qit", s)])
        P.dma("sp", ("a_wit", s), wit[s][:], k.wi[Tq, :], w=[("wit", s)])
        nblk = (L + 511) // 512
        for kb in range(nblk):
            w_ = min(512, L - kb * 512)
            cs = slice(kb * 512, kb * 512 + w_)
            for h in range(8):
                b = ci["i"] % 2
                ci["i"] += 1
                P.I("pe", "matmul", BI[b][:, 0:w_], lhsT=qit[s][:, h, :], rhs=kiT[:, cs], start=True, stop=True,
                    r=[("qit", s), "kiT"], w=[("bi", b)])
                r_ = ci["r"] % 3
                ci["r"] += 1
                P.I("act", "activation", out=rl[r_][:, 0:w_], in_=BI[b][:, 0:w_], func=AF.Relu,
                    r=[("bi", b)], w=[("rl", r_)])
                if h == 0:
                    P.I("dve", "tensor_scalar", out=Isc[:, cs], in0=rl[r_][:, 0:w_], scalar1=wit[s][:, 0:1],
                        scalar2=None, op0=ALU.mult, r=[("rl", r_), ("wit", s)], w=["I"])
                else:
                    P.I("dve", "scalar_tensor_tensor", out=Isc[:, cs], in0=rl[r_][:, 0:w_], scalar=wit[s][:, h:h + 1],
                        in1=Isc[:, cs], op0=ALU.mult, op1=ALU.add, r=[("rl", r_), ("wit", s), "I"],
                        w=["I"])
        IK = ["I"]
        P.I("dve", "tensor_reduce", out=st8[:, 0:1], in_=Isc[:, 0:L], axis=AX.X, op=ALU.max, r=IK, w=["st0"])
        P.I("dve", "tensor_reduce", out=st8[:, 1:2], in_=Isc[:, 0:L], axis=AX.X, op=ALU.min, r=IK, w=["st1"])
        P.I("dve", "tensor_tensor", out=st8[:, 2:3], in0=st8[:, 0:1], in1=st8[:, 1:2], op=ALU.subtract,
            r=["st0", "st1"], w=["st2"])
        P.I("dve", "tensor_scalar", out=st8[:, 2:3], in0=st8[:, 2:3], scalar1=1e-20, scalar2=None, op0=ALU.add,
            r=["st2"], w=["st2"])
        P.I("dve", "reciprocal", out=st8[:, 2:3], in_=st8[:, 2:3], r=["st2"], w=["st2"])
        P.I("dve", "tensor_scalar", out=Isc[:, 0:L], in0=Isc[:, 0:L], scalar1=st8[:, 1:2], scalar2=st8[:, 2:3],
            op0=ALU.subtract, op1=ALU.mult, r=IK + ["st1", "st2"], w=["I"])
        P.I("dve", "tensor_tensor", out=Isc[:, L - 128:L], in0=Isc[:, L - 128:L], in1=k.tri[:, :], op=ALU.add,
            r=["I"], w=["I"])
        P.I("dve", "memset", st8[:, 3:4], 0.5, w=["t"])
        for it in range(NBIS):
            wk = 2.0 ** -(it + 2)
            P.I("dve", "tensor_scalar", out=junk[:, 0:L], in0=Isc[:, 0:L], scalar1=st8[:, 3:4], scalar2=0.0,
                op0=ALU.is_ge, op1=ALU.add, accum_out=st8[:, 4:5], r=["I", "t"], w=["junk", "cnt"])
            P.I("dve", "tensor_scalar", out=st8[:, 5:6], in0=st8[:, 4:5], scalar1=topk, scalar2=2.0 * wk,
                op0=ALU.is_ge, op1=ALU.mult, r=["cnt"], w=["tmp"])
            P.I("dve", "scalar_tensor_tensor", out=st8[:, 3:4], in0=st8[:, 5:6], scalar=-wk, in1=st8[:, 3:4],
                op0=ALU.add, op1=ALU.add, r=["tmp", "t"], w=["t"])
        P.I("dve", "tensor_scalar", out=st8[:, 6:7], in0=st8[:, 3:4], scalar1=-(2.0 ** -(NBIS + 1)), scalar2=None,
            op0=ALU.add, r=["t"], w=["thr"])
        P.I("dve", "tensor_scalar", out=mb[:, 0:L], in0=Isc[:, 0:L], scalar1=st8[:, 6:7], scalar2=-30000.0,
            op0=ALU.is_lt, op1=ALU.mult, r=["I", "thr"], w=["mb"])
        for g in range(2):
            P.I("pe", "matmul", BO[g][:, :], lhsT=k.zeros_bf[:, 0:128], rhs=k.zeros_bf[:, :], start=True, stop=False,
                w=[("bo", g)])
        P.I("pe", "matmul", BD[:, 0:8], lhsT=k.zeros_bf[:, 0:128], rhs=k.zeros_bf[:, 0:8], start=True, stop=False,
            w=["bd"])
        for kt in range(nk):
            ks = slice(kt * 128, (kt + 1) * 128)
            last = (kt == nk - 1)
            for g in range(2):
                b = ci["s"] % 2
                ci["s"] += 1
                P.I("pe", "matmul", BS[b][:, :], lhsT=kT[:, g, ks],
                    rhs=qt[s][:, 4 * g:4 * g + 4, :].rearrange("p h t -> p (h t)"), start=True, stop=False,
                    r=["kT", ("qt", s)], w=[("bs", b)])
                P.I("pe", "matmul", BS[b][:, :], lhsT=mb[:, ks], rhs=k.i4[:, :], start=False, stop=True,
                    r=["mb"], w=[("bs", b)])
                p_ = ci["p"] % 3
                ci["p"] += 1
                P.I("act", "activation", out=pt[p_][:, :], in_=BS[b][:, :], func=AF.Exp, r=[("bs", b)],
                    w=[("pt", p_)])
                for hh in range(4):
                    P.I("pe", "matmul", BO[g][:, hh * 128:(hh + 1) * 128], lhsT=pt[p_][:, hh * 128:(hh + 1) * 128],
                        rhs=V[:, kt, g * 128:(g + 1) * 128], start=False, stop=last,
                        r=[("pt", p_), ("V", (kt // 8) * 8)], w=[("bo", g)])
                    P.I("pe", "matmul", BD[:, g * 4 + hh:g * 4 + hh + 1], lhsT=pt[p_][:, hh * 128:(hh + 1) * 128],
                        rhs=k.ones_bf[:, 0:1], start=False, stop=last, r=[("pt", p_)], w=["bd"])
        P.I("dve", "reciprocal", out=rden[:, :], in_=BD[:, 0:8], r=["bd"], w=["rden"])
        for g in range(2):
            P.I("dve", "tensor_tensor", out=osb[:, 4 * g:4 * g + 4, :],
                in0=BO[g][:, :].rearrange("p (h d) -> p h d", d=128),
                in1=rden[:, 4 * g:4 * g + 4].unsqueeze(2).to_broadcast([128, 4, 128]), op=ALU.mult,
                r=[("bo", g), "rden"], w=["osb"])
        P.dma("pool", "a_oa", k.oa[Tq, :], osb[:].rearrange("p h d -> p (h d)"), r=["osb"])
    P.end_phase()


def phase_C(k, l, src, last):
    P, NT = k.P, k.NT
    P.begin_phase()

    def T(name, shape, dt=F32):
        return P.sb(name, shape, dt)

    wo = T("wo", [128, 8, D], BF16)
    wov = k.wout[l].rearrange("(kc p) c -> p kc c", p=128)
    wof = [T("wof", [128, D]) for _ in range(2)]
    for kc in range(8):
        f = kc % 2
        P.dma("sp", ("c_wof", f), wof[f][:], wov[:, kc, :], w=[("wof", f)])
        P.I("pool", "tensor_copy", out=wo[:, kc, :], in_=wof[f][:], r=[("wof", f)], w=[("wo", kc)])
    fg = T("fg", [128, D])
    if last:
        P.dma("sp", "c_fg", fg[:], k.fgain.partition_broadcast(128), w=["fg"])
    oa = [T("oa", [128, D]) for _ in range(2)]
    za = [T("za", [128, D]) for _ in range(2)]
    gt = [T("gt", [128, 2 * D]) for _ in range(2)]
    yb = [T("yb", [128, D]) for _ in range(2)]
    xt = [T("xt", [128, D]) for _ in range(2)]
    mx = [T("mx", [128, D], BF16) for _ in range(2)]
    mT = [T("mT", [128, 8, 128], BF16) for _ in range(2)]
    xn = [T("xn", [128, D]) for _ in range(2)]
    junk = T("junk", [128, D], BF16)
    ss = T("ss", [128, 2])
    wk = [("wo", kc) for kc in range(8)]
    for i in range(NT):
        s = i % 2
        R = slice(i * 128, (i + 1) * 128)
        P.dma("sp", ("c_oa", s), oa[s][:], k.oa[R, :], w=[("oa", s)])
        P.dma("sp", ("c_za", s), za[s][:], k.za[R, :], w=[("za", s)])
        P.dma("sp", ("c_gt", s), gt[s][:], k.gt[R, :], w=[("gt", s)])
        P.dma("sp", ("c_yb", s), yb[s][:], k.yb[R, :], w=[("yb", s)])
        P.dma("sp", ("c_xt", s), xt[s][:], src[R, :], w=[("xt", s)])
        P.I("pool", "tensor_tensor", out=oa[s][:], in0=oa[s][:], in1=za[s][:], op=ALU.mult,
            r=[("oa", s), ("za", s)], w=[("oa", s)])
        P.I("pool", "tensor_tensor", out=oa[s][:], in0=oa[s][:], in1=gt[s][:, 0:D], op=ALU.mult,
            r=[("oa", s), ("gt", s)], w=[("oa", s)])
        P.I("dve", "tensor_tensor", out=yb[s][:], in0=yb[s][:], in1=gt[s][:, D:2 * D], op=ALU.mult,
            r=[("yb", s), ("gt", s)], w=[("yb", s)])
        P.I("dve", "tensor_tensor", out=mx[s][:], in0=oa[s][:], in1=yb[s][:], op=ALU.add,
            r=[("oa", s), ("yb", s)], w=[("mx", s)])
        bi, bank = next_bank(k)
        pT = bank[:].bitcast(BF16).rearrange("p (c t) -> p c t", t=128)
        for kc in range(8):
            P.I("pe", "transpose", out=pT[:, kc, :], in_=mx[s][:, kc * 128:(kc + 1) * 128], identity=k.ident_bf[:],
                r=[("mx", s)], w=[("pb", bi)])
        P.I("act", "copy", out=mT[s][:], in_=pT, r=[("pb", bi)], w=[("mT", s)])
        for half in range(2):
            bo, bko = next_bank(k)
            for kc in range(8):
                P.I("pe", "matmul", bko[:, :], lhsT=mT[s][:, kc, :], rhs=wo[:, kc, half * 512:(half + 1) * 512],
                    start=(kc == 0), stop=(kc == 7), r=[("mT", s), wk[kc]], w=[("pb", bo)])
            P.I("dve", "tensor_tensor", out=xn[s][:, half * 512:(half + 1) * 512], in0=xt[s][:, half * 512:(half + 1) * 512],
                in1=bko[:, :], op=ALU.add, r=[("xt", s), ("pb", bo)], w=[("xn", s, half)])
        xk = [("xn", s, 0), ("xn", s, 1)]
        if not last:
            P.dma("pool", ("c_st", s), k.x1[R, :], xn[s][:], r=xk)
        else:
            P.I("act", "activation", out=junk[:], in_=xn[s][:], func=AF.Square, accum_out=ss[:, s:s + 1],
                r=xk, w=["junk", ("ss", s)])
            P.I("act", "activation", out=ss[:, s:s + 1], in_=ss[:, s:s + 1], func=AF.Sqrt, bias=k.cst[:, 4:5],
                scale=1.0 / D, r=[("ss", s)], w=[("ss", s)])
            P.I("dve", "reciprocal", out=ss[:, s:s + 1], in_=ss[:, s:s + 1], r=[("ss", s)], w=[("ss", s)])
            P.I("dve", "scalar_tensor_tensor", out=xn[s][:], in0=xn[s][:], scalar=ss[:, s:s + 1], in1=fg[:],
                op0=ALU.mult, op1=ALU.mult, r=xk + [("ss", s), "fg"], w=xk)
            P.dma("pool", ("c_st", s), k.out[R, :], xn[s][:], r=xk)
    P.end_phase()


_CACHE = {}
SEQ_FULL = 8192
NLAYERS = 2
NCORES = 2


def _in_maps(inp, NL):
    consts = host_consts()
    per_layer = {}
    for l in range(NL):
        per_layer["wfm%d" % l] = host_w_fm(inp["w_in"], l)
        per_layer["wtm%d" % l] = host_w_tm(inp["w_in"], l)
        per_layer["wout%d" % l] = np.ascontiguousarray(inp["w_out"][l], dtype=np.float32)
        per_layer["ng%d" % l] = np.ascontiguousarray(inp["norm_gain"][l][None, :], dtype=np.float32)
        cw = np.asarray(inp["conv_w"][l], dtype=np.float32)
        per_layer["convw%d" % l] = np.ascontiguousarray(cw.reshape(4, 24, 128).transpose(2, 1, 0))
        per_layer["gdng%d" % l] = np.ascontiguousarray(inp["gdn_norm_gain"][l][None, :], dtype=np.float32)
        per_layer["alog%d" % l] = np.ascontiguousarray(inp["a_log"][l][None, :], dtype=np.float32)
        per_layer["dtb%d" % l] = np.ascontiguousarray(inp["dt_bias"][l][None, :], dtype=np.float32)
        g = np.asarray(inp["idx_k_gain"][l], dtype=np.float32)
        gs = np.zeros(64, np.float32)
        gs[:8] = g[8:16]
        gs[8:16] = g[0:8]
        per_layer["ikg%d" % l] = np.ascontiguousarray(np.stack([g, gs], 1))
    maps = []
    for c in range(NCORES):
        b = c % 2
        m = {"x": np.ascontiguousarray(inp["x"][b], dtype=np.float32),
             "pos": np.ascontiguousarray(inp["positions"][b:b + 1]).astype(np.int32),
             "fgain": np.ascontiguousarray(np.asarray(inp["final_gain"], dtype=np.float32)[None, :])}
        m.update(per_layer)
        m.update(consts)
        maps.append(m)
    return maps


def kernel(x, positions, norm_gain, w_in, conv_w, a_log, dt_bias, gdn_norm_gain, idx_k_gain, w_out, final_gain):
    inp = {"x": np.asarray(x), "positions": np.asarray(positions), "norm_gain": np.asarray(norm_gain),
           "w_in": np.asarray(w_in), "conv_w": np.asarray(conv_w), "a_log": np.asarray(a_log),
           "dt_bias": np.asarray(dt_bias), "gdn_norm_gain": np.asarray(gdn_norm_gain),
           "idx_k_gain": np.asarray(idx_k_gain), "w_out": np.asarray(w_out), "final_gain": np.asarray(final_gain)}
    B, S, _ = inp["x"].shape
    NL = inp["w_in"].shape[0]
    assert B == 2
    nc = build(S, NL)
    maps = _in_maps(inp, NL)
    res = run_bass_kernel_spmd(nc, maps, core_ids=list(range(NCORES)))
    out = np.stack([np.asarray(res.results[b]["out"], dtype=np.float32) for b in range(2)], axis=0)
    return out
```

```python
import numpy as np
from contextlib import ExitStack
import concourse.bass as bass
import concourse.mybir as mybir

F32 = mybir.dt.float32
BF16 = mybir.dt.bfloat16
I32 = mybir.dt.int32
ALU = mybir.AluOpType
AF = mybir.ActivationFunctionType
AX = mybir.AxisListType


class _Op:
    __slots__ = ("idx", "eng", "fn", "deps", "dma", "dseq", "sig", "signals")

    def __init__(self, idx, eng, fn, deps, dma):
        self.idx = idx
        self.eng = eng
        self.fn = fn
        self.deps = deps
        self.dma = dma
        self.dseq = 0
        self.sig = 0
        self.signals = False


_PSUM_T = ("pb", "bi", "bs", "bo")
_PSUM_S = ("b6", "b7", "bd")


def _is_psum_key(x):
    return (isinstance(x, tuple) and x[0] in _PSUM_T) or (isinstance(x, str) and x in _PSUM_S)


class Prog:
    ENGS = ("pe", "act", "dve", "pool", "sp")

    def __init__(self, nc):
        self.nc = nc
        self.ops = []
        self.flushed = 0
        self.lastw = {}
        self.readers = {}
        self.dma_prev = {}
        self.dma_cnt = {}
        self.last_eng = {}
        self.es = ExitStack()
        self.pes = None
        self._n = 0
        self.esem = {e: self.es.enter_context(nc.semaphore("s_" + e)) for e in self.ENGS}
        self.dsem = {}
        self.cnt = {e: 0 for e in self.ENGS}
        self.waited = {e: {} for e in self.ENGS}

    def _nm(self, name):
        self._n += 1
        return "%s_%d" % (name, self._n)

    def sb(self, name, shape, dt, persist=False):
        st = self.es if (persist or self.pes is None) else self.pes
        return st.enter_context(self.nc.sbuf_tensor(self._nm(name), list(shape), dt))

    def ps(self, name, shape, dt):
        return self.es.enter_context(self.nc.psum_tensor(self._nm(name), list(shape), dt))

    def begin_phase(self):
        self.pes = ExitStack()

    def end_phase(self):
        self.barrier()
        self.flush()
        self.pes.close()
        self.pes = None

    def op(self, eng, fn, r=(), w=(), dma=None, deps=None):
        idx = len(self.ops)
        deps = set(deps) if deps is not None else set()
        pr = [x for x in r if _is_psum_key(x)]
        if pr:
            r = [x for x in r if not _is_psum_key(x)]
            w = list(w) + pr
        for k in r:
            if k in self.lastw:
                deps.add(self.lastw[k])
        for k in w:
            if k in self.lastw:
                deps.add(self.lastw[k])
            for q in self.readers.get(k, ()):
                deps.add(q)
        if dma is not None:
            if dma in self.dma_prev:
                deps.add(self.dma_prev[dma])
            self.dma_prev[dma] = idx
        o = _Op(idx, eng, fn, deps, dma)
        if dma is not None:
            self.dma_cnt[dma] = self.dma_cnt.get(dma, 0) + 1
            o.dseq = self.dma_cnt[dma]
        else:
            self.last_eng[eng] = idx
        self.ops.append(o)
        for k in w:
            self.lastw[k] = idx
            self.readers[k] = []
        for k in r:
            lst = self.readers.setdefault(k, [])
            if dma is None:
                lst[:] = [q for q in lst if not (self.ops[q].dma is None and self.ops[q].eng == eng)]
            lst.append(idx)
        return idx

    def pe(self, fn, r=(), w=()):
        return self.op("pe", fn, r, w)

    def act(self, fn, r=(), w=()):
        return self.op("act", fn, r, w)

    def dve(self, fn, r=(), w=()):
        return self.op("dve", fn, r, w)

    def pool(self, fn, r=(), w=()):
        return self.op("pool", fn, r, w)

    def I(self, eng, meth, *args, r=(), w=(), **kw):
        return self.op(eng, lambda e: getattr(e, meth)(*args, **kw), r, w)

    def dma(self, q, key, out, in_, r=(), w=(), **kw):
        return self.op(q, lambda e: e.dma_start(out=out, in_=in_, **kw), r, w, dma=key)

    def barrier(self):
        deps = set(i for i in self.last_eng.values() if i >= self.flushed) | set(self.dma_prev.values())
        for e in self.ENGS:
            self.op(e, lambda eng: eng.nop(), deps=deps)
        self.lastw.clear()
        self.readers.clear()

    def flush(self):
        nc = self.nc
        ops = self.ops
        batch = ops[self.flushed:]
        for o in batch:
            for d in o.deps:
                p = ops[d]
                if p.dma is None:
                    if p.eng == "pe" and o.eng == "pe" and o.dma is None:
                        continue
                    assert d >= self.flushed, "dependency on already-flushed compute op"
                    p.signals = True
        for o in batch:
            if o.dma is None and o.signals:
                self.cnt[o.eng] += 1
                o.sig = self.cnt[o.eng]
            if o.dma is not None and o.dma not in self.dsem:
                self.dsem[o.dma] = self.es.enter_context(nc.semaphore(self._nm("d")))
        esem, dsem = self.esem, self.dsem
        per = {e: [o for o in batch if o.eng == e] for e in self.ENGS}

        def run(ename, eng):
            waited = self.waited[ename]
            for o in per[ename]:
                need = {}
                for d in o.deps:
                    p = ops[d]
                    if p.dma is not None:
                        s, v = dsem[p.dma], 16 * p.dseq
                    else:
                        if p.eng == "pe" and ename == "pe" and o.dma is None:
                            continue
                        s, v = esem[p.eng], p.sig
                    if waited.get(s.num, 0) >= v:
                        continue
                    if need.get(s.num, (None, 0))[1] < v:
                        need[s.num] = (s, v)
                for s, v in need.values():
                    eng.wait_ge(s, v)
                    waited[s.num] = v
                ins = o.fn(eng)
                if o.dma is not None:
                    ins.then_inc(dsem[o.dma], 16)
                elif o.signals:
                    ins.then_inc(esem[ename], 1)

        with nc.Block() as block:
            @block.tensor
            def _(e):
                run("pe", e)

            @block.scalar
            def _(e):
                run("act", e)

            @block.vector
            def _(e):
                run("dve", e)

            @block.gpsimd
            def _(e):
                run("pool", e)

            @block.sync
            def _(e):
                run("sp", e)
        self.flushed = len(ops)

    def finish(self):
        if self.flushed < len(self.ops):
            self.barrier()
            self.flush()
        self.es.close()

import math
import ml_dtypes
from concourse.bass_utils import run_bass_kernel_spmd
import math
import numpy as np
import ml_dtypes

D = 1024
EPS = 1e-6
THETA = 500000.0
NBIS = 15
TOPK = 256
PI = math.pi
NEG = -1.0e30


def fm_blocks():
    b = []
    for h in range(8):
        b.append(("qa", h, 128, 32))
    for g in range(2):
        b.append(("ka", g, 128, 32))
    for h in range(8):
        b.append(("qi", h, 64, 16))
    b.append(("ki", 0, 64, 16))
    for i in range(24):
        b.append(("gd", i, 128, 0))
    return b


FM = fm_blocks()
FM_GROUPS = [FM[0:8], FM[8:19], FM[19:27], FM[27:35], FM[35:43]]
NFM = sum(b[2] + b[3] for b in FM)
TM = ([("v", 1280, 256), ("za", 1536, 512), ("za", 2048, 512), ("wi", 3136, 8)] +
      [("gt", 7256 + 512 * i, 512) for i in range(4)] +
      [("zb", 6216, 512), ("zb", 6728, 512), ("ba", 7240, 16)])
TM_GROUPS = [TM[0:3], TM[3:6], TM[6:8], TM[8:11]]
NTM = sum(c[2] for c in TM)
WG_MAX = 1280


def host_w_fm(w_in, l):
    W = w_in[l]
    cols = []
    for kind, i, nm, ns in FM:
        if kind == "qa":
            c0 = i * 128
            cols += [W[:, c0:c0 + 128], W[:, c0 + 16:c0 + 32], W[:, c0:c0 + 16]]
        elif kind == "ka":
            c0 = 1024 + i * 128
            cols += [W[:, c0:c0 + 128], W[:, c0 + 16:c0 + 32], W[:, c0:c0 + 16]]
        elif kind == "qi":
            c0 = 2560 + i * 64
            cols += [W[:, c0:c0 + 64], W[:, c0 + 8:c0 + 16], W[:, c0:c0 + 8]]
        elif kind == "ki":
            c0 = 3072
            cols += [W[:, c0:c0 + 64], W[:, c0 + 8:c0 + 16], W[:, c0:c0 + 8]]
        else:
            c0 = 3144 + i * 128
            cols += [W[:, c0:c0 + 128]]
    return np.ascontiguousarray(np.concatenate(cols, axis=1))


def host_w_tm(w_in, l):
    W = w_in[l]
    return np.ascontiguousarray(np.concatenate([W[:, c0:c0 + n] for _, c0, n in TM], axis=1))


def host_consts():
    c = {}
    c["ident_bf"] = np.eye(128, dtype=ml_dtypes.bfloat16)
    c["ident_f"] = np.eye(128, dtype=np.float32)
    c["i4"] = np.concatenate([np.eye(128)] * 4, axis=1).astype(ml_dtypes.bfloat16)
    q = np.arange(128)[:, None]
    k = np.arange(128)[None, :]
    c["tri_bias"] = np.where(k <= q, 0.0, NEG).astype(np.float32)
    j = np.arange(64)[:, None]
    i = np.arange(64)[None, :]
    m = np.zeros((64, 5, 64), np.float32)
    m[:, 0, :] = (j <= i)
    m[:, 1, :] = (i < j)
    m[:, 2, :] = np.where(i >= j, 0.0, NEG)
    m[:, 3, :] = np.where(i < j, 0.0, NEG)
    m[:, 4, :] = np.eye(64)
    c["m64"] = m
    cs = np.zeros((128, 8), np.float32)
    r = np.arange(32)
    cs[:32, 0] = THETA ** (-(2.0 * (r % 16)) / 32.0)
    r16 = np.arange(16)
    cs[:16, 1] = THETA ** (-(2.0 * (r16 % 8)) / 16.0)
    cs[:32, 2] = np.where(r < 16, -1.0, 1.0)
    cs[:16, 3] = np.where(r16 < 8, -1.0, 1.0)
    cs[:, 4] = EPS
    cs[:, 5] = 1.0
    c["cst"] = cs
    return c


class K:
    pass


def build(SEQ, NL, dbg=(), phases=('tab', 'A1', 'A2', 'gdn', 'attn', 'C')):
    nc = bass.Bass("TRN2", target_bir_lowering=False)
    k = K()
    k.nc, k.SEQ, k.NL = nc, SEQ, NL
    NT = SEQ // 128
    NB = SEQ // 512
    k.NT, k.NB = NT, NB

    def inp(name, shape, dt=F32):
        return nc.dram_tensor(name, list(shape), dt, kind="ExternalInput").ap()

    def scr(name, shape, dt=F32):
        kind = "ExternalOutput" if name in dbg else "Internal"
        return nc.dram_tensor(name, list(shape), dt, kind=kind).ap()

    k.x = inp("x", [SEQ, D])
    k.pos = inp("pos", [1, SEQ], I32)
    k.fgain = inp("fgain", [1, D])
    k.wfm = [inp("wfm%d" % l, [D, NFM]) for l in range(NL)]
    k.wtm = [inp("wtm%d" % l, [D, NTM]) for l in range(NL)]
    k.wout = [inp("wout%d" % l, [D, D]) for l in range(NL)]
    k.ng = [inp("ng%d" % l, [1, D]) for l in range(NL)]
    k.convw = [inp("convw%d" % l, [128, 24, 4]) for l in range(NL)]
    k.gdng = [inp("gdng%d" % l, [1, 128]) for l in range(NL)]
    k.alog = [inp("alog%d" % l, [1, 8]) for l in range(NL)]
    k.dtb = [inp("dtb%d" % l, [1, 8]) for l in range(NL)]
    k.ikg = [inp("ikg%d" % l, [64, 2]) for l in range(NL)]
    k.c_ident_bf = inp("ident_bf", [128, 128], BF16)
    k.c_ident_f = inp("ident_f", [128, 128])
    k.c_i4 = inp("i4", [128, 512], BF16)
    k.c_tri = inp("tri_bias", [128, 128])
    k.c_m64 = inp("m64", [64, 5, 64])
    k.c_cst = inp("cst", [128, 8])
    k.out = nc.dram_tensor("out", [SEQ, D], F32, kind="ExternalOutput").ap()

    k.tab = scr("tab", [4, 32, SEQ])
    k.hT = scr("hT", [128, 8, SEQ], BF16)
    k.qT = scr("qT", [8, 128, SEQ], BF16)
    k.kT = scr("kT", [2, 128, SEQ], BF16)
    k.qiT = scr("qiT", [8, 64, SEQ], BF16)
    k.kiT = scr("kiT", [64, SEQ], BF16)
    k.gT = scr("gT", [24, 128, SEQ])
    k.v = scr("v", [SEQ, 256], BF16)
    k.za = scr("za", [SEQ, D])
    k.wi = scr("wi", [SEQ, 8])
    k.gt = scr("gt", [SEQ, 2 * D])
    k.zb = scr("zb", [SEQ, D])
    k.ba = scr("ba", [SEQ, 16])
    k.yb = scr("yb", [SEQ, D])
    k.oa = scr("oa", [SEQ, D])
    k.x1 = scr("x1", [SEQ, D])

    P = Prog(nc)
    k.P = P
    k.banks = [P.ps("bank", [128, 512], F32) for _ in range(8)]
    k.bi = 0

    k.ident_bf = P.sb("ident_bf", [128, 128], BF16, persist=True)
    k.ident_f = P.sb("ident_f", [128, 128], F32, persist=True)
    k.i4 = P.sb("i4", [128, 512], BF16, persist=True)
    k.tri = P.sb("tri", [128, 128], F32, persist=True)
    k.m64 = P.sb("m64", [64, 5, 64], F32, persist=True)
    k.cst = P.sb("cst", [128, 8], F32, persist=True)
    k.ones_f = P.sb("ones_f", [128, 128], F32, persist=True)
    k.ones_bf = P.sb("ones_bf", [128, 8], BF16, persist=True)
    k.zeros_bf = P.sb("zeros_bf", [128, 512], BF16, persist=True)
    P.begin_phase()
    for t, src in ((k.ident_bf, k.c_ident_bf), (k.ident_f, k.c_ident_f), (k.i4, k.c_i4), (k.tri, k.c_tri),
                   (k.m64, k.c_m64), (k.cst, k.c_cst)):
        P.dma("sp", ("cload", id(t)), t[:], src)
    P.pool(lambda e: e.memset(k.ones_f[:], 1.0))
    P.pool(lambda e: e.memset(k.ones_bf[:], 1.0))
    P.pool(lambda e: e.memset(k.zeros_bf[:], 0.0))
    P.end_phase()

    if 'tab' in phases:
        phase_tables(k)
    for l in range(NL):
        src = k.x if l == 0 else k.x1
        if 'A1' in phases:
            phase_A1(k, l, src)
        if 'A2' in phases:
            phase_A2(k, l)
        if 'gdn' in phases:
            phase_gdn(k, l)
        if 'attn' in phases:
            phase_attn(k, l)
        if 'C' in phases:
            phase_C(k, l, src, last=(l == NL - 1))
    P.finish()
    return nc


def next_bank(k):
    i = k.bi % 8
    k.bi += 1
    return i, k.banks[i]


def phase_tables(k):
    P, SEQ = k.P, k.SEQ
    P.begin_phase()
    CH = min(SEQ, 2048)
    posi = P.sb("posi", [32, CH], I32)
    posf = P.sb("posf", [32, CH], F32)
    ang = P.sb("ang", [32, CH], F32)
    u = P.sb("u", [32, CH], F32)
    ki = P.sb("ki", [32, CH], I32)
    kf = P.sb("kf", [32, CH], F32)
    r = P.sb("r", [32, CH], F32)
    w = P.sb("w", [32, CH], F32)
    res = P.sb("res", [32, CH], F32)
    C1 = float(np.float32(2 * PI))
    C2 = float(2 * PI - C1)
    LIM = 3.14159
    for c0 in range(0, SEQ, CH):
        P.dma("sp", "posi", posi[:], k.pos[:, c0:c0 + CH].partition_broadcast(32), w=["posi"])
        P.dve(lambda e: e.tensor_copy(out=posf[:], in_=posi[:]), r=["posi"], w=["posf"])
        for ti, (rows, fcol, scol, shift) in enumerate(((32, 0, None, PI / 2), (32, 0, 2, 0.0),
                                                        (16, 1, None, PI / 2), (16, 1, 3, 0.0))):
            R = slice(0, rows)
            P.dve(lambda e, R=R, fcol=fcol, shift=shift: e.tensor_scalar(
                out=ang[R, :], in0=posf[R, :], scalar1=k.cst[R, fcol:fcol + 1], scalar2=shift,
                op0=ALU.mult, op1=ALU.add), r=["posf"], w=["ang"])
            P.dve(lambda e, R=R: e.tensor_scalar(out=u[R, :], in0=ang[R, :], scalar1=1.0 / (2 * PI), scalar2=None,
                                                 op0=ALU.mult), r=["ang"], w=["u"])
            P.dve(lambda e, R=R: e.tensor_copy(out=ki[R, :], in_=u[R, :]), r=["u"], w=["ki"])
            P.dve(lambda e, R=R: e.tensor_copy(out=kf[R, :], in_=ki[R, :]), r=["ki"], w=["kf"])
            P.dve(lambda e, R=R: e.scalar_tensor_tensor(out=r[R, :], in0=kf[R, :], scalar=-C1, in1=ang[R, :],
                                                        op0=ALU.mult, op1=ALU.add), r=["kf", "ang"], w=["r"])
            P.dve(lambda e, R=R: e.scalar_tensor_tensor(out=r[R, :], in0=kf[R, :], scalar=-C2, in1=r[R, :],
                                                        op0=ALU.mult, op1=ALU.add), r=["kf", "r"], w=["r"])
            P.dve(lambda e, R=R: e.tensor_scalar(out=w[R, :], in0=r[R, :], scalar1=PI, scalar2=-2 * PI,
                                                 op0=ALU.is_gt, op1=ALU.mult), r=["r"], w=["w"])
            P.dve(lambda e, R=R: e.tensor_tensor(out=r[R, :], in0=r[R, :], in1=w[R, :], op=ALU.add),
                  r=["r", "w"], w=["r"])
            P.dve(lambda e, R=R: e.tensor_scalar(out=w[R, :], in0=r[R, :], scalar1=-PI, scalar2=2 * PI,
                                                 op0=ALU.is_lt, op1=ALU.mult), r=["r"], w=["w"])
            P.dve(lambda e, R=R: e.tensor_tensor(out=r[R, :], in0=r[R, :], in1=w[R, :], op=ALU.add),
                  r=["r", "w"], w=["r"])
            P.dve(lambda e, R=R: e.tensor_scalar(out=r[R, :], in0=r[R, :], scalar1=-LIM, scalar2=LIM,
                                                 op0=ALU.max, op1=ALU.min), r=["r"], w=["r"])
            if scol is None:
                P.act(lambda e, R=R: e.activation(out=res[R, :], in_=r[R, :], func=AF.Sin), r=["r"], w=["res"])
            else:
                P.act(lambda e, R=R, scol=scol: e.activation(out=res[R, :], in_=r[R, :], func=AF.Sin,
                                                             scale=k.cst[R, scol:scol + 1]), r=["r"], w=["res"])
            P.dma("sp", "tabst", k.tab[ti, 0:rows, c0:c0 + CH], res[R, :], r=["res"])
    P.end_phase()


def phase_A1(k, l, src):
    P, NT = k.P, k.NT
    P.begin_phase()
    gain_bc = P.sb("gain_bc", [128, D], F32)
    P.dma("sp", "gain", gain_bc[:], k.ng[l].partition_broadcast(128), w=["gain"])
    xt = [P.sb("xt", [128, D], F32) for _ in range(2)]
    junk = P.sb("junk", [128, D], BF16)
    ss = P.sb("ss", [128, NT], F32)
    rstd = P.sb("rstd", [128, NT], F32)
    hb = [P.sb("hb", [128, D], BF16) for _ in range(2)]
    ht = [P.sb("ht", [128, 8, 128], BF16) for _ in range(2)]
    for i in range(NT):
        s = i % 2
        P.dma("sp", ("xt", s), xt[s][:], src[i * 128:(i + 1) * 128, :], w=[("xt", s)])
        P.act(lambda e, s=s, i=i: e.activation(out=junk[:], in_=xt[s][:], func=AF.Square,
                                               accum_out=ss[:, i:i + 1]),
              r=[("xt", s)], w=["junk", ("ss", i)])
        P.act(lambda e, i=i: e.activation(out=rstd[:, i:i + 1], in_=ss[:, i:i + 1], func=AF.Sqrt,
                                          bias=k.cst[:, 4:5], scale=1.0 / D),
              r=[("ss", i)], w=[("rstd", i)])
        P.dve(lambda e, i=i: e.reciprocal(out=rstd[:, i:i + 1], in_=rstd[:, i:i + 1]),
              r=[("rstd", i)], w=[("rstd", i)])
        P.dve(lambda e, s=s, i=i: e.scalar_tensor_tensor(out=hb[s][:], in0=xt[s][:], scalar=rstd[:, i:i + 1],
                                                         in1=gain_bc[:], op0=ALU.mult, op1=ALU.mult),
              r=[("xt", s), ("rstd", i), "gain"], w=[("hb", s)])
        bi, bank = next_bank(k)
        pT = bank[:].bitcast(BF16).rearrange("p (c t) -> p c t", t=128)
        for kc in range(8):
            P.pe(lambda e, s=s, kc=kc, pT=pT: e.transpose(out=pT[:, kc, :], in_=hb[s][:, kc * 128:(kc + 1) * 128],
                                                          identity=k.ident_bf[:]),
                 r=[("hb", s)], w=[("pb", bi)])
        P.act(lambda e, s=s, pT=pT: e.copy(out=ht[s][:], in_=pT), r=[("pb", bi)], w=[("ht", s)])
        P.dma("pool", ("hts", s), k.hT[:, :, i * 128:(i + 1) * 128], ht[s][:], r=[("ht", s)])
    P.end_phase()


def phase_A2(k, l):
    P, NB, SEQ = k.P, k.NB, k.SEQ
    IDX_SCALE = (8 ** -0.5) * (64 ** -0.5)
    P.begin_phase()
    wb = [P.sb("wb", [128, 8, WG_MAX], BF16) for _ in range(2)]
    wf = [P.sb("wf", [128, WG_MAX], F32) for _ in range(2)]
    hb = [P.sb("hblk", [128, 8, 512], BF16) for _ in range(2)]
    ca = [P.sb("ca", [32, 512], F32) for _ in range(2)]
    sa = [P.sb("sa", [32, 512], F32) for _ in range(2)]
    cas = [P.sb("cas", [32, 512], F32) for _ in range(2)]
    sas = [P.sb("sas", [32, 512], F32) for _ in range(2)]
    ci = [P.sb("ci", [16, 512], F32) for _ in range(2)]
    si = [P.sb("si", [16, 512], F32) for _ in range(2)]
    t1 = P.sb("t1", [32, 512], F32)
    t2 = P.sb("t2", [32, 512], F32)
    obf = [P.sb("obf", [128, 512], BF16) for _ in range(3)]
    off = [P.sb("off", [128, 512], F32) for _ in range(3)]
    sq = P.sb("sq", [64, 512], F32)
    rinv = P.sb("rinv", [64, 512], F32)
    kn = P.sb("kn", [64, 512], F32)
    ksw = P.sb("ksw", [16, 512], F32)
    ikg = P.sb("ikg", [64, 2], F32)
    P.dma("sp", "ikg", ikg[:], k.ikg[l], w=["ikg"])
    wfv = k.wfm[l].rearrange("(kc p) c -> p kc c", p=128)
    wtv = k.wtm[l].rearrange("(kc p) c -> p kc c", p=128)
    st = {"w": 0, "h": 0, "o": 0, "q": 0}

    def load_w(view, c0, n):
        s = st["w"] % 2
        st["w"] += 1
        for kc in range(8):
            f = kc % 2
            P.dma("sp", ("wf", f), wf[f][:, :n], view[:, kc, c0:c0 + n], w=[("wf", f)])
            P.I("pool", "tensor_copy", out=wb[s][:, kc, :n], in_=wf[f][:, :n], r=[("wf", f)], w=[("wb", s, kc)])
        return s

    def load_h(tb, need_tab):
        s = st["h"] % 2
        st["h"] += 1
        T = slice(tb * 512, (tb + 1) * 512)
        P.dma("sp", ("hblk", s), hb[s][:], k.hT[:, :, T], w=[("hblk", s)])
        if need_tab:
            P.dma("sp", ("ca", s), ca[s][:], k.tab[0, :, T], w=[("ca", s)])
            P.dma("sp", ("sa", s), sa[s][:], k.tab[1, :, T], w=[("sa", s)])
            P.dma("sp", ("ci", s), ci[s][:], k.tab[2, 0:16, T], w=[("ci", s)])
            P.dma("sp", ("si", s), si[s][:], k.tab[3, 0:16, T], w=[("si", s)])
            P.I("dve", "tensor_scalar", out=cas[s][:], in0=ca[s][:], scalar1=128 ** -0.5, scalar2=None, op0=ALU.mult,
                r=[("ca", s)], w=[("cas", s)])
            P.I("dve", "tensor_scalar", out=sas[s][:], in0=sa[s][:], scalar1=128 ** -0.5, scalar2=None, op0=ALU.mult,
                r=[("sa", s)], w=[("sas", s)])
        return s

    def qname():
        st["q"] += 1
        return "sp" if st["q"] % 2 == 0 else "pool"

    def rope(nrot, bankm, bm, banksw, bs, cos_t, ckey, sin_t, skey, scale, o):
        P.I("dve", "tensor_tensor", out=t1[0:nrot, :], in0=bankm[0:nrot, :], in1=cos_t[0:nrot, :], op=ALU.mult,
            r=[("pb", bm), ckey], w=["t1"])
        P.I("dve", "tensor_tensor", out=t2[0:nrot, :], in0=banksw[0:nrot, :], in1=sin_t[0:nrot, :], op=ALU.mult,
            r=[("pb", bs), skey], w=["t2"])
        P.I("dve", "tensor_tensor", out=obf[o][0:nrot, :], in0=t1[0:nrot, :], in1=t2[0:nrot, :], op=ALU.add,
            r=["t1", "t2"], w=[("obf", o)])

    import os
    for gi, grp in enumerate(FM_GROUPS):
        if str(gi) not in os.environ.get('MK_FM', '01234'):
            continue
        gc0 = sum(b[2] + b[3] for g in FM_GROUPS[:gi] for b in g)
        gn = sum(b[2] + b[3] for b in grp)
        ws = load_w(wfv, gc0, gn)
        wk = [("wb", ws, kc) for kc in range(8)]
        for tb in range(NB):
            hs = load_h(tb, gi <= 1)
            T = slice(tb * 512, (tb + 1) * 512)
            c0 = 0
            for (kind, idx, nm, ns) in grp:
                bm, bankm = next_bank(k)
                for kc in range(8):
                    P.pe(lambda e, ws=ws, hs=hs, kc=kc, c0=c0, nm=nm, bankm=bankm: e.matmul(
                        bankm[0:nm, :], lhsT=wb[ws][:, kc, c0:c0 + nm], rhs=hb[hs][:, kc, :],
                        start=(kc == 0), stop=(kc == 7)), r=[wk[kc], ("hblk", hs)], w=[("pb", bm)])
                bs, banksw = None, None
                if ns and os.environ.get("MK_QA", "full") != "copy":
                    bs, banksw = next_bank(k)
                    for kc in range(8):
                        P.pe(lambda e, ws=ws, hs=hs, kc=kc, c0=c0, nm=nm, ns=ns, banksw=banksw: e.matmul(
                            banksw[0:ns, :], lhsT=wb[ws][:, kc, c0 + nm:c0 + nm + ns], rhs=hb[hs][:, kc, :],
                            start=(kc == 0), stop=(kc == 7)), r=[wk[kc], ("hblk", hs)], w=[("pb", bs)])
                c0 += nm + ns
                o = st["o"] % 3
                st["o"] += 1
                if kind in ("qa", "ka"):
                    scale = 128 ** -0.5 if kind == "qa" else 1.0
                    P.act(lambda e, o=o, bankm=bankm, scale=scale: e.activation(
                        out=obf[o][:, :], in_=bankm[:, :], func=AF.Copy, scale=scale),
                        r=[("pb", bm)], w=[("obf", o)])
                    if os.environ.get("MK_QA", "full") == "full":
                        if kind == "qa":
                            rope(32, bankm, bm, banksw, bs, cas[hs], ("cas", hs), sas[hs], ("sas", hs), scale, o)
                        else:
                            rope(32, bankm, bm, banksw, bs, ca[hs], ("ca", hs), sa[hs], ("sa", hs), scale, o)
                    dst = (k.qT if kind == "qa" else k.kT)[idx, :, T]
                    P.dma(qname(), ("obf", o), dst, obf[o][:, :], r=[("obf", o)])
                elif kind == "qi":
                    P.act(lambda e, o=o, bankm=bankm: e.copy(out=obf[o][0:64, :], in_=bankm[0:64, :]),
                          r=[("pb", bm)], w=[("obf", o)])
                    rope(16, bankm, bm, banksw, bs, ci[hs], ("ci", hs), si[hs], ("si", hs), 1.0, o)
                    P.dma(qname(), ("obf", o), k.qiT[idx, :, T], obf[o][0:64, :], r=[("obf", o)])
                elif kind == "ki":
                    P.act(lambda e, bankm=bankm: e.activation(out=sq[:, :], in_=bankm[0:64, :], func=AF.Square),
                          r=[("pb", bm)], w=["sq"])
                    bq, bankq = next_bank(k)
                    P.pe(lambda e, bankq=bankq: e.matmul(bankq[0:64, :], lhsT=k.ones_f[0:64, 0:64], rhs=sq[:, :],
                                                         start=True, stop=True), r=["sq"], w=[("pb", bq)])
                    P.act(lambda e, bankq=bankq: e.activation(out=rinv[:, :], in_=bankq[0:64, :], func=AF.Sqrt,
                                                              bias=k.cst[0:64, 4:5], scale=1.0 / 64),
                          r=[("pb", bq)], w=["rinv"])
                    P.dve(lambda e: e.reciprocal(out=rinv[:, :], in_=rinv[:, :]), r=["rinv"], w=["rinv"])
                    P.dve(lambda e, bankm=bankm: e.scalar_tensor_tensor(
                        out=kn[:, :], in0=bankm[0:64, :], scalar=ikg[:, 0:1], in1=rinv[:, :],
                        op0=ALU.mult, op1=ALU.mult), r=[("pb", bm), "rinv", "ikg"], w=["kn"])
                    P.dve(lambda e, banksw=banksw: e.scalar_tensor_tensor(
                        out=ksw[:, :], in0=banksw[0:16, :], scalar=ikg[0:16, 1:2], in1=rinv[0:16, :],
                        op0=ALU.mult, op1=ALU.mult), r=[("pb", bs), "rinv", "ikg"], w=["ksw"])
                    P.act(lambda e, o=o: e.copy(out=obf[o][0:64, :], in_=kn[:, :]), r=["kn"], w=[("obf", o)])
                    P.dve(lambda e, hs=hs: e.tensor_tensor(out=t1[0:16, :], in0=kn[0:16, :], in1=ci[hs][:, :],
                                                           op=ALU.mult), r=["kn", ("ci", hs)], w=["t1"])
                    P.dve(lambda e, hs=hs: e.tensor_tensor(out=t2[0:16, :], in0=ksw[:, :], in1=si[hs][:, :],
                                                           op=ALU.mult), r=["ksw", ("si", hs)], w=["t2"])
                    P.dve(lambda e, o=o: e.tensor_tensor(out=obf[o][0:16, :], in0=t1[0:16, :], in1=t2[0:16, :],
                                                         op=ALU.add), r=["t1", "t2"], w=[("obf", o)])
                    P.dma(qname(), ("obf", o), k.kiT[:, T], obf[o][0:64, :], r=[("obf", o)])
                else:
                    if o % 2 == 0:
                        P.act(lambda e, o=o, bankm=bankm: e.copy(out=off[o][:, :], in_=bankm[:, :]),
                              r=[("pb", bm)], w=[("off", o)])
                    else:
                        P.dve(lambda e, o=o, bankm=bankm: e.tensor_copy(out=off[o][:, :], in_=bankm[:, :]),
                              r=[("pb", bm)], w=[("off", o)])
                    P.dma(qname(), ("off", o), k.gT[idx, :, T], off[o][:, :], r=[("off", o)])

    for gi, grp in enumerate(TM_GROUPS):
        if str(gi) not in os.environ.get('MK_TM', '0123'):
            continue
        gc0 = sum(c[2] for g in TM_GROUPS[:gi] for c in g)
        gn = sum(c[2] for c in grp)
        ws = load_w(wtv, gc0, gn)
        wk = [("wb", ws, kc) for kc in range(8)]
        for tb in range(NB):
            hs = load_h(tb, False)
            for tt in range(4):
                R = slice(tb * 512 + tt * 128, tb * 512 + (tt + 1) * 128)
                c0 = 0
                for (kind, csrc, n) in grp:
                    bm, bankm = next_bank(k)
                    for kc in range(8):
                        P.pe(lambda e, ws=ws, hs=hs, kc=kc, c0=c0, n=n, tt=tt, bankm=bankm: e.matmul(
                            bankm[:, 0:n], lhsT=hb[hs][:, kc, tt * 128:(tt + 1) * 128], rhs=wb[ws][:, kc, c0:c0 + n],
                            start=(kc == 0), stop=(kc == 7)), r=[wk[kc], ("hblk", hs)], w=[("pb", bm)])
                    c0 += n
                    o = st["o"] % 3
                    st["o"] += 1
                    if kind == "v":
                        P.act(lambda e, o=o, bankm=bankm, n=n: e.copy(out=obf[o][:, 0:n], in_=bankm[:, 0:n]),
                              r=[("pb", bm)], w=[("obf", o)])
                        P.dma(qname(), ("obf", o), k.v[R, :], obf[o][:, 0:n], r=[("obf", o)])
                        continue
                    if kind in ("za", "zb"):
                        P.act(lambda e, o=o, bankm=bankm, n=n: e.activation(out=off[o][:, 0:n], in_=bankm[:, 0:n],
                                                                            func=AF.Silu),
                              r=[("pb", bm)], w=[("off", o)])
                        dst = (k.za[R, csrc - 1536:csrc - 1536 + n] if kind == "za"
                               else k.zb[R, csrc - 6216:csrc - 6216 + n])
                    elif kind == "gt":
                        P.act(lambda e, o=o, bankm=bankm, n=n: e.activation(out=off[o][:, 0:n], in_=bankm[:, 0:n],
                                                                            func=AF.Sigmoid),
                              r=[("pb", bm)], w=[("off", o)])
                        dst = k.gt[R, csrc - 7256:csrc - 7256 + n]
                    elif kind == "wi":
                        P.dve(lambda e, o=o, bankm=bankm, n=n: e.tensor_scalar(
                            out=off[o][:, 0:n], in0=bankm[:, 0:n], scalar1=IDX_SCALE, scalar2=None, op0=ALU.mult),
                            r=[("pb", bm)], w=[("off", o)])
                        dst = k.wi[R, :]
                    else:
                        P.dve(lambda e, o=o, bankm=bankm, n=n: e.tensor_copy(out=off[o][:, 0:n], in_=bankm[:, 0:n]),
                              r=[("pb", bm)], w=[("off", o)])
                        dst = k.ba[R, :]
                    P.dma(qname(), ("off", o), dst, off[o][:, 0:n], r=[("off", o)])
    P.end_phase()


def phase_gdn(k, l):
    P, SEQ = k.P, k.SEQ
    NS = SEQ // 512
    DKS = 128 ** -0.5
    P.begin_phase()
    U, SU, MBu, MBl, I64 = (k.m64[:, i, :] for i in range(5))

    def bcn(a):
        return a.unsqueeze(1).to_broadcast([64, 8, 64])

    def bci(a, n=64):
        return a.unsqueeze(2).to_broadcast([64, a.shape[1], n])

    def v3(a, n):
        return a.rearrange("p (c n) -> p c n", n=n)

    rb = {"i": 0}

    def nb4():
        i = rb["i"] % 4
        rb["i"] += 1
        return i, k.banks[i]

    def T(name, shape, dt=F32):
        return P.sb(name, shape, dt)

    convw = T("convw", [128, 24, 4])
    gg = T("gg", [64, 128])
    alog = T("alog", [64, 8])
    dtb = T("dtb", [64, 8])
    negA = T("negA", [64, 8])
    P.dma("sp", "g_convw", convw[:], k.convw[l], w=["convw"])
    P.dma("sp", "g_gg", gg[:], k.gdng[l].partition_broadcast(64), w=["gg"])
    P.dma("sp", "g_alog", alog[:], k.alog[l].partition_broadcast(64), w=["alog"])
    P.dma("sp", "g_dtb", dtb[:], k.dtb[l].partition_broadcast(64), w=["dtb"])
    P.I("act", "activation", out=negA[:], in_=alog[:], func=AF.Exp, r=["alog"], w=["negA"])
    P.I("dve", "tensor_scalar", out=negA[:], in0=negA[:], scalar1=-1.0, scalar2=None, op0=ALU.mult,
        r=["negA"], w=["negA"])
    S = [[T("S", [128, 128]) for _ in range(2)] for _ in range(8)]
    for h in range(8):
        P.I("pool", "memset", S[h][0][:], 0.0, w=[("S", h, 0)])
    bat = T("bat", [64, 8, 16])
    beta = T("beta", [64, 8, 8])
    xa = T("xa", [64, 8, 8])
    gall = T("gall", [64, 8, 8])
    eps64 = k.cst[0:64, 4:5]
    one64 = k.cst[0:64, 5:6]

    def make_set(sid, b6i, b7i):
        B6, B7 = k.banks[b6i], k.banks[b7i]
        K6, K7 = ("pb", b6i), ("pb", b7i)

        def N(x):
            return (sid, x)

        xin = [T("xin", [128, 515]) for _ in range(3)]
        y = [T("y", [128, 512]) for _ in range(3)]
        sqt = T("sqt", [128, 512])
        ctmp = T("ctmp", [128, 512])
        rin = [T("rin", [128, 512]) for _ in range(2)]
        qn = T("qn", [128, 512])
        kn = T("kn", [128, 512])
        egbc = sqt
        qd = T("qd", [128, 512])
        gh = T("gh", [64, 8])
        bh = T("bh", [64, 8])
        nbh = T("nbh", [64, 8])
        gcs = T("gcs", [64, 8])
        dgl = T("dgl", [64, 8])
        egl = T("egl", [64, 8])
        eg = T("eg", [64, 8])
        beg = T("beg", [64, 8])
        gtot = T("gtot", [128, 8])
        Gm = T("Gm", [64, 8, 64])
        d3 = T("d3", [64, 8, 64])
        DTu = T("DTu", [64, 8, 64])
        DTl = T("DTl", [64, 8, 64])
        Bm = Gm
        Mfac = T("Mfac", [64, 8, 64])
        Nm = [T("Nm", [64, 8, 64]) for _ in range(2)]
        NTm = [T("NTm", [64, 8, 64]) for _ in range(2)]
        Rm = [T("Rm", [64, 8, 64]) for _ in range(2)]
        qkT = T("qkT", [64, 8, 64])
        kbg = T("kbg", [64, 8, 128])
        kdec = T("kdec", [64, 8, 128])
        vb = T("vb", [64, 8, 128])
        us = T("us", [64, 8, 128])
        os_ = T("os", [64, 8, 128])
        sq3, zbt, y1 = kbg, vb, kdec
        wTs = T("wTs", [128, 512])
        vnew = T("vnew", [64, 128])
        ssq = T("ssq", [64, 8])
        rinv8 = T("rinv8", [64, 8])

        def unit(s, h):
            t0 = s * 512
            for i, blk in enumerate((h, 8 + h, 16 + h)):
                if s == 0:
                    P.I("pool", "memset", xin[i][:, 0:3], 0.0, w=[N(("xin", i))])
                    P.dma("sp", (sid, "g_xin", i), xin[i][:, 3:515], k.gT[blk, :, 0:512], w=[N(("xin", i))])
                else:
                    P.dma("sp", (sid, "g_xin", i), xin[i][:, :], k.gT[blk, :, t0 - 3:t0 + 512], w=[N(("xin", i))])
                P.I("pool", "tensor_scalar", out=y[i][:], in0=xin[i][:, 0:512], scalar1=convw[:, blk, 0:1],
                    scalar2=None, op0=ALU.mult, r=[N(("xin", i)), "convw"], w=[N(("y", i))])
                for j in range(1, 4):
                    P.I("pool", "tensor_scalar", out=ctmp[:], in0=xin[i][:, j:j + 512],
                        scalar1=convw[:, blk, j:j + 1], scalar2=None, op0=ALU.mult,
                        r=[N(("xin", i)), "convw"], w=[N("ctmp")])
                    P.I("pool", "tensor_tensor", out=y[i][:], in0=y[i][:], in1=ctmp[:], op=ALU.add,
                        r=[N("ctmp"), N(("y", i))], w=[N(("y", i))])
                P.I("act", "activation", out=y[i][:], in_=y[i][:], func=AF.Silu, r=[N(("y", i))], w=[N(("y", i))])
                yield
            for i in range(2):
                P.I("act", "activation", out=sqt[:], in_=y[i][:], func=AF.Square, r=[N(("y", i))], w=[N("sqt")])
                bi, bk = nb4()
                P.I("pe", "matmul", bk[:, :], lhsT=k.ones_f[:, :], rhs=sqt[:], start=True, stop=True,
                    r=[N("sqt")], w=[("pb", bi)])
                P.I("act", "activation", out=rin[i][:], in_=bk[:, :], func=AF.Sqrt, bias=k.cst[:, 4:5],
                    r=[("pb", bi)], w=[N(("rin", i))])
                P.I("dve", "reciprocal", out=rin[i][:], in_=rin[i][:], r=[N(("rin", i))], w=[N(("rin", i))])
            P.I("dve", "scalar_tensor_tensor", out=qn[:], in0=y[0][:], scalar=DKS, in1=rin[0][:], op0=ALU.mult,
                op1=ALU.mult, r=[N(("y", 0)), N(("rin", 0))], w=[N("qn")])
            P.I("dve", "tensor_tensor", out=kn[:], in0=y[1][:], in1=rin[1][:], op=ALU.mult,
                r=[N(("y", 1)), N(("rin", 1))], w=[N("kn")])
            yield
            P.I("dve", "tensor_copy", out=gh[:], in_=gall[:, :, h], r=["gall"], w=[N("gh")])
            P.I("dve", "tensor_copy", out=bh[:], in_=beta[:, :, h], r=["beta"], w=[N("bh")])
            P.I("dve", "tensor_scalar", out=nbh[:], in0=beta[:, :, h], scalar1=-1.0, scalar2=None, op0=ALU.mult,
                r=["beta"], w=[N("nbh")])
            P.I("pe", "matmul", B6[0:64, 0:8], lhsT=U, rhs=gh[:], start=True, stop=True, r=[N("gh")], w=[K6])
            P.I("pe", "matmul", B6[:, 8:16], lhsT=k.ones_f[0:64, :], rhs=gh[:], start=True, stop=True,
                r=[N("gh")], w=[K6])
            P.I("dve", "tensor_copy", out=gcs[:], in_=B6[0:64, 0:8], r=[K6], w=[N("gcs")])
            P.I("dve", "tensor_tensor", out=dgl[:], in0=B6[0:64, 8:16], in1=gcs[:], op=ALU.subtract,
                r=[K6, N("gcs")], w=[N("dgl")])
            P.I("act", "activation", out=egl[:], in_=dgl[:], func=AF.Exp, r=[N("dgl")], w=[N("egl")])
            P.I("act", "activation", out=eg[:], in_=gcs[:], func=AF.Exp, r=[N("gcs")], w=[N("eg")])
            P.I("act", "activation", out=gtot[:], in_=B6[:, 8:16], func=AF.Exp, r=[K6], w=[N("gtot")])
            P.I("dve", "tensor_tensor", out=beg[:], in0=bh[:], in1=eg[:], op=ALU.mult, r=[N("bh"), N("eg")],
                w=[N("beg")])
            P.I("dve", "tensor_tensor", out=Gm[:], in0=bcn(U), in1=bci(gh[:]), op=ALU.mult, r=[N("gh")], w=[N("Gm")])
            bB, bkB = nb4()
            P.I("pe", "matmul", bkB[:, :], lhsT=k.ones_f[0:64, :], rhs=Gm[:].rearrange("p c n -> p (c n)"),
                start=True, stop=True, r=[N("Gm")], w=[("pb", bB)])
            P.I("act", "activation", out=egbc[:], in_=bkB[:, :], func=AF.Exp, r=[("pb", bB)], w=[N("sqt")])
            P.I("dve", "tensor_tensor", out=qd[:], in0=qn[:], in1=egbc[:], op=ALU.mult, r=[N("qn"), N("sqt")],
                w=[N("qd")])
            P.I("dve", "tensor_tensor", out=d3[:], in0=v3(bkB[0:64, :], 64), in1=bci(gcs[:]), op=ALU.subtract,
                r=[("pb", bB), N("gcs")], w=[N("d3")])
            yield
            P.I("dve", "tensor_tensor", out=DTu[:], in0=d3[:], in1=bcn(MBu), op=ALU.add, r=[N("d3")], w=[N("DTu")])
            P.I("act", "activation", out=DTu[:], in_=DTu[:], func=AF.Exp, r=[N("DTu")], w=[N("DTu")])
            P.I("dve", "scalar_tensor_tensor", out=DTl[:], in0=d3[:], scalar=-1.0, in1=bcn(MBl), op0=ALU.mult,
                op1=ALU.add, r=[N("d3")], w=[N("DTl")])
            P.I("act", "activation", out=DTl[:], in_=DTl[:], func=AF.Exp, r=[N("DTl")], w=[N("DTl")])
            P.I("dve", "tensor_tensor", out=DTl[:], in0=DTl[:], in1=bci(nbh[:]), op=ALU.mult,
                r=[N("DTl"), N("nbh")], w=[N("DTl")])
            P.I("dve", "tensor_tensor", out=Bm[:], in0=bcn(I64), in1=bci(nbh[:]), op=ALU.mult, r=[N("nbh")],
                w=[N("Gm")])
            bN, bkN = nb4()
            P.I("pe", "matmul", bkN[0:64, :], lhsT=SU, rhs=Bm[:].rearrange("p c n -> p (c n)"), start=True, stop=True,
                r=[N("Gm")], w=[("pb", bN)])
            P.I("dve", "tensor_tensor", out=Mfac[:], in0=DTu[:], in1=v3(bkN[0:64, :], 64), op=ALU.mult,
                r=[N("DTu"), ("pb", bN)], w=[N("Mfac")])
            yield
            for half in range(2):
                bi, bk = nb4()
                for m in range(4):
                    c = half * 4 + m
                    P.I("pe", "transpose", out=bk[0:64, m * 128:(m + 1) * 128], in_=kn[:, c * 64:(c + 1) * 64],
                        identity=k.ident_f[:, :], r=[N("kn")], w=[("pb", bi)])
                hs = slice(half * 4, half * 4 + 4)
                P.I("dve", "tensor_tensor", out=kbg[:, hs, :], in0=v3(bk[0:64, :], 128),
                    in1=beg[:, hs].unsqueeze(2).to_broadcast([64, 4, 128]), op=ALU.mult,
                    r=[("pb", bi), N("beg")], w=[N(("kbg", half))])
                P.I("dve", "tensor_tensor", out=kdec[:, hs, :], in0=v3(bk[0:64, :], 128),
                    in1=egl[:, hs].unsqueeze(2).to_broadcast([64, 4, 128]), op=ALU.mult,
                    r=[("pb", bi), N("egl")], w=[N(("kdec", half))])
                bi, bk = nb4()
                for m in range(4):
                    c = half * 4 + m
                    P.I("pe", "transpose", out=bk[0:64, m * 128:(m + 1) * 128], in_=y[2][:, c * 64:(c + 1) * 64],
                        identity=k.ident_f[:, :], r=[N(("y", 2))], w=[("pb", bi)])
                P.I("dve", "tensor_tensor", out=vb[:, hs, :], in0=v3(bk[0:64, :], 128),
                    in1=bh[:, hs].unsqueeze(2).to_broadcast([64, 4, 128]), op=ALU.mult,
                    r=[("pb", bi), N("bh")], w=[N(("vb", half))])
                yield
            bKK, bkKK = nb4()
            for m in range(8):
                cs = slice(m * 64, (m + 1) * 64)
                P.I("pe", "matmul", bkKK[0:64, cs], lhsT=kn[:, cs], rhs=kn[:, cs], start=True, stop=True,
                    r=[N("kn")], w=[("pb", bKK)])
            bKQ, bkKQ = nb4()
            for m in range(8):
                cs = slice(m * 64, (m + 1) * 64)
                P.I("pe", "matmul", bkKQ[0:64, cs], lhsT=kn[:, cs], rhs=qn[:, cs], start=True, stop=True,
                    r=[N("kn"), N("qn")], w=[("pb", bKQ)])
            P.I("dve", "tensor_tensor", out=Nm[0][:], in0=v3(bkKK[0:64, :], 64), in1=Mfac[:], op=ALU.mult,
                r=[("pb", bKK), N("Mfac")], w=[N(("Nm", 0))])
            P.I("dve", "tensor_tensor", out=NTm[0][:], in0=v3(bkKK[0:64, :], 64), in1=DTl[:], op=ALU.mult,
                r=[("pb", bKK), N("DTl")], w=[N(("NTm", 0))])
            P.I("dve", "tensor_tensor", out=qkT[:], in0=v3(bkKQ[0:64, :], 64), in1=DTu[:], op=ALU.mult,
                r=[("pb", bKQ), N("DTu")], w=[N("qkT")])
            P.I("dve", "tensor_tensor", out=Rm[0][:], in0=Nm[0][:], in1=bcn(I64), op=ALU.add,
                r=[N(("Nm", 0))], w=[N(("Rm", 0))])
            yield
            cn, cr = 0, 0
            for it in range(5):
                nn = 1 - cn
                if it < 4:
                    bA, bkA = nb4()
                    for m in range(8):
                        cs = slice(m * 64, (m + 1) * 64)
                        P.I("pe", "matmul", bkA[0:64, cs], lhsT=NTm[cn][:, m, :], rhs=Nm[cn][:, m, :], start=True,
                            stop=True, r=[N(("NTm", cn)), N(("Nm", cn))], w=[("pb", bA)])
                bBt, bkBt = nb4()
                for m in range(8):
                    cs = slice(m * 64, (m + 1) * 64)
                    P.I("pe", "matmul", bkBt[0:64, cs], lhsT=Nm[cn][:, m, :], rhs=NTm[cn][:, m, :], start=True,
                        stop=True, r=[N(("NTm", cn)), N(("Nm", cn))], w=[("pb", bBt)])
                P.I("act", "copy", out=NTm[nn][:], in_=v3(bkBt[0:64, :], 64), r=[("pb", bBt)], w=[N(("NTm", nn))])
                bC, bkC = nb4()
                for m in range(8):
                    cs = slice(m * 64, (m + 1) * 64)
                    P.I("pe", "matmul", bkC[0:64, cs], lhsT=NTm[nn][:, m, :], rhs=Rm[cr][:, m, :], start=True,
                        stop=True, r=[N(("NTm", nn)), N(("Rm", cr))], w=[("pb", bC)])
                P.I("dve", "tensor_tensor", out=Rm[1 - cr][:], in0=Rm[cr][:], in1=v3(bkC[0:64, :], 64), op=ALU.add,
                    r=[N(("Rm", cr)), ("pb", bC)], w=[N(("Rm", 1 - cr))])
                cr = 1 - cr
                if it < 4:
                    P.I("act", "copy", out=Nm[nn][:], in_=v3(bkA[0:64, :], 64), r=[("pb", bA)], w=[N(("Nm", nn))])
                cn = nn
                yield
            R = Rm[cr]
            rk = N(("Rm", cr))
            for half in range(2):
                bi, bk = nb4()
                for m in range(4):
                    c = half * 4 + m
                    P.I("pe", "matmul", bk[0:64, m * 128:(m + 1) * 128], lhsT=R[:, c, :], rhs=vb[:, c, :], start=True,
                        stop=True, r=[rk, N(("vb", half))], w=[("pb", bi)])
                P.I("act", "copy", out=us[:, half * 4:half * 4 + 4, :], in_=v3(bk[0:64, :], 128),
                    r=[("pb", bi)], w=[N(("us", half))])
            bW, bkW = nb4()
            for m in range(8):
                P.I("pe", "matmul", bkW[:, m * 64:(m + 1) * 64], lhsT=kbg[:, m, :], rhs=R[:, m, :], start=True,
                    stop=True, r=[rk, N(("kbg", m // 4))], w=[("pb", bW)])
            P.I("dve", "tensor_copy", out=wTs[:], in_=bkW[:, :], r=[("pb", bW)], w=[N("wTs")])
            yield
            for m in range(8):
                cur = (s * 8 + m) % 2
                Sc, Sn = S[h][cur], S[h][1 - cur]
                cs = slice(m * 64, (m + 1) * 64)
                P.I("pe", "matmul", B6[0:64, 128:256], lhsT=wTs[:, cs], rhs=Sc[:], start=True, stop=True,
                    r=[N("wTs"), ("S", h, cur)], w=[K6])
                P.I("dve", "tensor_tensor", out=vnew[:], in0=us[:, m, :], in1=B6[0:64, 128:256], op=ALU.subtract,
                    r=[N(("us", m // 4)), K6], w=[N("vnew")])
                oc = slice((m % 4) * 128, (m % 4 + 1) * 128)
                P.I("pe", "matmul", B7[0:64, oc], lhsT=qd[:, cs], rhs=Sc[:], start=True, stop=False,
                    r=[N("qd"), ("S", h, cur)], w=[K7])
                P.I("pe", "matmul", B7[0:64, oc], lhsT=qkT[:, m, :], rhs=vnew[:], start=False, stop=True,
                    r=[N("qkT"), N("vnew")], w=[K7])
                P.I("pe", "matmul", B6[:, 256:384], lhsT=kdec[:, m, :], rhs=vnew[:], start=True, stop=True,
                    r=[N(("kdec", m // 4)), N("vnew")], w=[K6])
                P.I("dve", "scalar_tensor_tensor", out=Sn[:], in0=Sc[:], scalar=gtot[:, m:m + 1], in1=B6[:, 256:384],
                    op0=ALU.mult, op1=ALU.add, r=[("S", h, cur), N("gtot"), K6], w=[("S", h, 1 - cur)])
                if m % 4 == 3:
                    P.I("act", "copy", out=os_[:, m - 3:m + 1, :], in_=v3(B7[0:64, :], 128), r=[K7],
                        w=[N(("os", m // 4))])
                yield
            okeys = [N(("os", 0)), N(("os", 1))]
            kb2 = [N(("kbg", 0)), N(("kbg", 1))]
            vb2 = [N(("vb", 0)), N(("vb", 1))]
            kd2 = [N(("kdec", 0)), N(("kdec", 1))]
            P.I("dve", "tensor_tensor", out=sq3[:], in0=os_[:], in1=os_[:], op=ALU.mult, r=okeys, w=kb2)
            P.I("dve", "tensor_reduce", out=ssq[:], in_=sq3[:], axis=AX.X, op=ALU.add, r=kb2, w=[N("ssq")])
            P.I("act", "activation", out=rinv8[:], in_=ssq[:], func=AF.Sqrt, bias=eps64, scale=1.0 / 128,
                r=[N("ssq")], w=[N("rinv8")])
            P.I("dve", "reciprocal", out=rinv8[:], in_=rinv8[:], r=[N("rinv8")], w=[N("rinv8")])
            P.dma("sp", (sid, "g_zbt"), zbt[:],
                  k.zb[t0:t0 + 512, h * 128:(h + 1) * 128].rearrange("(n p) c -> p n c", p=64), w=vb2)
            P.I("dve", "tensor_tensor", out=y1[:], in0=os_[:], in1=bci(rinv8[:], 128), op=ALU.mult,
                r=okeys + [N("rinv8")], w=kd2)
            P.I("pool", "tensor_tensor", out=y1[:], in0=y1[:], in1=gg[:].unsqueeze(1).to_broadcast([64, 8, 128]),
                op=ALU.mult, r=kd2 + ["gg"], w=kd2)
            P.I("pool", "tensor_tensor", out=y1[:], in0=y1[:], in1=zbt[:], op=ALU.mult, r=kd2 + vb2, w=kd2)
            P.dma("pool", (sid, "g_yb"), k.yb[t0:t0 + 512, h * 128:(h + 1) * 128].rearrange("(n p) c -> p n c", p=64),
                  y1[:], r=kd2)
            yield
        return unit

    units = [make_set(0, 6, 7), make_set(1, 4, 5)]
    for s in range(NS):
        t0 = s * 512
        P.dma("sp", "g_bat", bat[:], k.ba[t0:t0 + 512, :].rearrange("(n p) c -> p n c", p=64), w=["bat"])
        P.I("act", "activation", out=beta[:], in_=bat[:, :, 0:8], func=AF.Sigmoid, r=["bat"], w=["beta"])
        P.I("dve", "tensor_tensor", out=xa[:], in0=bat[:, :, 8:16], in1=dtb[:].unsqueeze(1).to_broadcast([64, 8, 8]),
            op=ALU.add, r=["bat", "dtb"], w=["xa"])
        P.I("act", "activation", out=xa[:], in_=xa[:], func=AF.Exp, r=["xa"], w=["xa"])
        P.I("act", "activation", out=xa[:], in_=xa[:], func=AF.Ln, bias=one64, r=["xa"], w=["xa"])
        P.I("dve", "tensor_tensor", out=gall[:], in0=xa[:], in1=negA[:].unsqueeze(1).to_broadcast([64, 8, 8]),
            op=ALU.mult, r=["xa", "negA"], w=["gall"])
        for h0 in range(0, 8, 2):
            gens = [units[0](s, h0), units[1](s, h0 + 1)]
            while gens:
                for g_ in list(gens):
                    try:
                        next(g_)
                    except StopIteration:
                        gens.remove(g_)
    P.end_phase()


def phase_attn(k, l):
    P, SEQ, NT = k.P, k.SEQ, k.NT
    H2 = SEQ // 2
    P.begin_phase()

    def T(name, shape, dt=F32):
        return P.sb(name, shape, dt)

    kT = T("kT", [128, 2, SEQ], BF16)
    V = T("V", [128, NT, 256], BF16)
    kiT = T("kiT", [128, H2], BF16)
    Isc = [T("Isc", [128, SEQ]) for _ in range(2)]
    mb = [T("mb", [128, SEQ], BF16) for _ in range(2)]
    qt = [T("qt", [128, 8, 128], BF16) for _ in range(2)]
    qit = [T("qit", [128, 8, 128], BF16) for _ in range(2)]
    wit = [T("wit", [128, 8]) for _ in range(2)]
    rl = [T("rl", [128, 512]) for _ in range(2)]
    pt = [T("pt", [128, 512], BF16) for _ in range(3)]
    osb = T("osb", [128, 8, 128])
    st8 = [T("st8", [128, 8]) for _ in range(2)]
    rden = T("rden", [128, 8])
    for g in range(2):
        P.dma("sp", ("a_kT", g), kT[:, g, :], k.kT[g, :, :], w=["kT"])
    vv = k.v.rearrange("(t p) c -> p t c", p=128)
    for t0 in range(0, NT, 8):
        P.dma("sp", ("a_V", t0), V[:, t0:t0 + 8, :], vv[:, t0:t0 + 8, :], w=[("V", t0)])
    P.dma("sp", "a_kiT0", kiT[0:64, :], k.kiT[:, 0:H2], w=["kiT"])
    P.dma("sp", "a_kiT1", kiT[64:128, :], k.kiT[:, H2:SEQ], w=["kiT"])
    BI = [k.banks[0], k.banks[1]]
    BS = [k.banks[2], k.banks[3]]
    BO = [k.banks[4], k.banks[5]]
    BD = k.banks[6]
    ci = {"i": 0, "s": 0, "r": 0, "p": 0}
    topk = float(min(TOPK, SEQ // 4))

    def PRE(j):
        s = j % 2
        L = (j + 1) * 128
        Tq = slice(j * 128, (j + 1) * 128)
        I_, M_, S8 = Isc[s], mb[s], st8[s]
        IK, MK = ("I", s), ("mb", s)
        P.dma("sp", ("a_qt", s), qt[s][:], k.qT[:, :, Tq].rearrange("h d t -> d h t"), w=[("qt", s)])
        P.dma("sp", ("a_qit0", s), qit[s][0:64], k.qiT[:, :, Tq].rearrange("h d t -> d h t"), w=[("qit", s)])
        P.dma("sp", ("a_qit1", s), qit[s][64:128], k.qiT[:, :, Tq].rearrange("h d t -> d h t"), w=[("qit", s)])
        P.dma("sp", ("a_wit", s), wit[s][:], k.wi[Tq, :], w=[("wit", s)])
        nblk = (L + 511) // 512
        for kb in range(nblk):
            w_ = min(512, L - kb * 512)
            cs = slice(kb * 512, kb * 512 + w_)
            if kb * 512 < H2:
                pr, kc = slice(0, 64), slice(kb * 512, kb * 512 + w_)
            else:
                pr, kc = slice(64, 128), slice(kb * 512 - H2, kb * 512 - H2 + w_)
            for h in range(8):
                b = ci["i"] % 2
                ci["i"] += 1
                P.I("pe", "matmul", BI[b][:, 0:w_], lhsT=qit[s][pr, h, :], rhs=kiT[pr, kc], start=True, stop=True,
                    r=[("qit", s), "kiT"], w=[("bi", b)])
                r_ = ci["r"] % 2
                ci["r"] += 1
                P.I("act", "activation", out=rl[r_][:, 0:w_], in_=BI[b][:, 0:w_], func=AF.Relu,
                    r=[("bi", b)], w=[("rl", r_)])
                if h == 0:
                    P.I("dve", "tensor_scalar", out=I_[:, cs], in0=rl[r_][:, 0:w_], scalar1=wit[s][:, 0:1],
                        scalar2=None, op0=ALU.mult, r=[("rl", r_), ("wit", s)], w=[IK])
                else:
                    P.I("dve", "scalar_tensor_tensor", out=I_[:, cs], in0=rl[r_][:, 0:w_], scalar=wit[s][:, h:h + 1],
                        in1=I_[:, cs], op0=ALU.mult, op1=ALU.add, r=[("rl", r_), ("wit", s), IK], w=[IK])
        sk = lambda n: ("st", s, n)
        P.I("dve", "tensor_reduce", out=S8[:, 0:1], in_=I_[:, 0:L], axis=AX.X, op=ALU.max, r=[IK], w=[sk(0)])
        P.I("dve", "tensor_reduce", out=S8[:, 1:2], in_=I_[:, 0:L], axis=AX.X, op=ALU.min, r=[IK], w=[sk(1)])
        P.I("dve", "tensor_tensor", out=S8[:, 2:3], in0=S8[:, 0:1], in1=S8[:, 1:2], op=ALU.subtract,
            r=[sk(0), sk(1)], w=[sk(2)])
        P.I("dve", "tensor_scalar", out=S8[:, 2:3], in0=S8[:, 2:3], scalar1=1e-20, scalar2=None, op0=ALU.add,
            r=[sk(2)], w=[sk(2)])
        P.I("dve", "reciprocal", out=S8[:, 2:3], in_=S8[:, 2:3], r=[sk(2)], w=[sk(2)])
        P.I("dve", "tensor_scalar", out=I_[:, 0:L], in0=I_[:, 0:L], scalar1=S8[:, 1:2], scalar2=S8[:, 2:3],
            op0=ALU.subtract, op1=ALU.mult, r=[IK, sk(1), sk(2)], w=[IK])
        P.I("dve", "tensor_tensor", out=I_[:, L - 128:L], in0=I_[:, L - 128:L], in1=k.tri[:, :], op=ALU.add,
            r=[IK], w=[IK])
        P.I("dve", "memset", S8[:, 3:4], 0.5, w=[sk(3)])
        for it in range(NBIS):
            wk = 2.0 ** -(it + 2)
            P.I("dve", "tensor_scalar", out=M_[:, 0:L], in0=I_[:, 0:L], scalar1=S8[:, 3:4], scalar2=0.0,
                op0=ALU.is_ge, op1=ALU.add, accum_out=S8[:, 4:5], r=[IK, sk(3)], w=[MK, sk(4)])
            P.I("dve", "tensor_scalar", out=S8[:, 5:6], in0=S8[:, 4:5], scalar1=topk, scalar2=2.0 * wk,
                op0=ALU.is_ge, op1=ALU.mult, r=[sk(4)], w=[sk(5)])
            P.I("dve", "scalar_tensor_tensor", out=S8[:, 3:4], in0=S8[:, 5:6], scalar=-wk, in1=S8[:, 3:4],
                op0=ALU.add, op1=ALU.add, r=[sk(5), sk(3)], w=[sk(3)])
        P.I("dve", "tensor_scalar", out=S8[:, 6:7], in0=S8[:, 3:4], scalar1=-(2.0 ** -(NBIS + 1)), scalar2=None,
            op0=ALU.add, r=[sk(3)], w=[sk(6)])
        P.I("dve", "tensor_scalar", out=M_[:, 0:L], in0=I_[:, 0:L], scalar1=S8[:, 6:7], scalar2=-30000.0,
            op0=ALU.is_lt, op1=ALU.mult, r=[IK, sk(6)], w=[MK])

    def ATT(j):
        s = j % 2
        nk = j + 1
        Tq = slice(j * 128, (j + 1) * 128)
        M_, MK = mb[s], ("mb", s)
        for g in range(2):
            P.I("pe", "matmul", BO[g][:, :], lhsT=k.zeros_bf[:, 0:128], rhs=k.zeros_bf[:, :], start=True, stop=False,
                w=[("bo", g)])
        P.I("pe", "matmul", BD[:, 0:8], lhsT=k.zeros_bf[:, 0:128], rhs=k.zeros_bf[:, 0:8], start=True, stop=False,
            w=["bd"])
        tiles = [(kt, g) for kt in range(nk) for g in range(2)]
        bsel = {}

        def emit_S(i):
            kt, g = tiles[i]
            ks = slice(kt * 128, (kt + 1) * 128)
            b = ci["s"] % 2
            ci["s"] += 1
            bsel[i] = b
            P.I("pe", "matmul", BS[b][:, :], lhsT=kT[:, g, ks],
                rhs=qt[s][:, 4 * g:4 * g + 4, :].rearrange("p h t -> p (h t)"), start=True, stop=False,
                r=["kT", ("qt", s)], w=[("bs", b)])
            P.I("pe", "matmul", BS[b][:, :], lhsT=M_[:, ks], rhs=k.i4[:, :], start=False, stop=True,
                r=[MK], w=[("bs", b)])

        emit_S(0)
        for i, (kt, g) in enumerate(tiles):
            if i + 1 < len(tiles):
                emit_S(i + 1)
            b = bsel[i]
            last = (kt == nk - 1)
            p_ = ci["p"] % 3
            ci["p"] += 1
            P.I("act", "activation", out=pt[p_][:, :], in_=BS[b][:, :], func=AF.Exp, r=[("bs", b)], w=[("pt", p_)])
            for hh in range(4):
                P.I("pe", "matmul", BO[g][:, hh * 128:(hh + 1) * 128], lhsT=pt[p_][:, hh * 128:(hh + 1) * 128],
                    rhs=V[:, kt, g * 128:(g + 1) * 128], start=False, stop=last,
                    r=[("pt", p_), ("V", (kt // 8) * 8)], w=[("bo", g)])
                P.I("pe", "matmul", BD[:, g * 4 + hh:g * 4 + hh + 1], lhsT=pt[p_][:, hh * 128:(hh + 1) * 128],
                    rhs=k.ones_bf[:, 0:1], start=False, stop=last, r=[("pt", p_)], w=["bd"])
        P.I("dve", "reciprocal", out=rden[:, :], in_=BD[:, 0:8], r=["bd"], w=["rden"])
        for g in range(2):
            P.I("dve", "tensor_tensor", out=osb[:, 4 * g:4 * g + 4, :],
                in0=BO[g][:, :].rearrange("p (h d) -> p h d", d=128),
                in1=rden[:, 4 * g:4 * g + 4].unsqueeze(2).to_broadcast([128, 4, 128]), op=ALU.mult,
                r=[("bo", g), "rden"], w=["osb"])
        P.dma("pool", "a_oa", k.oa[Tq, :], osb[:].rearrange("p h d -> p (h d)"), r=["osb"])

    PRE(0)
    for j in range(NT):
        if j + 1 < NT:
            PRE(j + 1)
        ATT(j)
    P.end_phase()


def phase_C(k, l, src, last):
    P, NT = k.P, k.NT
    P.begin_phase()

    def T(name, shape, dt=F32):
        return P.sb(name, shape, dt)

    wo = T("wo", [128, 8, D], BF16)
    wov = k.wout[l].rearrange("(kc p) c -> p kc c", p=128)
    wof = [T("wof", [128, D]) for _ in range(2)]
    for kc in range(8):
        f = kc % 2
        P.dma("sp", ("c_wof", f), wof[f][:], wov[:, kc, :], w=[("wof", f)])
        P.I("pool", "tensor_copy", out=wo[:, kc, :], in_=wof[f][:], r=[("wof", f)], w=[("wo", kc)])
    fg = T("fg", [128, D])
    if last:
        P.dma("sp", "c_fg", fg[:], k.fgain.partition_broadcast(128), w=["fg"])
    oa = [T("oa", [128, D]) for _ in range(2)]
    za = [T("za", [128, D]) for _ in range(2)]
    gt = [T("gt", [128, 2 * D]) for _ in range(2)]
    yb = [T("yb", [128, D]) for _ in range(2)]
    xt = [T("xt", [128, D]) for _ in range(2)]
    mx = [T("mx", [128, D], BF16) for _ in range(2)]
    mT = [T("mT", [128, 8, 128], BF16) for _ in range(2)]
    xn = [T("xn", [128, D]) for _ in range(2)]
    junk = T("junk", [128, D], BF16)
    ss = T("ss", [128, 2])
    wk = [("wo", kc) for kc in range(8)]
    for i in range(NT):
        s = i % 2
        R = slice(i * 128, (i + 1) * 128)
        P.dma("sp", ("c_oa", s), oa[s][:], k.oa[R, :], w=[("oa", s)])
        P.dma("sp", ("c_za", s), za[s][:], k.za[R, :], w=[("za", s)])
        P.dma("sp", ("c_gt", s), gt[s][:], k.gt[R, :], w=[("gt", s)])
        P.dma("sp", ("c_yb", s), yb[s][:], k.yb[R, :], w=[("yb", s)])
        P.dma("sp", ("c_xt", s), xt[s][:], src[R, :], w=[("xt", s)])
        P.I("pool", "tensor_tensor", out=oa[s][:], in0=oa[s][:], in1=za[s][:], op=ALU.mult,
            r=[("oa", s), ("za", s)], w=[("oa", s)])
        P.I("pool", "tensor_tensor", out=oa[s][:], in0=oa[s][:], in1=gt[s][:, 0:D], op=ALU.mult,
            r=[("oa", s), ("gt", s)], w=[("oa", s)])
        P.I("dve", "tensor_tensor", out=yb[s][:], in0=yb[s][:], in1=gt[s][:, D:2 * D], op=ALU.mult,
            r=[("yb", s), ("gt", s)], w=[("yb", s)])
        P.I("dve", "tensor_tensor", out=mx[s][:], in0=oa[s][:], in1=yb[s][:], op=ALU.add,
            r=[("oa", s), ("yb", s)], w=[("mx", s)])
        bi, bank = next_bank(k)
        pT = bank[:].bitcast(BF16).rearrange("p (c t) -> p c t", t=128)
        for kc in range(8):
            P.I("pe", "transpose", out=pT[:, kc, :], in_=mx[s][:, kc * 128:(kc + 1) * 128], identity=k.ident_bf[:],
                r=[("mx", s)], w=[("pb", bi)])
        P.I("act", "copy", out=mT[s][:], in_=pT, r=[("pb", bi)], w=[("mT", s)])
        for half in range(2):
            bo, bko = next_bank(k)
            for kc in range(8):
                P.I("pe", "matmul", bko[:, :], lhsT=mT[s][:, kc, :], rhs=wo[:, kc, half * 512:(half + 1) * 512],
                    start=(kc == 0), stop=(kc == 7), r=[("mT", s), wk[kc]], w=[("pb", bo)])
            P.I("dve", "tensor_tensor", out=xn[s][:, half * 512:(half + 1) * 512], in0=xt[s][:, half * 512:(half + 1) * 512],
                in1=bko[:, :], op=ALU.add, r=[("xt", s), ("pb", bo)], w=[("xn", s, half)])
        xk = [("xn", s, 0), ("xn", s, 1)]
        if not last:
            P.dma("pool", ("c_st", s), k.x1[R, :], xn[s][:], r=xk)
        else:
            P.I("act", "activation", out=junk[:], in_=xn[s][:], func=AF.Square, accum_out=ss[:, s:s + 1],
                r=xk, w=["junk", ("ss", s)])
            P.I("act", "activation", out=ss[:, s:s + 1], in_=ss[:, s:s + 1], func=AF.Sqrt, bias=k.cst[:, 4:5],
                scale=1.0 / D, r=[("ss", s)], w=[("ss", s)])
            P.I("dve", "reciprocal", out=ss[:, s:s + 1], in_=ss[:, s:s + 1], r=[("ss", s)], w=[("ss", s)])
            P.I("dve", "scalar_tensor_tensor", out=xn[s][:], in0=xn[s][:], scalar=ss[:, s:s + 1], in1=fg[:],
                op0=ALU.mult, op1=ALU.mult, r=xk + [("ss", s), "fg"], w=xk)
            P.dma("pool", ("c_st", s), k.out[R, :], xn[s][:], r=xk)
    P.end_phase()


_CACHE = {}
SEQ_FULL = 8192
NLAYERS = 2
NCORES = 2


def _in_maps(inp, NL):
    consts = host_consts()
    per_layer = {}
    for l in range(NL):
        per_layer["wfm%d" % l] = host_w_fm(inp["w_in"], l)
        per_layer["wtm%d" % l] = host_w_tm(inp["w_in"], l)
        per_layer["wout%d" % l] = np.ascontiguousarray(inp["w_out"][l], dtype=np.float32)
        per_layer["ng%d" % l] = np.ascontiguousarray(inp["norm_gain"][l][None, :], dtype=np.float32)
        cw = np.asarray(inp["conv_w"][l], dtype=np.float32)
        per_layer["convw%d" % l] = np.ascontiguousarray(cw.reshape(4, 24, 128).transpose(2, 1, 0))
        per_layer["gdng%d" % l] = np.ascontiguousarray(inp["gdn_norm_gain"][l][None, :], dtype=np.float32)
        per_layer["alog%d" % l] = np.ascontiguousarray(inp["a_log"][l][None, :], dtype=np.float32)
        per_layer["dtb%d" % l] = np.ascontiguousarray(inp["dt_bias"][l][None, :], dtype=np.float32)
        g = np.asarray(inp["idx_k_gain"][l], dtype=np.float32)
        gs = np.zeros(64, np.float32)
        gs[:8] = g[8:16]
        gs[8:16] = g[0:8]
        per_layer["ikg%d" % l] = np.ascontiguousarray(np.stack([g, gs], 1))
    maps = []
    for c in range(NCORES):
        b = c % 2
        m = {"x": np.ascontiguousarray(inp["x"][b], dtype=np.float32),
             "pos": np.ascontiguousarray(inp["positions"][b:b + 1]).astype(np.int32),
             "fgain": np.ascontiguousarray(np.asarray(inp["final_gain"], dtype=np.float32)[None, :])}
        m.update(per_layer)
        m.update(consts)
        maps.append(m)
    return maps


def kernel(x, positions, norm_gain, w_in, conv_w, a_log, dt_bias, gdn_norm_gain, idx_k_gain, w_out, final_gain):
    inp = {"x": np.asarray(x), "positions": np.asarray(positions), "norm_gain": np.asarray(norm_gain),
           "w_in": np.asarray(w_in), "conv_w": np.asarray(conv_w), "a_log": np.asarray(a_log),
           "dt_bias": np.asarray(dt_bias), "gdn_norm_gain": np.asarray(gdn_norm_gain),
           "idx_k_gain": np.asarray(idx_k_gain), "w_out": np.asarray(w_out), "final_gain": np.asarray(final_gain)}
    B, S, _ = inp["x"].shape
    NL = inp["w_in"].shape[0]
    assert B == 2
    nc = build(S, NL)
    maps = _in_maps(inp, NL)
    res = run_bass_kernel_spmd(nc, maps, core_ids=list(range(NCORES)))
    out = np.stack([np.asarray(res.results[b]["out"], dtype=np.float32) for b in range(2)], axis=0)
    return out
```

```python
import numpy as np
from contextlib import ExitStack
import concourse.bass as bass
import concourse.mybir as mybir

F32 = mybir.dt.float32
BF16 = mybir.dt.bfloat16
I32 = mybir.dt.int32
ALU = mybir.AluOpType
AF = mybir.ActivationFunctionType
AX = mybir.AxisListType


class _Op:
    __slots__ = ("idx", "eng", "fn", "deps", "dma", "dseq", "sig", "signals")

    def __init__(self, idx, eng, fn, deps, dma):
        self.idx = idx
        self.eng = eng
        self.fn = fn
        self.deps = deps
        self.dma = dma
        self.dseq = 0
        self.sig = 0
        self.signals = False


_PSUM_T = ("pb", "bi", "bs", "bo")
_PSUM_S = ("b6", "b7", "bd")


def _is_psum_key(x):
    return (isinstance(x, tuple) and x[0] in _PSUM_T) or (isinstance(x, str) and x in _PSUM_S)


class Prog:
    ENGS = ("pe", "act", "dve", "pool", "sp")

    def __init__(self, nc):
        self.nc = nc
        self.ops = []
        self.flushed = 0
        self.lastw = {}
        self.readers = {}
        self.dma_prev = {}
        self.dma_cnt = {}
        self.last_eng = {}
        self.es = ExitStack()
        self.pes = None
        self._n = 0
        self.esem = {e: self.es.enter_context(nc.semaphore("s_" + e)) for e in self.ENGS}
        self.dsem = {}
        self.cnt = {e: 0 for e in self.ENGS}
        self.waited = {e: {} for e in self.ENGS}

    def _nm(self, name):
        self._n += 1
        return "%s_%d" % (name, self._n)

    def sb(self, name, shape, dt, persist=False):
        st = self.es if (persist or self.pes is None) else self.pes
        return st.enter_context(self.nc.sbuf_tensor(self._nm(name), list(shape), dt))

    def ps(self, name, shape, dt):
        return self.es.enter_context(self.nc.psum_tensor(self._nm(name), list(shape), dt))

    def begin_phase(self):
        self.pes = ExitStack()

    def end_phase(self):
        self.barrier()
        self.flush()
        self.pes.close()
        self.pes = None

    def op(self, eng, fn, r=(), w=(), dma=None, deps=None):
        idx = len(self.ops)
        deps = set(deps) if deps is not None else set()
        pr = [x for x in r if _is_psum_key(x)]
        if pr:
            r = [x for x in r if not _is_psum_key(x)]
            w = list(w) + pr
        for k in r:
            if k in self.lastw:
                deps.add(self.lastw[k])
        for k in w:
            if k in self.lastw:
                deps.add(self.lastw[k])
            for q in self.readers.get(k, ()):
                deps.add(q)
        if dma is not None:
            if dma in self.dma_prev:
                deps.add(self.dma_prev[dma])
            self.dma_prev[dma] = idx
        o = _Op(idx, eng, fn, deps, dma)
        if dma is not None:
            self.dma_cnt[dma] = self.dma_cnt.get(dma, 0) + 1
            o.dseq = self.dma_cnt[dma]
        else:
            self.last_eng[eng] = idx
        self.ops.append(o)
        for k in w:
            self.lastw[k] = idx
            self.readers[k] = []
        for k in r:
            lst = self.readers.setdefault(k, [])
            if dma is None:
                lst[:] = [q for q in lst if not (self.ops[q].dma is None and self.ops[q].eng == eng)]
            lst.append(idx)
        return idx

    def pe(self, fn, r=(), w=()):
        return self.op("pe", fn, r, w)

    def act(self, fn, r=(), w=()):
        return self.op("act", fn, r, w)

    def dve(self, fn, r=(), w=()):
        return self.op("dve", fn, r, w)

    def pool(self, fn, r=(), w=()):
        return self.op("pool", fn, r, w)

    def I(self, eng, meth, *args, r=(), w=(), **kw):
        return self.op(eng, lambda e: getattr(e, meth)(*args, **kw), r, w)

    def dma(self, q, key, out, in_, r=(), w=(), **kw):
        return self.op(q, lambda e: e.dma_start(out=out, in_=in_, **kw), r, w, dma=key)

    def barrier(self):
        deps = set(i for i in self.last_eng.values() if i >= self.flushed) | set(self.dma_prev.values())
        for e in self.ENGS:
            self.op(e, lambda eng: eng.nop(), deps=deps)
        self.lastw.clear()
        self.readers.clear()

    def flush(self):
        nc = self.nc
        ops = self.ops
        batch = ops[self.flushed:]
        for o in batch:
            for d in o.deps:
                p = ops[d]
                if p.dma is None:
                    if p.eng == "pe" and o.eng == "pe" and o.dma is None:
                        continue
                    assert d >= self.flushed, "dependency on already-flushed compute op"
                    p.signals = True
        for o in batch:
            if o.dma is None and o.signals:
                self.cnt[o.eng] += 1
                o.sig = self.cnt[o.eng]
            if o.dma is not None and o.dma not in self.dsem:
                self.dsem[o.dma] = self.es.enter_context(nc.semaphore(self._nm("d")))
        esem, dsem = self.esem, self.dsem
        per = {e: [o for o in batch if o.eng == e] for e in self.ENGS}

        def run(ename, eng):
            waited = self.waited[ename]
            for o in per[ename]:
                need = {}
                for d in o.deps:
                    p = ops[d]
                    if p.dma is not None:
                        s, v = dsem[p.dma], 16 * p.dseq
                    else:
                        if p.eng == "pe" and ename == "pe" and o.dma is None:
                            continue
                        s, v = esem[p.eng], p.sig
                    if waited.get(s.num, 0) >= v:
                        continue
                    if need.get(s.num, (None, 0))[1] < v:
                        need[s.num] = (s, v)
                for s, v in need.values():
                    eng.wait_ge(s, v)
                    waited[s.num] = v
                ins = o.fn(eng)
                if o.dma is not None:
                    ins.then_inc(dsem[o.dma], 16)
                elif o.signals:
                    ins.then_inc(esem[ename], 1)

        with nc.Block() as block:
            @block.tensor
            def _(e):
                run("pe", e)

            @block.scalar
            def _(e):
                run("act", e)

            @block.vector
            def _(e):
                run("dve", e)

            @block.gpsimd
            def _(e):
                run("pool", e)

            @block.sync
            def _(e):
                run("sp", e)
        self.flushed = len(ops)

    def finish(self):
        if self.flushed < len(self.ops):
            self.barrier()
            self.flush()
        self.es.close()

import math
import ml_dtypes
from concourse.bass_utils import run_bass_kernel_spmd
import math
import numpy as np
import ml_dtypes

D = 1024
EPS = 1e-6
THETA = 500000.0
NBIS = 15
TOPK = 256
PI = math.pi
NEG = -1.0e30


def fm_blocks():
    b = []
    for h in range(8):
        b.append(("qa", h, 128, 32))
    for g in range(2):
        b.append(("ka", g, 128, 32))
    for h in range(8):
        b.append(("qi", h, 64, 16))
    b.append(("ki", 0, 64, 16))
    for i in range(24):
        b.append(("gd", i, 128, 0))
    return b


FM = fm_blocks()
FM_GROUPS = [FM[0:8], FM[8:19], FM[19:27], FM[27:35], FM[35:43]]
NFM = sum(b[2] + b[3] for b in FM)
TM = ([("v", 1280, 256), ("za", 1536, 512), ("za", 2048, 512), ("wi", 3136, 8)] +
      [("gt", 7256 + 512 * i, 512) for i in range(4)] +
      [("zb", 6216, 512), ("zb", 6728, 512), ("ba", 7240, 16)])
TM_GROUPS = [TM[0:3], TM[3:6], TM[6:8], TM[8:11]]
NTM = sum(c[2] for c in TM)
WG_MAX = 1280


def host_w_fm(w_in, l):
    W = w_in[l]
    cols = []
    for kind, i, nm, ns in FM:
        if kind == "qa":
            c0 = i * 128
            cols += [W[:, c0:c0 + 128], W[:, c0 + 16:c0 + 32], W[:, c0:c0 + 16]]
        elif kind == "ka":
            c0 = 1024 + i * 128
            cols += [W[:, c0:c0 + 128], W[:, c0 + 16:c0 + 32], W[:, c0:c0 + 16]]
        elif kind == "qi":
            c0 = 2560 + i * 64
            cols += [W[:, c0:c0 + 64], W[:, c0 + 8:c0 + 16], W[:, c0:c0 + 8]]
        elif kind == "ki":
            c0 = 3072
            cols += [W[:, c0:c0 + 64], W[:, c0 + 8:c0 + 16], W[:, c0:c0 + 8]]
        else:
            c0 = 3144 + i * 128
            cols += [W[:, c0:c0 + 128]]
    return np.ascontiguousarray(np.concatenate(cols, axis=1))


def host_w_tm(w_in, l):
    W = w_in[l]
    return np.ascontiguousarray(np.concatenate([W[:, c0:c0 + n] for _, c0, n in TM], axis=1))


def host_consts():
    c = {}
    c["ident_bf"] = np.eye(128, dtype=ml_dtypes.bfloat16)
    c["ident_f"] = np.eye(128, dtype=np.float32)
    c["i4"] = np.concatenate([np.eye(128)] * 4, axis=1).astype(ml_dtypes.bfloat16)
    q = np.arange(128)[:, None]
    k = np.arange(128)[None, :]
    c["tri_bias"] = np.where(k <= q, 0.0, NEG).astype(np.float32)
    j = np.arange(64)[:, None]
    i = np.arange(64)[None, :]
    m = np.zeros((64, 5, 64), np.float32)
    m[:, 0, :] = (j <= i)
    m[:, 1, :] = (i < j)
    m[:, 2, :] = np.where(i >= j, 0.0, NEG)
    m[:, 3, :] = np.where(i < j, 0.0, NEG)
    m[:, 4, :] = np.eye(64)
    c["m64"] = m
    cs = np.zeros((128, 8), np.float32)
    r = np.arange(32)
    cs[:32, 0] = THETA ** (-(2.0 * (r % 16)) / 32.0)
    r16 = np.arange(16)
    cs[:16, 1] = THETA ** (-(2.0 * (r16 % 8)) / 16.0)
    cs[:32, 2] = np.where(r < 16, -1.0, 1.0)
    cs[:16, 3] = np.where(r16 < 8, -1.0, 1.0)
    cs[:, 4] = EPS
    cs[:, 5] = 1.0
    c["cst"] = cs
    return c


class K:
    pass


def build(SEQ, NL, dbg=(), phases=('tab', 'A1', 'A2', 'gdn', 'attn', 'C')):
    nc = bass.Bass("TRN2", target_bir_lowering=False)
    k = K()
    k.nc, k.SEQ, k.NL = nc, SEQ, NL
    NT = SEQ // 128
    NB = SEQ // 512
    k.NT, k.NB = NT, NB

    def inp(name, shape, dt=F32):
        return nc.dram_tensor(name, list(shape), dt, kind="ExternalInput").ap()

    def scr(name, shape, dt=F32):
        kind = "ExternalOutput" if name in dbg else "Internal"
        return nc.dram_tensor(name, list(shape), dt, kind=kind).ap()

    k.x = inp("x", [SEQ, D])
    k.pos = inp("pos", [1, SEQ], I32)
    k.fgain = inp("fgain", [1, D])
    k.wfm = [inp("wfm%d" % l, [D, NFM]) for l in range(NL)]
    k.wtm = [inp("wtm%d" % l, [D, NTM]) for l in range(NL)]
    k.wout = [inp("wout%d" % l, [D, D]) for l in range(NL)]
    k.ng = [inp("ng%d" % l, [1, D]) for l in range(NL)]
    k.convw = [inp("convw%d" % l, [128, 24, 4]) for l in range(NL)]
    k.gdng = [inp("gdng%d" % l, [1, 128]) for l in range(NL)]
    k.alog = [inp("alog%d" % l, [1, 8]) for l in range(NL)]
    k.dtb = [inp("dtb%d" % l, [1, 8]) for l in range(NL)]
    k.ikg = [inp("ikg%d" % l, [64, 2]) for l in range(NL)]
    k.c_ident_bf = inp("ident_bf", [128, 128], BF16)
    k.c_ident_f = inp("ident_f", [128, 128])
    k.c_i4 = inp("i4", [128, 512], BF16)
    k.c_tri = inp("tri_bias", [128, 128])
    k.c_m64 = inp("m64", [64, 5, 64])
    k.c_cst = inp("cst", [128, 8])
    k.out = nc.dram_tensor("out", [SEQ, D], F32, kind="ExternalOutput").ap()

    k.tab = scr("tab", [4, 32, SEQ])
    k.hT = scr("hT", [128, 8, SEQ], BF16)
    k.qT = scr("qT", [8, 128, SEQ], BF16)
    k.kT = scr("kT", [2, 128, SEQ], BF16)
    k.qiT = scr("qiT", [8, 64, SEQ], BF16)
    k.kiT = scr("kiT", [64, SEQ], BF16)
    k.gT = scr("gT", [24, 128, SEQ])
    k.v = scr("v", [SEQ, 256], BF16)
    k.za = scr("za", [SEQ, D])
    k.wi = scr("wi", [SEQ, 8])
    k.gt = scr("gt", [SEQ, 2 * D])
    k.zb = scr("zb", [SEQ, D])
    k.ba = scr("ba", [SEQ, 16])
    k.yb = scr("yb", [SEQ, D])
    k.oa = scr("oa", [SEQ, D])
    k.x1 = scr("x1", [SEQ, D])

    P = Prog(nc)
    k.P = P
    k.banks = [P.ps("bank", [128, 512], F32) for _ in range(8)]
    k.bi = 0

    k.ident_bf = P.sb("ident_bf", [128, 128], BF16, persist=True)
    k.ident_f = P.sb("ident_f", [128, 128], F32, persist=True)
    k.i4 = P.sb("i4", [128, 512], BF16, persist=True)
    k.tri = P.sb("tri", [128, 128], F32, persist=True)
    k.m64 = P.sb("m64", [64, 5, 64], F32, persist=True)
    k.cst = P.sb("cst", [128, 8], F32, persist=True)
    k.ones_f = P.sb("ones_f", [128, 128], F32, persist=True)
    k.ones_bf = P.sb("ones_bf", [128, 8], BF16, persist=True)
    k.zeros_bf = P.sb("zeros_bf", [128, 512], BF16, persist=True)
    k.ones_b = P.sb("ones_b", [128, 128], BF16, persist=True)
    k.su_b = P.sb("su_b", [64, 64], BF16, persist=True)
    P.begin_phase()
    for t, src in ((k.ident_bf, k.c_ident_bf), (k.ident_f, k.c_ident_f), (k.i4, k.c_i4), (k.tri, k.c_tri),
                   (k.m64, k.c_m64), (k.cst, k.c_cst)):
        P.dma("sp", ("cload", id(t)), t[:], src, w=[("cst", "m64")] if t is k.m64 else [])
    P.pool(lambda e: e.memset(k.ones_f[:], 1.0))
    P.pool(lambda e: e.memset(k.ones_bf[:], 1.0))
    P.pool(lambda e: e.memset(k.zeros_bf[:], 0.0))
    P.pool(lambda e: e.memset(k.ones_b[:], 1.0))
    P.I("dve", "tensor_copy", out=k.su_b[:], in_=k.m64[:, 1, :], r=[("cst", "m64")], w=["su_b"])
    P.end_phase()

    if 'tab' in phases:
        phase_tables(k)
    for l in range(NL):
        src = k.x if l == 0 else k.x1
        if 'A1' in phases:
            phase_A1(k, l, src)
        if 'A2' in phases:
            phase_A2(k, l)
        if 'gdn' in phases:
            phase_gdn(k, l)
        if 'attn' in phases:
            phase_attn(k, l)
        if 'C' in phases:
            phase_C(k, l, src, last=(l == NL - 1))
    P.finish()
    return nc


def next_bank(k):
    i = k.bi % 8
    k.bi += 1
    return i, k.banks[i]


def phase_tables(k):
    P, SEQ = k.P, k.SEQ
    P.begin_phase()
    CH = min(SEQ, 2048)
    posi = P.sb("posi", [32, CH], I32)
    posf = P.sb("posf", [32, CH], F32)
    ang = P.sb("ang", [32, CH], F32)
    u = P.sb("u", [32, CH], F32)
    ki = P.sb("ki", [32, CH], I32)
    kf = P.sb("kf", [32, CH], F32)
    r = P.sb("r", [32, CH], F32)
    w = P.sb("w", [32, CH], F32)
    res = P.sb("res", [32, CH], F32)
    C1 = float(np.float32(2 * PI))
    C2 = float(2 * PI - C1)
    LIM = 3.14159
    for c0 in range(0, SEQ, CH):
        P.dma("sp", "posi", posi[:], k.pos[:, c0:c0 + CH].partition_broadcast(32), w=["posi"])
        P.dve(lambda e: e.tensor_copy(out=posf[:], in_=posi[:]), r=["posi"], w=["posf"])
        for ti, (rows, fcol, scol, shift) in enumerate(((32, 0, None, PI / 2), (32, 0, 2, 0.0),
                                                        (16, 1, None, PI / 2), (16, 1, 3, 0.0))):
            R = slice(0, rows)
            P.dve(lambda e, R=R, fcol=fcol, shift=shift: e.tensor_scalar(
                out=ang[R, :], in0=posf[R, :], scalar1=k.cst[R, fcol:fcol + 1], scalar2=shift,
                op0=ALU.mult, op1=ALU.add), r=["posf"], w=["ang"])
            P.dve(lambda e, R=R: e.tensor_scalar(out=u[R, :], in0=ang[R, :], scalar1=1.0 / (2 * PI), scalar2=None,
                                                 op0=ALU.mult), r=["ang"], w=["u"])
            P.dve(lambda e, R=R: e.tensor_copy(out=ki[R, :], in_=u[R, :]), r=["u"], w=["ki"])
            P.dve(lambda e, R=R: e.tensor_copy(out=kf[R, :], in_=ki[R, :]), r=["ki"], w=["kf"])
            P.dve(lambda e, R=R: e.scalar_tensor_tensor(out=r[R, :], in0=kf[R, :], scalar=-C1, in1=ang[R, :],
                                                        op0=ALU.mult, op1=ALU.add), r=["kf", "ang"], w=["r"])
            P.dve(lambda e, R=R: e.scalar_tensor_tensor(out=r[R, :], in0=kf[R, :], scalar=-C2, in1=r[R, :],
                                                        op0=ALU.mult, op1=ALU.add), r=["kf", "r"], w=["r"])
            P.dve(lambda e, R=R: e.tensor_scalar(out=w[R, :], in0=r[R, :], scalar1=PI, scalar2=-2 * PI,
                                                 op0=ALU.is_gt, op1=ALU.mult), r=["r"], w=["w"])
            P.dve(lambda e, R=R: e.tensor_tensor(out=r[R, :], in0=r[R, :], in1=w[R, :], op=ALU.add),
                  r=["r", "w"], w=["r"])
            P.dve(lambda e, R=R: e.tensor_scalar(out=w[R, :], in0=r[R, :], scalar1=-PI, scalar2=2 * PI,
                                                 op0=ALU.is_lt, op1=ALU.mult), r=["r"], w=["w"])
            P.dve(lambda e, R=R: e.tensor_tensor(out=r[R, :], in0=r[R, :], in1=w[R, :], op=ALU.add),
                  r=["r", "w"], w=["r"])
            P.dve(lambda e, R=R: e.tensor_scalar(out=r[R, :], in0=r[R, :], scalar1=-LIM, scalar2=LIM,
                                                 op0=ALU.max, op1=ALU.min), r=["r"], w=["r"])
            if scol is None:
                P.act(lambda e, R=R: e.activation(out=res[R, :], in_=r[R, :], func=AF.Sin), r=["r"], w=["res"])
            else:
                P.act(lambda e, R=R, scol=scol: e.activation(out=res[R, :], in_=r[R, :], func=AF.Sin,
                                                             scale=k.cst[R, scol:scol + 1]), r=["r"], w=["res"])
            P.dma("sp", "tabst", k.tab[ti, 0:rows, c0:c0 + CH], res[R, :], r=["res"])
    P.end_phase()


def phase_A1(k, l, src):
    P, NT = k.P, k.NT
    P.begin_phase()
    gain_bc = P.sb("gain_bc", [128, D], F32)
    P.dma("sp", "gain", gain_bc[:], k.ng[l].partition_broadcast(128), w=["gain"])
    xt = [P.sb("xt", [128, D], F32) for _ in range(2)]
    junk = P.sb("junk", [128, D], BF16)
    ss = P.sb("ss", [128, NT], F32)
    rstd = P.sb("rstd", [128, NT], F32)
    hb = [P.sb("hb", [128, D], BF16) for _ in range(2)]
    ht = [P.sb("ht", [128, 8, 128], BF16) for _ in range(2)]
    for i in range(NT):
        s = i % 2
        P.dma("sp", ("xt", s), xt[s][:], src[i * 128:(i + 1) * 128, :], w=[("xt", s)])
        P.act(lambda e, s=s, i=i: e.activation(out=junk[:], in_=xt[s][:], func=AF.Square,
                                               accum_out=ss[:, i:i + 1]),
              r=[("xt", s)], w=["junk", ("ss", i)])
        P.act(lambda e, i=i: e.activation(out=rstd[:, i:i + 1], in_=ss[:, i:i + 1], func=AF.Sqrt,
                                          bias=k.cst[:, 4:5], scale=1.0 / D),
              r=[("ss", i)], w=[("rstd", i)])
        P.dve(lambda e, i=i: e.reciprocal(out=rstd[:, i:i + 1], in_=rstd[:, i:i + 1]),
              r=[("rstd", i)], w=[("rstd", i)])
        P.dve(lambda e, s=s, i=i: e.scalar_tensor_tensor(out=hb[s][:], in0=xt[s][:], scalar=rstd[:, i:i + 1],
                                                         in1=gain_bc[:], op0=ALU.mult, op1=ALU.mult),
              r=[("xt", s), ("rstd", i), "gain"], w=[("hb", s)])
        bi, bank = next_bank(k)
        pT = bank[:].bitcast(BF16).rearrange("p (c t) -> p c t", t=128)
        for kc in range(8):
            P.pe(lambda e, s=s, kc=kc, pT=pT: e.transpose(out=pT[:, kc, :], in_=hb[s][:, kc * 128:(kc + 1) * 128],
                                                          identity=k.ident_bf[:]),
                 r=[("hb", s)], w=[("pb", bi)])
        P.act(lambda e, s=s, pT=pT: e.copy(out=ht[s][:], in_=pT), r=[("pb", bi)], w=[("ht", s)])
        P.dma("pool", ("hts", s), k.hT[:, :, i * 128:(i + 1) * 128], ht[s][:], r=[("ht", s)])
    P.end_phase()


def phase_A2(k, l):
    P, NB, SEQ = k.P, k.NB, k.SEQ
    IDX_SCALE = (8 ** -0.5) * (64 ** -0.5)
    P.begin_phase()
    wb = [P.sb("wb", [128, 8, WG_MAX], BF16) for _ in range(2)]
    wf = [P.sb("wf", [128, WG_MAX], F32) for _ in range(2)]
    hb = [P.sb("hblk", [128, 8, 512], BF16) for _ in range(2)]
    ca = [P.sb("ca", [32, 512], F32) for _ in range(2)]
    sa = [P.sb("sa", [32, 512], F32) for _ in range(2)]
    cas = [P.sb("cas", [32, 512], F32) for _ in range(2)]
    sas = [P.sb("sas", [32, 512], F32) for _ in range(2)]
    ci = [P.sb("ci", [16, 512], F32) for _ in range(2)]
    si = [P.sb("si", [16, 512], F32) for _ in range(2)]
    t1 = P.sb("t1", [32, 512], F32)
    t2 = P.sb("t2", [32, 512], F32)
    obf = [P.sb("obf", [128, 512], BF16) for _ in range(3)]
    off = [P.sb("off", [128, 512], F32) for _ in range(3)]
    sq = P.sb("sq", [64, 512], F32)
    rinv = P.sb("rinv", [64, 512], F32)
    kn = P.sb("kn", [64, 512], F32)
    ksw = P.sb("ksw", [16, 512], F32)
    ikg = P.sb("ikg", [64, 2], F32)
    P.dma("sp", "ikg", ikg[:], k.ikg[l], w=["ikg"])
    wfv = k.wfm[l].rearrange("(kc p) c -> p kc c", p=128)
    wtv = k.wtm[l].rearrange("(kc p) c -> p kc c", p=128)
    st = {"w": 0, "h": 0, "o": 0, "q": 0}

    def load_w(view, c0, n):
        s = st["w"] % 2
        st["w"] += 1
        for kc in range(8):
            f = kc % 2
            P.dma("sp", ("wf", f), wf[f][:, :n], view[:, kc, c0:c0 + n], w=[("wf", f)])
            P.I("pool", "tensor_copy", out=wb[s][:, kc, :n], in_=wf[f][:, :n], r=[("wf", f)], w=[("wb", s, kc)])
        return s

    def load_h(tb, need_tab):
        s = st["h"] % 2
        st["h"] += 1
        T = slice(tb * 512, (tb + 1) * 512)
        P.dma("sp", ("hblk", s), hb[s][:], k.hT[:, :, T], w=[("hblk", s)])
        if need_tab:
            P.dma("sp", ("ca", s), ca[s][:], k.tab[0, :, T], w=[("ca", s)])
            P.dma("sp", ("sa", s), sa[s][:], k.tab[1, :, T], w=[("sa", s)])
            P.dma("sp", ("ci", s), ci[s][:], k.tab[2, 0:16, T], w=[("ci", s)])
            P.dma("sp", ("si", s), si[s][:], k.tab[3, 0:16, T], w=[("si", s)])
            P.I("dve", "tensor_scalar", out=cas[s][:], in0=ca[s][:], scalar1=128 ** -0.5, scalar2=None, op0=ALU.mult,
                r=[("ca", s)], w=[("cas", s)])
            P.I("dve", "tensor_scalar", out=sas[s][:], in0=sa[s][:], scalar1=128 ** -0.5, scalar2=None, op0=ALU.mult,
                r=[("sa", s)], w=[("sas", s)])
        return s

    def qname():
        st["q"] += 1
        return "sp" if st["q"] % 2 == 0 else "pool"

    def rope(nrot, bankm, bm, banksw, bs, cos_t, ckey, sin_t, skey, scale, o):
        P.I("dve", "tensor_tensor", out=t1[0:nrot, :], in0=bankm[0:nrot, :], in1=cos_t[0:nrot, :], op=ALU.mult,
            r=[("pb", bm), ckey], w=["t1"])
        P.I("dve", "tensor_tensor", out=t2[0:nrot, :], in0=banksw[0:nrot, :], in1=sin_t[0:nrot, :], op=ALU.mult,
            r=[("pb", bs), skey], w=["t2"])
        P.I("dve", "tensor_tensor", out=obf[o][0:nrot, :], in0=t1[0:nrot, :], in1=t2[0:nrot, :], op=ALU.add,
            r=["t1", "t2"], w=[("obf", o)])

    import os
    for gi, grp in enumerate(FM_GROUPS):
        if str(gi) not in os.environ.get('MK_FM', '01234'):
            continue
        gc0 = sum(b[2] + b[3] for g in FM_GROUPS[:gi] for b in g)
        gn = sum(b[2] + b[3] for b in grp)
        ws = load_w(wfv, gc0, gn)
        wk = [("wb", ws, kc) for kc in range(8)]
        for tb in range(NB):
            hs = load_h(tb, gi <= 1)
            T = slice(tb * 512, (tb + 1) * 512)
            c0 = 0
            for (kind, idx, nm, ns) in grp:
                bm, bankm = next_bank(k)
                for kc in range(8):
                    P.pe(lambda e, ws=ws, hs=hs, kc=kc, c0=c0, nm=nm, bankm=bankm: e.matmul(
                        bankm[0:nm, :], lhsT=wb[ws][:, kc, c0:c0 + nm], rhs=hb[hs][:, kc, :],
                        start=(kc == 0), stop=(kc == 7)), r=[wk[kc], ("hblk", hs)], w=[("pb", bm)])
                bs, banksw = None, None
                if ns and os.environ.get("MK_QA", "full") != "copy":
                    bs, banksw = next_bank(k)
                    for kc in range(8):
                        P.pe(lambda e, ws=ws, hs=hs, kc=kc, c0=c0, nm=nm, ns=ns, banksw=banksw: e.matmul(
                            banksw[0:ns, :], lhsT=wb[ws][:, kc, c0 + nm:c0 + nm + ns], rhs=hb[hs][:, kc, :],
                            start=(kc == 0), stop=(kc == 7)), r=[wk[kc], ("hblk", hs)], w=[("pb", bs)])
                c0 += nm + ns
                o = st["o"] % 3
                st["o"] += 1
                if kind in ("qa", "ka"):
                    scale = 128 ** -0.5 if kind == "qa" else 1.0
                    P.act(lambda e, o=o, bankm=bankm, scale=scale: e.activation(
                        out=obf[o][:, :], in_=bankm[:, :], func=AF.Copy, scale=scale),
                        r=[("pb", bm)], w=[("obf", o)])
                    if os.environ.get("MK_QA", "full") == "full":
                        if kind == "qa":
                            rope(32, bankm, bm, banksw, bs, cas[hs], ("cas", hs), sas[hs], ("sas", hs), scale, o)
                        else:
                            rope(32, bankm, bm, banksw, bs, ca[hs], ("ca", hs), sa[hs], ("sa", hs), scale, o)
                    dst = (k.qT if kind == "qa" else k.kT)[idx, :, T]
                    P.dma(qname(), ("obf", o), dst, obf[o][:, :], r=[("obf", o)])
                elif kind == "qi":
                    P.act(lambda e, o=o, bankm=bankm: e.copy(out=obf[o][0:64, :], in_=bankm[0:64, :]),
                          r=[("pb", bm)], w=[("obf", o)])
                    rope(16, bankm, bm, banksw, bs, ci[hs], ("ci", hs), si[hs], ("si", hs), 1.0, o)
                    P.dma(qname(), ("obf", o), k.qiT[idx, :, T], obf[o][0:64, :], r=[("obf", o)])
                elif kind == "ki":
                    P.act(lambda e, bankm=bankm: e.activation(out=sq[:, :], in_=bankm[0:64, :], func=AF.Square),
                          r=[("pb", bm)], w=["sq"])
                    bq, bankq = next_bank(k)
                    P.pe(lambda e, bankq=bankq: e.matmul(bankq[0:64, :], lhsT=k.ones_f[0:64, 0:64], rhs=sq[:, :],
                                                         start=True, stop=True), r=["sq"], w=[("pb", bq)])
                    P.act(lambda e, bankq=bankq: e.activation(out=rinv[:, :], in_=bankq[0:64, :], func=AF.Sqrt,
                                                              bias=k.cst[0:64, 4:5], scale=1.0 / 64),
                          r=[("pb", bq)], w=["rinv"])
                    P.dve(lambda e: e.reciprocal(out=rinv[:, :], in_=rinv[:, :]), r=["rinv"], w=["rinv"])
                    P.dve(lambda e, bankm=bankm: e.scalar_tensor_tensor(
                        out=kn[:, :], in0=bankm[0:64, :], scalar=ikg[:, 0:1], in1=rinv[:, :],
                        op0=ALU.mult, op1=ALU.mult), r=[("pb", bm), "rinv", "ikg"], w=["kn"])
                    P.dve(lambda e, banksw=banksw: e.scalar_tensor_tensor(
                        out=ksw[:, :], in0=banksw[0:16, :], scalar=ikg[0:16, 1:2], in1=rinv[0:16, :],
                        op0=ALU.mult, op1=ALU.mult), r=[("pb", bs), "rinv", "ikg"], w=["ksw"])
                    P.act(lambda e, o=o: e.copy(out=obf[o][0:64, :], in_=kn[:, :]), r=["kn"], w=[("obf", o)])
                    P.dve(lambda e, hs=hs: e.tensor_tensor(out=t1[0:16, :], in0=kn[0:16, :], in1=ci[hs][:, :],
                                                           op=ALU.mult), r=["kn", ("ci", hs)], w=["t1"])
                    P.dve(lambda e, hs=hs: e.tensor_tensor(out=t2[0:16, :], in0=ksw[:, :], in1=si[hs][:, :],
                                                           op=ALU.mult), r=["ksw", ("si", hs)], w=["t2"])
                    P.dve(lambda e, o=o: e.tensor_tensor(out=obf[o][0:16, :], in0=t1[0:16, :], in1=t2[0:16, :],
                                                         op=ALU.add), r=["t1", "t2"], w=[("obf", o)])
                    P.dma(qname(), ("obf", o), k.kiT[:, T], obf[o][0:64, :], r=[("obf", o)])
                else:
                    if o % 2 == 0:
                        P.act(lambda e, o=o, bankm=bankm: e.copy(out=off[o][:, :], in_=bankm[:, :]),
                              r=[("pb", bm)], w=[("off", o)])
                    else:
                        P.dve(lambda e, o=o, bankm=bankm: e.tensor_copy(out=off[o][:, :], in_=bankm[:, :]),
                              r=[("pb", bm)], w=[("off", o)])
                    P.dma(qname(), ("off", o), k.gT[idx, :, T], off[o][:, :], r=[("off", o)])

    for gi, grp in enumerate(TM_GROUPS):
        if str(gi) not in os.environ.get('MK_TM', '0123'):
            continue
        gc0 = sum(c[2] for g in TM_GROUPS[:gi] for c in g)
        gn = sum(c[2] for c in grp)
        ws = load_w(wtv, gc0, gn)
        wk = [("wb", ws, kc) for kc in range(8)]
        for tb in range(NB):
            hs = load_h(tb, False)
            for tt in range(4):
                R = slice(tb * 512 + tt * 128, tb * 512 + (tt + 1) * 128)
                c0 = 0
                for (kind, csrc, n) in grp:
                    bm, bankm = next_bank(k)
                    for kc in range(8):
                        P.pe(lambda e, ws=ws, hs=hs, kc=kc, c0=c0, n=n, tt=tt, bankm=bankm: e.matmul(
                            bankm[:, 0:n], lhsT=hb[hs][:, kc, tt * 128:(tt + 1) * 128], rhs=wb[ws][:, kc, c0:c0 + n],
                            start=(kc == 0), stop=(kc == 7)), r=[wk[kc], ("hblk", hs)], w=[("pb", bm)])
                    c0 += n
                    o = st["o"] % 3
                    st["o"] += 1
                    if kind == "v":
                        P.act(lambda e, o=o, bankm=bankm, n=n: e.copy(out=obf[o][:, 0:n], in_=bankm[:, 0:n]),
                              r=[("pb", bm)], w=[("obf", o)])
                        P.dma(qname(), ("obf", o), k.v[R, :], obf[o][:, 0:n], r=[("obf", o)])
                        continue
                    if kind in ("za", "zb"):
                        P.act(lambda e, o=o, bankm=bankm, n=n: e.activation(out=off[o][:, 0:n], in_=bankm[:, 0:n],
                                                                            func=AF.Silu),
                              r=[("pb", bm)], w=[("off", o)])
                        dst = (k.za[R, csrc - 1536:csrc - 1536 + n] if kind == "za"
                               else k.zb[R, csrc - 6216:csrc - 6216 + n])
                    elif kind == "gt":
                        P.act(lambda e, o=o, bankm=bankm, n=n: e.activation(out=off[o][:, 0:n], in_=bankm[:, 0:n],
                                                                            func=AF.Sigmoid),
                              r=[("pb", bm)], w=[("off", o)])
                        dst = k.gt[R, csrc - 7256:csrc - 7256 + n]
                    elif kind == "wi":
                        P.dve(lambda e, o=o, bankm=bankm, n=n: e.tensor_scalar(
                            out=off[o][:, 0:n], in0=bankm[:, 0:n], scalar1=IDX_SCALE, scalar2=None, op0=ALU.mult),
                            r=[("pb", bm)], w=[("off", o)])
                        dst = k.wi[R, :]
                    else:
                        P.dve(lambda e, o=o, bankm=bankm, n=n: e.tensor_copy(out=off[o][:, 0:n], in_=bankm[:, 0:n]),
                              r=[("pb", bm)], w=[("off", o)])
                        dst = k.ba[R, :]
                    P.dma(qname(), ("off", o), dst, off[o][:, 0:n], r=[("off", o)])
    P.end_phase()


def phase_gdn(k, l):
    P, SEQ = k.P, k.SEQ
    NS = SEQ // 512
    DKS = 128 ** -0.5
    P.begin_phase()
    U, SU, MBu, MBl, I64 = (k.m64[:, i, :] for i in range(5))

    def bcn(a):
        return a.unsqueeze(1).to_broadcast([64, 8, 64])

    def bci(a, n=64):
        return a.unsqueeze(2).to_broadcast([64, a.shape[1], n])

    def v3(a, n):
        return a.rearrange("p (c n) -> p c n", n=n)

    rb = {"i": 0}

    def nb4():
        i = rb["i"] % 4
        rb["i"] += 1
        return i, k.banks[i]

    def T(name, shape, dt=F32):
        return P.sb(name, shape, dt)

    convw = T("convw", [128, 24, 4])
    gg = T("gg", [64, 128])
    alog = T("alog", [64, 8])
    dtb = T("dtb", [64, 8])
    negA = T("negA", [64, 8])
    P.dma("sp", "g_convw", convw[:], k.convw[l], w=["convw"])
    P.dma("sp", "g_gg", gg[:], k.gdng[l].partition_broadcast(64), w=["gg"])
    P.dma("sp", "g_alog", alog[:], k.alog[l].partition_broadcast(64), w=["alog"])
    P.dma("sp", "g_dtb", dtb[:], k.dtb[l].partition_broadcast(64), w=["dtb"])
    P.I("act", "activation", out=negA[:], in_=alog[:], func=AF.Exp, r=["alog"], w=["negA"])
    P.I("dve", "tensor_scalar", out=negA[:], in0=negA[:], scalar1=-1.0, scalar2=None, op0=ALU.mult,
        r=["negA"], w=["negA"])
    S = [[T("S", [128, 128]) for _ in range(2)] for _ in range(8)]
    Sb = [[T("Sb", [128, 128], BF16) for _ in range(2)] for _ in range(8)]
    for h in range(8):
        P.I("pool", "memset", S[h][0][:], 0.0, w=[("S", h, 0)])
        P.I("pool", "memset", Sb[h][0][:], 0.0, w=[("Sb", h, 0)])
    bat = T("bat", [64, 8, 16])
    beta = T("beta", [64, 8, 8])
    xa = T("xa", [64, 8, 8])
    gall = T("gall", [64, 8, 8])
    eps64 = k.cst[0:64, 4:5]
    one64 = k.cst[0:64, 5:6]

    def make_set(sid, b6i, b7i):
        B6, B7 = k.banks[b6i], k.banks[b7i]
        K6, K7 = ("pb", b6i), ("pb", b7i)

        def N(x):
            return (sid, x)

        xin = [T("xin", [128, 515]) for _ in range(3)]
        y = [T("y", [128, 512]) for _ in range(3)]
        sqt = T("sqt", [128, 512], BF16)
        yvb = T("yvb", [128, 512], BF16)
        ctmp = T("ctmp", [128, 512])
        rin = [T("rin", [128, 512]) for _ in range(2)]
        qn = T("qn", [128, 512], BF16)
        kn = T("kn", [128, 512], BF16)
        egbc = ctmp
        qd = T("qd", [128, 512], BF16)
        gh = T("gh", [64, 8])
        bh = T("bh", [64, 8])
        nbh = T("nbh", [64, 8])
        gcs = T("gcs", [64, 8])
        dgl = T("dgl", [64, 8])
        egl = T("egl", [64, 8])
        eg = T("eg", [64, 8])
        beg = T("beg", [64, 8])
        gtot = T("gtot", [128, 8])
        Gm = T("Gm", [64, 8, 64])
        d3 = T("d3", [64, 8, 64])
        DTu = T("DTu", [64, 8, 64])
        DTl = T("DTl", [64, 8, 64])
        Bm = T("Bm", [64, 8, 64], BF16)
        Mfac = T("Mfac", [64, 8, 64])
        Nm = [T("Nm", [64, 8, 64], BF16) for _ in range(2)]
        NTm = [T("NTm", [64, 8, 64], BF16) for _ in range(2)]
        Rm = [T("Rm", [64, 8, 64], BF16) for _ in range(2)]
        qkT = T("qkT", [64, 8, 64], BF16)
        kbg = T("kbg", [64, 8, 128], BF16)
        kdec = T("kdec", [64, 8, 128], BF16)
        vb = T("vb", [64, 8, 128], BF16)
        us = T("us", [64, 8, 128])
        os_ = T("os", [64, 8, 128])
        sq3 = T("sq3", [64, 8, 128])
        zbt = T("zbt", [64, 8, 128])
        y1 = T("y1", [64, 8, 128])
        wTs = T("wTs", [128, 512], BF16)
        vnew = T("vnew", [64, 128], BF16)
        ssq = T("ssq", [64, 8])
        rinv8 = T("rinv8", [64, 8])

        def unit(s, h):
            t0 = s * 512
            for i, blk in enumerate((h, 8 + h, 16 + h)):
                if s == 0:
                    P.I("pool", "memset", xin[i][:, 0:3], 0.0, w=[N(("xin", i))])
                    P.dma("sp", (sid, "g_xin", i), xin[i][:, 3:515], k.gT[blk, :, 0:512], w=[N(("xin", i))])
                else:
                    P.dma("sp", (sid, "g_xin", i), xin[i][:, :], k.gT[blk, :, t0 - 3:t0 + 512], w=[N(("xin", i))])
                P.I("pool", "tensor_scalar", out=y[i][:], in0=xin[i][:, 0:512], scalar1=convw[:, blk, 0:1],
                    scalar2=None, op0=ALU.mult, r=[N(("xin", i)), "convw"], w=[N(("y", i))])
                for j in range(1, 4):
                    P.I("pool", "tensor_scalar", out=ctmp[:], in0=xin[i][:, j:j + 512],
                        scalar1=convw[:, blk, j:j + 1], scalar2=None, op0=ALU.mult,
                        r=[N(("xin", i)), "convw"], w=[N("ctmp")])
                    P.I("pool", "tensor_tensor", out=y[i][:], in0=y[i][:], in1=ctmp[:], op=ALU.add,
                        r=[N("ctmp"), N(("y", i))], w=[N(("y", i))])
                if i < 2:
                    P.I("act", "activation", out=y[i][:], in_=y[i][:], func=AF.Silu, r=[N(("y", i))],
                        w=[N(("y", i))])
                else:
                    P.I("act", "activation", out=yvb[:], in_=y[i][:], func=AF.Silu, r=[N(("y", i))], w=[N("yvb")])
                yield
            for i in range(2):
                P.I("act", "activation", out=sqt[:], in_=y[i][:], func=AF.Square, r=[N(("y", i))], w=[N("sqt")])
                bi, bk = nb4()
                P.I("pe", "matmul", bk[:, :], lhsT=k.ones_b[:, :], rhs=sqt[:], start=True, stop=True,
                    r=[N("sqt")], w=[("pb", bi)])
                P.I("act", "activation", out=rin[i][:], in_=bk[:, :], func=AF.Sqrt, bias=k.cst[:, 4:5],
                    r=[("pb", bi)], w=[N(("rin", i))])
                P.I("dve", "reciprocal", out=rin[i][:], in_=rin[i][:], r=[N(("rin", i))], w=[N(("rin", i))])
            P.I("dve", "scalar_tensor_tensor", out=qn[:], in0=y[0][:], scalar=DKS, in1=rin[0][:], op0=ALU.mult,
                op1=ALU.mult, r=[N(("y", 0)), N(("rin", 0))], w=[N("qn")])
            P.I("dve", "tensor_tensor", out=kn[:], in0=y[1][:], in1=rin[1][:], op=ALU.mult,
                r=[N(("y", 1)), N(("rin", 1))], w=[N("kn")])
            yield
            P.I("dve", "tensor_copy", out=gh[:], in_=gall[:, :, h], r=["gall"], w=[N("gh")])
            P.I("dve", "tensor_copy", out=bh[:], in_=beta[:, :, h], r=["beta"], w=[N("bh")])
            P.I("dve", "tensor_scalar", out=nbh[:], in0=beta[:, :, h], scalar1=-1.0, scalar2=None, op0=ALU.mult,
                r=["beta"], w=[N("nbh")])
            P.I("pe", "matmul", B6[0:64, 0:8], lhsT=U, rhs=gh[:], start=True, stop=True, r=[N("gh")], w=[K6])
            P.I("pe", "matmul", B6[:, 8:16], lhsT=k.ones_f[0:64, :], rhs=gh[:], start=True, stop=True,
                r=[N("gh")], w=[K6])
            P.I("dve", "tensor_copy", out=gcs[:], in_=B6[0:64, 0:8], r=[K6], w=[N("gcs")])
            P.I("dve", "tensor_tensor", out=dgl[:], in0=B6[0:64, 8:16], in1=gcs[:], op=ALU.subtract,
                r=[K6, N("gcs")], w=[N("dgl")])
            P.I("act", "activation", out=egl[:], in_=dgl[:], func=AF.Exp, r=[N("dgl")], w=[N("egl")])
            P.I("act", "activation", out=eg[:], in_=gcs[:], func=AF.Exp, r=[N("gcs")], w=[N("eg")])
            P.I("act", "activation", out=gtot[:], in_=B6[:, 8:16], func=AF.Exp, r=[K6], w=[N("gtot")])
            P.I("dve", "tensor_tensor", out=beg[:], in0=bh[:], in1=eg[:], op=ALU.mult, r=[N("bh"), N("eg")],
                w=[N("beg")])
            P.I("dve", "tensor_tensor", out=Gm[:], in0=bcn(U), in1=bci(gh[:]), op=ALU.mult, r=[N("gh")], w=[N("Gm")])
            bB, bkB = nb4()
            P.I("pe", "matmul", bkB[:, :], lhsT=k.ones_f[0:64, :], rhs=Gm[:].rearrange("p c n -> p (c n)"),
                start=True, stop=True, r=[N("Gm")], w=[("pb", bB)])
            P.I("act", "activation", out=egbc[:], in_=bkB[:, :], func=AF.Exp, r=[("pb", bB)], w=[N("ctmp")])
            P.I("dve", "tensor_tensor", out=qd[:], in0=qn[:], in1=egbc[:], op=ALU.mult, r=[N("qn"), N("ctmp")],
                w=[N("qd")])
            P.I("dve", "tensor_tensor", out=d3[:], in0=v3(bkB[0:64, :], 64), in1=bci(gcs[:]), op=ALU.subtract,
                r=[("pb", bB), N("gcs")], w=[N("d3")])
            yield
            P.I("dve", "tensor_tensor", out=DTu[:], in0=d3[:], in1=bcn(MBu), op=ALU.add, r=[N("d3")], w=[N("DTu")])
            P.I("act", "activation", out=DTu[:], in_=DTu[:], func=AF.Exp, r=[N("DTu")], w=[N("DTu")])
            P.I("dve", "scalar_tensor_tensor", out=DTl[:], in0=d3[:], scalar=-1.0, in1=bcn(MBl), op0=ALU.mult,
                op1=ALU.add, r=[N("d3")], w=[N("DTl")])
            P.I("act", "activation", out=DTl[:], in_=DTl[:], func=AF.Exp, r=[N("DTl")], w=[N("DTl")])
            P.I("dve", "tensor_tensor", out=DTl[:], in0=DTl[:], in1=bci(nbh[:]), op=ALU.mult,
                r=[N("DTl"), N("nbh")], w=[N("DTl")])
            P.I("dve", "tensor_tensor", out=Bm[:], in0=bcn(I64), in1=bci(nbh[:]), op=ALU.mult, r=[N("nbh")],
                w=[N("Bm")])
            bN, bkN = nb4()
            P.I("pe", "matmul", bkN[0:64, :], lhsT=k.su_b[:, :], rhs=Bm[:].rearrange("p c n -> p (c n)"), start=True,
                stop=True, r=[N("Bm")], w=[("pb", bN)])
            P.I("dve", "tensor_tensor", out=Mfac[:], in0=DTu[:], in1=v3(bkN[0:64, :], 64), op=ALU.mult,
                r=[N("DTu"), ("pb", bN)], w=[N("Mfac")])
            yield
            for half in range(2):
                bi, bk = nb4()
                bkb = bk[:].bitcast(BF16)
                for m in range(4):
                    c = half * 4 + m
                    P.I("pe", "transpose", out=bkb[0:64, m * 128:(m + 1) * 128], in_=kn[:, c * 64:(c + 1) * 64],
                        identity=k.ident_bf[:, :], r=[N("kn")], w=[("pb", bi)])
                hs = slice(half * 4, half * 4 + 4)
                P.I("dve", "tensor_tensor", out=kbg[:, hs, :], in0=v3(bkb[0:64, 0:512], 128),
                    in1=beg[:, hs].unsqueeze(2).to_broadcast([64, 4, 128]), op=ALU.mult,
                    r=[("pb", bi), N("beg")], w=[N(("kbg", half))])
                P.I("dve", "tensor_tensor", out=kdec[:, hs, :], in0=v3(bkb[0:64, 0:512], 128),
                    in1=egl[:, hs].unsqueeze(2).to_broadcast([64, 4, 128]), op=ALU.mult,
                    r=[("pb", bi), N("egl")], w=[N(("kdec", half))])
                bi, bk = nb4()
                bkb = bk[:].bitcast(BF16)
                for m in range(4):
                    c = half * 4 + m
                    P.I("pe", "transpose", out=bkb[0:64, m * 128:(m + 1) * 128], in_=yvb[:, c * 64:(c + 1) * 64],
                        identity=k.ident_bf[:, :], r=[N("yvb")], w=[("pb", bi)])
                P.I("dve", "tensor_tensor", out=vb[:, hs, :], in0=v3(bkb[0:64, 0:512], 128),
                    in1=bh[:, hs].unsqueeze(2).to_broadcast([64, 4, 128]), op=ALU.mult,
                    r=[("pb", bi), N("bh")], w=[N(("vb", half))])
                yield
            bKK, bkKK = nb4()
            for m in range(8):
                cs = slice(m * 64, (m + 1) * 64)
                P.I("pe", "matmul", bkKK[0:64, cs], lhsT=kn[:, cs], rhs=kn[:, cs], start=True, stop=True,
                    r=[N("kn")], w=[("pb", bKK)])
            bKQ, bkKQ = nb4()
            for m in range(8):
                cs = slice(m * 64, (m + 1) * 64)
                P.I("pe", "matmul", bkKQ[0:64, cs], lhsT=kn[:, cs], rhs=qn[:, cs], start=True, stop=True,
                    r=[N("kn"), N("qn")], w=[("pb", bKQ)])
            P.I("dve", "tensor_tensor", out=Nm[0][:], in0=v3(bkKK[0:64, :], 64), in1=Mfac[:], op=ALU.mult,
                r=[("pb", bKK), N("Mfac")], w=[N(("Nm", 0))])
            P.I("dve", "tensor_tensor", out=NTm[0][:], in0=v3(bkKK[0:64, :], 64), in1=DTl[:], op=ALU.mult,
                r=[("pb", bKK), N("DTl")], w=[N(("NTm", 0))])
            P.I("dve", "tensor_tensor", out=qkT[:], in0=v3(bkKQ[0:64, :], 64), in1=DTu[:], op=ALU.mult,
                r=[("pb", bKQ), N("DTu")], w=[N("qkT")])
            P.I("dve", "tensor_tensor", out=Rm[0][:], in0=Nm[0][:], in1=bcn(I64), op=ALU.add,
                r=[N(("Nm", 0))], w=[N(("Rm", 0))])
            yield
            cn, cr = 0, 0
            for it in range(5):
                nn = 1 - cn
                if it < 4:
                    bA, bkA = nb4()
                    for m in range(8):
                        cs = slice(m * 64, (m + 1) * 64)
                        P.I("pe", "matmul", bkA[0:64, cs], lhsT=NTm[cn][:, m, :], rhs=Nm[cn][:, m, :], start=True,
                            stop=True, r=[N(("NTm", cn)), N(("Nm", cn))], w=[("pb", bA)])
                bBt, bkBt = nb4()
                for m in range(8):
                    cs = slice(m * 64, (m + 1) * 64)
                    P.I("pe", "matmul", bkBt[0:64, cs], lhsT=Nm[cn][:, m, :], rhs=NTm[cn][:, m, :], start=True,
                        stop=True, r=[N(("NTm", cn)), N(("Nm", cn))], w=[("pb", bBt)])
                P.I("act", "copy", out=NTm[nn][:], in_=v3(bkBt[0:64, :], 64), r=[("pb", bBt)], w=[N(("NTm", nn))])
                bC, bkC = nb4()
                for m in range(8):
                    cs = slice(m * 64, (m + 1) * 64)
                    P.I("pe", "matmul", bkC[0:64, cs], lhsT=NTm[nn][:, m, :], rhs=Rm[cr][:, m, :], start=True,
                        stop=True, r=[N(("NTm", nn)), N(("Rm", cr))], w=[("pb", bC)])
                P.I("dve", "tensor_tensor", out=Rm[1 - cr][:], in0=Rm[cr][:], in1=v3(bkC[0:64, :], 64), op=ALU.add,
                    r=[N(("Rm", cr)), ("pb", bC)], w=[N(("Rm", 1 - cr))])
                cr = 1 - cr
                if it < 4:
                    P.I("act", "copy", out=Nm[nn][:], in_=v3(bkA[0:64, :], 64), r=[("pb", bA)], w=[N(("Nm", nn))])
                cn = nn
                yield
            R = Rm[cr]
            rk = N(("Rm", cr))
            for half in range(2):
                bi, bk = nb4()
                for m in range(4):
                    c = half * 4 + m
                    P.I("pe", "matmul", bk[0:64, m * 128:(m + 1) * 128], lhsT=R[:, c, :], rhs=vb[:, c, :], start=True,
                        stop=True, r=[rk, N(("vb", half))], w=[("pb", bi)])
                P.I("act", "copy", out=us[:, half * 4:half * 4 + 4, :], in_=v3(bk[0:64, :], 128),
                    r=[("pb", bi)], w=[N(("us", half))])
            bW, bkW = nb4()
            for m in range(8):
                P.I("pe", "matmul", bkW[:, m * 64:(m + 1) * 64], lhsT=kbg[:, m, :], rhs=R[:, m, :], start=True,
                    stop=True, r=[rk, N(("kbg", m // 4))], w=[("pb", bW)])
            P.I("dve", "tensor_copy", out=wTs[:], in_=bkW[:, :], r=[("pb", bW)], w=[N("wTs")])
            yield
            for m in range(8):
                cur = (s * 8 + m) % 2
                Sc, Sn = S[h][cur], S[h][1 - cur]
                Sbc, Sbn = Sb[h][cur], Sb[h][1 - cur]
                cs = slice(m * 64, (m + 1) * 64)
                P.I("pe", "matmul", B6[0:64, 128:256], lhsT=wTs[:, cs], rhs=Sbc[:], start=True, stop=True,
                    r=[N("wTs"), ("Sb", h, cur)], w=[K6])
                P.I("dve", "tensor_tensor", out=vnew[:], in0=us[:, m, :], in1=B6[0:64, 128:256], op=ALU.subtract,
                    r=[N(("us", m // 4)), K6], w=[N("vnew")])
                oc = slice((m % 4) * 128, (m % 4 + 1) * 128)
                P.I("pe", "matmul", B7[0:64, oc], lhsT=qd[:, cs], rhs=Sbc[:], start=True, stop=False,
                    r=[N("qd"), ("Sb", h, cur)], w=[K7])
                P.I("pe", "matmul", B7[0:64, oc], lhsT=qkT[:, m, :], rhs=vnew[:], start=False, stop=True,
                    r=[N("qkT"), N("vnew")], w=[K7])
                P.I("pe", "matmul", B6[:, 256:384], lhsT=kdec[:, m, :], rhs=vnew[:], start=True, stop=True,
                    r=[N(("kdec", m // 4)), N("vnew")], w=[K6])
                P.I("dve", "scalar_tensor_tensor", out=Sbn[:], in0=Sc[:], scalar=gtot[:, m:m + 1],
                    in1=B6[:, 256:384], op0=ALU.mult, op1=ALU.add, r=[("S", h, cur), N("gtot"), K6],
                    w=[("Sb", h, 1 - cur)])
                P.I("dve", "scalar_tensor_tensor", out=Sn[:], in0=Sc[:], scalar=gtot[:, m:m + 1], in1=B6[:, 256:384],
                    op0=ALU.mult, op1=ALU.add, r=[("S", h, cur), N("gtot"), K6], w=[("S", h, 1 - cur)])
                if m % 4 == 3:
                    P.I("act", "copy", out=os_[:, m - 3:m + 1, :], in_=v3(B7[0:64, :], 128), r=[K7],
                        w=[N(("os", m // 4))])
                yield
            okeys = [N(("os", 0)), N(("os", 1))]
            kb2 = [N(("kbg", 0)), N(("kbg", 1))]
            vb2 = [N(("vb", 0)), N(("vb", 1))]
            kd2 = [N(("kdec", 0)), N(("kdec", 1))]
            kb2, vb2, kd2 = [N("sq3")], [N("zbt")], [N("y1")]
            P.I("dve", "tensor_tensor", out=sq3[:], in0=os_[:], in1=os_[:], op=ALU.mult, r=okeys, w=kb2)
            P.I("dve", "tensor_reduce", out=ssq[:], in_=sq3[:], axis=AX.X, op=ALU.add, r=kb2, w=[N("ssq")])
            P.I("act", "activation", out=rinv8[:], in_=ssq[:], func=AF.Sqrt, bias=eps64, scale=1.0 / 128,
                r=[N("ssq")], w=[N("rinv8")])
            P.I("dve", "reciprocal", out=rinv8[:], in_=rinv8[:], r=[N("rinv8")], w=[N("rinv8")])
            P.dma("sp", (sid, "g_zbt"), zbt[:],
                  k.zb[t0:t0 + 512, h * 128:(h + 1) * 128].rearrange("(n p) c -> p n c", p=64), w=vb2)
            P.I("dve", "tensor_tensor", out=y1[:], in0=os_[:], in1=bci(rinv8[:], 128), op=ALU.mult,
                r=okeys + [N("rinv8")], w=kd2)
            P.I("pool", "tensor_tensor", out=y1[:], in0=y1[:], in1=gg[:].unsqueeze(1).to_broadcast([64, 8, 128]),
                op=ALU.mult, r=kd2 + ["gg"], w=kd2)
            P.I("pool", "tensor_tensor", out=y1[:], in0=y1[:], in1=zbt[:], op=ALU.mult, r=kd2 + vb2, w=kd2)
            P.dma("pool", (sid, "g_yb"), k.yb[t0:t0 + 512, h * 128:(h + 1) * 128].rearrange("(n p) c -> p n c", p=64),
                  y1[:], r=kd2)
            yield
        return unit

    units = [make_set(0, 6, 7), make_set(1, 4, 5)]
    for s in range(NS):
        t0 = s * 512
        P.dma("sp", "g_bat", bat[:], k.ba[t0:t0 + 512, :].rearrange("(n p) c -> p n c", p=64), w=["bat"])
        P.I("act", "activation", out=beta[:], in_=bat[:, :, 0:8], func=AF.Sigmoid, r=["bat"], w=["beta"])
        P.I("dve", "tensor_tensor", out=xa[:], in0=bat[:, :, 8:16], in1=dtb[:].unsqueeze(1).to_broadcast([64, 8, 8]),
            op=ALU.add, r=["bat", "dtb"], w=["xa"])
        P.I("act", "activation", out=xa[:], in_=xa[:], func=AF.Exp, r=["xa"], w=["xa"])
        P.I("act", "activation", out=xa[:], in_=xa[:], func=AF.Ln, bias=one64, r=["xa"], w=["xa"])
        P.I("dve", "tensor_tensor", out=gall[:], in0=xa[:], in1=negA[:].unsqueeze(1).to_broadcast([64, 8, 8]),
            op=ALU.mult, r=["xa", "negA"], w=["gall"])
        for h0 in range(0, 8, 2):
            gens = [units[0](s, h0), units[1](s, h0 + 1)]
            while gens:
                for g_ in list(gens):
                    try:
                        next(g_)
                    except StopIteration:
                        gens.remove(g_)
    P.end_phase()


def phase_attn(k, l):
    P, SEQ, NT = k.P, k.SEQ, k.NT
    H2 = SEQ // 2
    P.begin_phase()

    def T(name, shape, dt=F32):
        return P.sb(name, shape, dt)

    kT = T("kT", [128, 2, SEQ], BF16)
    V = T("V", [128, NT, 256], BF16)
    kiT = T("kiT", [128, H2], BF16)
    Isc = [T("Isc", [128, SEQ]) for _ in range(2)]
    mb = [T("mb", [128, SEQ], BF16) for _ in range(2)]
    qt = [T("qt", [128, 8, 128], BF16) for _ in range(2)]
    qit = [T("qit", [128, 8, 128], BF16) for _ in range(2)]
    wit = [T("wit", [128, 8]) for _ in range(2)]
    rl = [T("rl", [128, 512]) for _ in range(2)]
    pt = [T("pt", [128, 512], BF16) for _ in range(3)]
    osb = T("osb", [128, 8, 128])
    st8 = [T("st8", [128, 8]) for _ in range(2)]
    rden = T("rden", [128, 8])
    for g in range(2):
        P.dma("sp", ("a_kT", g), kT[:, g, :], k.kT[g, :, :], w=["kT"])
    vv = k.v.rearrange("(t p) c -> p t c", p=128)
    for t0 in range(0, NT, 8):
        P.dma("sp", ("a_V", t0), V[:, t0:t0 + 8, :], vv[:, t0:t0 + 8, :], w=[("V", t0)])
    P.dma("sp", "a_kiT0", kiT[0:64, :], k.kiT[:, 0:H2], w=["kiT"])
    P.dma("sp", "a_kiT1", kiT[64:128, :], k.kiT[:, H2:SEQ], w=["kiT"])
    BI = [k.banks[0], k.banks[1]]
    BS = [k.banks[2], k.banks[3]]
    BO = [k.banks[4], k.banks[5]]
    BD = k.banks[6]
    ci = {"i": 0, "s": 0, "r": 0, "p": 0}
    topk = float(min(TOPK, SEQ // 4))

    def PRE(j):
        s = j % 2
        L = (j + 1) * 128
        Tq = slice(j * 128, (j + 1) * 128)
        I_, M_, S8 = Isc[s], mb[s], st8[s]
        IK, MK = ("I", s), ("mb", s)
        P.dma("sp", ("a_qt", s), qt[s][:], k.qT[:, :, Tq].rearrange("h d t -> d h t"), w=[("qt", s)])
        P.dma("sp", ("a_qit0", s), qit[s][0:64], k.qiT[:, :, Tq].rearrange("h d t -> d h t"), w=[("qit", s)])
        P.dma("sp", ("a_qit1", s), qit[s][64:128], k.qiT[:, :, Tq].rearrange("h d t -> d h t"), w=[("qit", s)])
        P.dma("sp", ("a_wit", s), wit[s][:], k.wi[Tq, :], w=[("wit", s)])
        nblk = (L + 511) // 512
        for kb in range(nblk):
            w_ = min(512, L - kb * 512)
            cs = slice(kb * 512, kb * 512 + w_)
            if kb * 512 < H2:
                pr, kc = slice(0, 64), slice(kb * 512, kb * 512 + w_)
            else:
                pr, kc = slice(64, 128), slice(kb * 512 - H2, kb * 512 - H2 + w_)
            for h in range(8):
                b = ci["i"] % 2
                ci["i"] += 1
                P.I("pe", "matmul", BI[b][:, 0:w_], lhsT=qit[s][pr, h, :], rhs=kiT[pr, kc], start=True, stop=True,
                    r=[("qit", s), "kiT"], w=[("bi", b)])
                r_ = ci["r"] % 2
                ci["r"] += 1
                P.I("act", "activation", out=rl[r_][:, 0:w_], in_=BI[b][:, 0:w_], func=AF.Relu,
                    r=[("bi", b)], w=[("rl", r_)])
                if h == 0:
                    P.I("dve", "tensor_scalar", out=I_[:, cs], in0=rl[r_][:, 0:w_], scalar1=wit[s][:, 0:1],
                        scalar2=None, op0=ALU.mult, r=[("rl", r_), ("wit", s)], w=[IK])
                else:
                    P.I("dve", "scalar_tensor_tensor", out=I_[:, cs], in0=rl[r_][:, 0:w_], scalar=wit[s][:, h:h + 1],
                        in1=I_[:, cs], op0=ALU.mult, op1=ALU.add, r=[("rl", r_), ("wit", s), IK], w=[IK])
        sk = lambda n: ("st", s, n)
        P.I("dve", "tensor_reduce", out=S8[:, 0:1], in_=I_[:, 0:L], axis=AX.X, op=ALU.max, r=[IK], w=[sk(0)])
        P.I("dve", "tensor_reduce", out=S8[:, 1:2], in_=I_[:, 0:L], axis=AX.X, op=ALU.min, r=[IK], w=[sk(1)])
        P.I("dve", "tensor_tensor", out=S8[:, 2:3], in0=S8[:, 0:1], in1=S8[:, 1:2], op=ALU.subtract,
            r=[sk(0), sk(1)], w=[sk(2)])
        P.I("dve", "tensor_scalar", out=S8[:, 2:3], in0=S8[:, 2:3], scalar1=1e-20, scalar2=None, op0=ALU.add,
            r=[sk(2)], w=[sk(2)])
        P.I("dve", "reciprocal", out=S8[:, 2:3], in_=S8[:, 2:3], r=[sk(2)], w=[sk(2)])
        P.I("dve", "tensor_scalar", out=I_[:, 0:L], in0=I_[:, 0:L], scalar1=S8[:, 1:2], scalar2=S8[:, 2:3],
            op0=ALU.subtract, op1=ALU.mult, r=[IK, sk(1), sk(2)], w=[IK])
        P.I("dve", "tensor_tensor", out=I_[:, L - 128:L], in0=I_[:, L - 128:L], in1=k.tri[:, :], op=ALU.add,
            r=[IK], w=[IK])
        P.I("dve", "memset", S8[:, 3:4], 0.5, w=[sk(3)])
        for it in range(NBIS):
            wk = 2.0 ** -(it + 2)
            P.I("dve", "tensor_scalar", out=M_[:, 0:L], in0=I_[:, 0:L], scalar1=S8[:, 3:4], scalar2=0.0,
                op0=ALU.is_ge, op1=ALU.add, accum_out=S8[:, 4:5], r=[IK, sk(3)], w=[MK, sk(4)])
            P.I("dve", "tensor_scalar", out=S8[:, 5:6], in0=S8[:, 4:5], scalar1=topk, scalar2=2.0 * wk,
                op0=ALU.is_ge, op1=ALU.mult, r=[sk(4)], w=[sk(5)])
            P.I("dve", "scalar_tensor_tensor", out=S8[:, 3:4], in0=S8[:, 5:6], scalar=-wk, in1=S8[:, 3:4],
                op0=ALU.add, op1=ALU.add, r=[sk(5), sk(3)], w=[sk(3)])
        P.I("dve", "tensor_scalar", out=S8[:, 6:7], in0=S8[:, 3:4], scalar1=-(2.0 ** -(NBIS + 1)), scalar2=None,
            op0=ALU.add, r=[sk(3)], w=[sk(6)])
        P.I("dve", "tensor_scalar", out=M_[:, 0:L], in0=I_[:, 0:L], scalar1=S8[:, 6:7], scalar2=-30000.0,
            op0=ALU.is_lt, op1=ALU.mult, r=[IK, sk(6)], w=[MK])

    def ATT(j):
        s = j % 2
        nk = j + 1
        Tq = slice(j * 128, (j + 1) * 128)
        M_, MK = mb[s], ("mb", s)
        for g in range(2):
            P.I("pe", "matmul", BO[g][:, :], lhsT=k.zeros_bf[:, 0:128], rhs=k.zeros_bf[:, :], start=True, stop=False,
                w=[("bo", g)])
        P.I("pe", "matmul", BD[:, 0:8], lhsT=k.zeros_bf[:, 0:128], rhs=k.zeros_bf[:, 0:8], start=True, stop=False,
            w=["bd"])
        tiles = [(kt, g) for kt in range(nk) for g in range(2)]
        bsel = {}

        def emit_S(i):
            kt, g = tiles[i]
            ks = slice(kt * 128, (kt + 1) * 128)
            b = ci["s"] % 2
            ci["s"] += 1
            bsel[i] = b
            P.I("pe", "matmul", BS[b][:, :], lhsT=kT[:, g, ks],
                rhs=qt[s][:, 4 * g:4 * g + 4, :].rearrange("p h t -> p (h t)"), start=True, stop=False,
                r=["kT", ("qt", s)], w=[("bs", b)])
            P.I("pe", "matmul", BS[b][:, :], lhsT=M_[:, ks], rhs=k.i4[:, :], start=False, stop=True,
                r=[MK], w=[("bs", b)])

        emit_S(0)
        for i, (kt, g) in enumerate(tiles):
            if i + 1 < len(tiles):
                emit_S(i + 1)
            b = bsel[i]
            last = (kt == nk - 1)
            p_ = ci["p"] % 3
            ci["p"] += 1
            P.I("act", "activation", out=pt[p_][:, :], in_=BS[b][:, :], func=AF.Exp, r=[("bs", b)], w=[("pt", p_)])
            for hh in range(4):
                P.I("pe", "matmul", BO[g][:, hh * 128:(hh + 1) * 128], lhsT=pt[p_][:, hh * 128:(hh + 1) * 128],
                    rhs=V[:, kt, g * 128:(g + 1) * 128], start=False, stop=last,
                    r=[("pt", p_), ("V", (kt // 8) * 8)], w=[("bo", g)])
                P.I("pe", "matmul", BD[:, g * 4 + hh:g * 4 + hh + 1], lhsT=pt[p_][:, hh * 128:(hh + 1) * 128],
                    rhs=k.ones_bf[:, 0:1], start=False, stop=last, r=[("pt", p_)], w=["bd"])
        P.I("dve", "reciprocal", out=rden[:, :], in_=BD[:, 0:8], r=["bd"], w=["rden"])
        for g in range(2):
            P.I("dve", "tensor_tensor", out=osb[:, 4 * g:4 * g + 4, :],
                in0=BO[g][:, :].rearrange("p (h d) -> p h d", d=128),
                in1=rden[:, 4 * g:4 * g + 4].unsqueeze(2).to_broadcast([128, 4, 128]), op=ALU.mult,
                r=[("bo", g), "rden"], w=["osb"])
        P.dma("pool", "a_oa", k.oa[Tq, :], osb[:].rearrange("p h d -> p (h d)"), r=["osb"])

    PRE(0)
    for j in range(NT):
        if j + 1 < NT:
            PRE(j + 1)
        ATT(j)
    P.end_phase()


def phase_C(k, l, src, last):
    P, NT = k.P, k.NT
    P.begin_phase()

    def T(name, shape, dt=F32):
        return P.sb(name, shape, dt)

    wo = T("wo", [128, 8, D], BF16)
    wov = k.wout[l].rearrange("(kc p) c -> p kc c", p=128)
    wof = [T("wof", [128, D]) for _ in range(2)]
    for kc in range(8):
        f = kc % 2
        P.dma("sp", ("c_wof", f), wof[f][:], wov[:, kc, :], w=[("wof", f)])
        P.I("pool", "tensor_copy", out=wo[:, kc, :], in_=wof[f][:], r=[("wof", f)], w=[("wo", kc)])
    fg = T("fg", [128, D])
    if last:
        P.dma("sp", "c_fg", fg[:], k.fgain.partition_broadcast(128), w=["fg"])
    oa = [T("oa", [128, D]) for _ in range(2)]
    za = [T("za", [128, D]) for _ in range(2)]
    gt = [T("gt", [128, 2 * D]) for _ in range(2)]
    yb = [T("yb", [128, D]) for _ in range(2)]
    xt = [T("xt", [128, D]) for _ in range(2)]
    mx = [T("mx", [128, D], BF16) for _ in range(2)]
    mT = [T("mT", [128, 8, 128], BF16) for _ in range(2)]
    xn = [T("xn", [128, D]) for _ in range(2)]
    junk = T("junk", [128, D], BF16)
    ss = T("ss", [128, 2])
    wk = [("wo", kc) for kc in range(8)]
    for i in range(NT):
        s = i % 2
        R = slice(i * 128, (i + 1) * 128)
        P.dma("sp", ("c_oa", s), oa[s][:], k.oa[R, :], w=[("oa", s)])
        P.dma("sp", ("c_za", s), za[s][:], k.za[R, :], w=[("za", s)])
        P.dma("sp", ("c_gt", s), gt[s][:], k.gt[R, :], w=[("gt", s)])
        P.dma("sp", ("c_yb", s), yb[s][:], k.yb[R, :], w=[("yb", s)])
        P.dma("sp", ("c_xt", s), xt[s][:], src[R, :], w=[("xt", s)])
        P.I("pool", "tensor_tensor", out=oa[s][:], in0=oa[s][:], in1=za[s][:], op=ALU.mult,
            r=[("oa", s), ("za", s)], w=[("oa", s)])
        P.I("pool", "tensor_tensor", out=oa[s][:], in0=oa[s][:], in1=gt[s][:, 0:D], op=ALU.mult,
            r=[("oa", s), ("gt", s)], w=[("oa", s)])
        P.I("dve", "tensor_tensor", out=yb[s][:], in0=yb[s][:], in1=gt[s][:, D:2 * D], op=ALU.mult,
            r=[("yb", s), ("gt", s)], w=[("yb", s)])
        P.I("dve", "tensor_tensor", out=mx[s][:], in0=oa[s][:], in1=yb[s][:], op=ALU.add,
            r=[("oa", s), ("yb", s)], w=[("mx", s)])
        bi, bank = next_bank(k)
        pT = bank[:].bitcast(BF16).rearrange("p (c t) -> p c t", t=128)
        for kc in range(8):
            P.I("pe", "transpose", out=pT[:, kc, :], in_=mx[s][:, kc * 128:(kc + 1) * 128], identity=k.ident_bf[:],
                r=[("mx", s)], w=[("pb", bi)])
        P.I("act", "copy", out=mT[s][:], in_=pT, r=[("pb", bi)], w=[("mT", s)])
        for half in range(2):
            bo, bko = next_bank(k)
            for kc in range(8):
                P.I("pe", "matmul", bko[:, :], lhsT=mT[s][:, kc, :], rhs=wo[:, kc, half * 512:(half + 1) * 512],
                    start=(kc == 0), stop=(kc == 7), r=[("mT", s), wk[kc]], w=[("pb", bo)])
            P.I("dve", "tensor_tensor", out=xn[s][:, half * 512:(half + 1) * 512], in0=xt[s][:, half * 512:(half + 1) * 512],
                in1=bko[:, :], op=ALU.add, r=[("xt", s), ("pb", bo)], w=[("xn", s, half)])
        xk = [("xn", s, 0), ("xn", s, 1)]
        if not last:
            P.dma("pool", ("c_st", s), k.x1[R, :], xn[s][:], r=xk)
        else:
            P.I("act", "activation", out=junk[:], in_=xn[s][:], func=AF.Square, accum_out=ss[:, s:s + 1],
                r=xk, w=["junk", ("ss", s)])
            P.I("act", "activation", out=ss[:, s:s + 1], in_=ss[:, s:s + 1], func=AF.Sqrt, bias=k.cst[:, 4:5],
                scale=1.0 / D, r=[("ss", s)], w=[("ss", s)])
            P.I("dve", "reciprocal", out=ss[:, s:s + 1], in_=ss[:, s:s + 1], r=[("ss", s)], w=[("ss", s)])
            P.I("dve", "scalar_tensor_tensor", out=xn[s][:], in0=xn[s][:], scalar=ss[:, s:s + 1], in1=fg[:],
                op0=ALU.mult, op1=ALU.mult, r=xk + [("ss", s), "fg"], w=xk)
            P.dma("pool", ("c_st", s), k.out[R, :], xn[s][:], r=xk)
    P.end_phase()


_CACHE = {}
SEQ_FULL = 8192
NLAYERS = 2
NCORES = 2


def _in_maps(inp, NL):
    consts = host_consts()
    per_layer = {}
    for l in range(NL):
        per_layer["wfm%d" % l] = host_w_fm(inp["w_in"], l)
        per_layer["wtm%d" % l] = host_w_tm(inp["w_in"], l)
        per_layer["wout%d" % l] = np.ascontiguousarray(inp["w_out"][l], dtype=np.float32)
        per_layer["ng%d" % l] = np.ascontiguousarray(inp["norm_gain"][l][None, :], dtype=np.float32)
        cw = np.asarray(inp["conv_w"][l], dtype=np.float32)
        per_layer["convw%d" % l] = np.ascontiguousarray(cw.reshape(4, 24, 128).transpose(2, 1, 0))
        per_layer["gdng%d" % l] = np.ascontiguousarray(inp["gdn_norm_gain"][l][None, :], dtype=np.float32)
        per_layer["alog%d" % l] = np.ascontiguousarray(inp["a_log"][l][None, :], dtype=np.float32)
        per_layer["dtb%d" % l] = np.ascontiguousarray(inp["dt_bias"][l][None, :], dtype=np.float32)
        g = np.asarray(inp["idx_k_gain"][l], dtype=np.float32)
        gs = np.zeros(64, np.float32)
        gs[:8] = g[8:16]
        gs[8:16] = g[0:8]
        per_layer["ikg%d" % l] = np.ascontiguousarray(np.stack([g, gs], 1))
    maps = []
    for c in range(NCORES):
        b = c % 2
        m = {"x": np.ascontiguousarray(inp["x"][b], dtype=np.float32),
             "pos": np.ascontiguousarray(inp["positions"][b:b + 1]).astype(np.int32),
             "fgain": np.ascontiguousarray(np.asarray(inp["final_gain"], dtype=np.float32)[None, :])}
        m.update(per_layer)
        m.update(consts)
        maps.append(m)
    return maps


def kernel(x, positions, norm_gain, w_in, conv_w, a_log, dt_bias, gdn_norm_gain, idx_k_gain, w_out, final_gain):
    inp = {"x": np.asarray(x), "positions": np.asarray(positions), "norm_gain": np.asarray(norm_gain),
           "w_in": np.asarray(w_in), "conv_w": np.asarray(conv_w), "a_log": np.asarray(a_log),
           "dt_bias": np.asarray(dt_bias), "gdn_norm_gain": np.asarray(gdn_norm_gain),
           "idx_k_gain": np.asarray(idx_k_gain), "w_out": np.asarray(w_out), "final_gain": np.asarray(final_gain)}
    B, S, _ = inp["x"].shape
    NL = inp["w_in"].shape[0]
    assert B == 2
    nc = build(S, NL)
    maps = _in_maps(inp, NL)
    res = run_bass_kernel_spmd(nc, maps, core_ids=list(range(NCORES)))
    out = np.stack([np.asarray(res.results[b]["out"], dtype=np.float32) for b in range(2)], axis=0)
    return out
```

```python
import numpy as np
from contextlib import ExitStack
import concourse.bass as bass
import concourse.mybir as mybir

F32 = mybir.dt.float32
BF16 = mybir.dt.bfloat16
I32 = mybir.dt.int32
ALU = mybir.AluOpType
AF = mybir.ActivationFunctionType
AX = mybir.AxisListType


class _Op:
    __slots__ = ("idx", "eng", "fn", "deps", "dma", "dseq", "sig", "signals")

    def __init__(self, idx, eng, fn, deps, dma):
        self.idx = idx
        self.eng = eng
        self.fn = fn
        self.deps = deps
        self.dma = dma
        self.dseq = 0
        self.sig = 0
        self.signals = False


_PSUM_T = ("pb", "bi", "bs", "bo")
_PSUM_S = ("b6", "b7", "bd")


def _is_psum_key(x):
    return (isinstance(x, tuple) and x[0] in _PSUM_T) or (isinstance(x, str) and x in _PSUM_S)


class Prog:
    ENGS = ("pe", "act", "dve", "pool", "sp")

    def __init__(self, nc):
        self.nc = nc
        self.ops = []
        self.flushed = 0
        self.lastw = {}
        self.readers = {}
        self.dma_prev = {}
        self.dma_cnt = {}
        self.last_eng = {}
        self.es = ExitStack()
        self.pes = None
        self._n = 0
        self.esem = {e: self.es.enter_context(nc.semaphore("s_" + e)) for e in self.ENGS}
        self.dsem = {}
        self.cnt = {e: 0 for e in self.ENGS}
        self.waited = {e: {} for e in self.ENGS}

    def _nm(self, name):
        self._n += 1
        return "%s_%d" % (name, self._n)

    def sb(self, name, shape, dt, persist=False):
        st = self.es if (persist or self.pes is None) else self.pes
        return st.enter_context(self.nc.sbuf_tensor(self._nm(name), list(shape), dt))

    def ps(self, name, shape, dt):
        return self.es.enter_context(self.nc.psum_tensor(self._nm(name), list(shape), dt))

    def begin_phase(self):
        self.pes = ExitStack()

    def end_phase(self):
        self.barrier()
        self.flush()
        self.pes.close()
        self.pes = None

    def op(self, eng, fn, r=(), w=(), dma=None, deps=None):
        idx = len(self.ops)
        deps = set(deps) if deps is not None else set()
        pr = [x for x in r if _is_psum_key(x)]
        if pr:
            r = [x for x in r if not _is_psum_key(x)]
            w = list(w) + pr
        for k in r:
            if k in self.lastw:
                deps.add(self.lastw[k])
        for k in w:
            if k in self.lastw:
                deps.add(self.lastw[k])
            for q in self.readers.get(k, ()):
                deps.add(q)
        if dma is not None:
            if dma in self.dma_prev:
                deps.add(self.dma_prev[dma])
            self.dma_prev[dma] = idx
        o = _Op(idx, eng, fn, deps, dma)
        if dma is not None:
            self.dma_cnt[dma] = self.dma_cnt.get(dma, 0) + 1
            o.dseq = self.dma_cnt[dma]
        else:
            self.last_eng[eng] = idx
        self.ops.append(o)
        for k in w:
            self.lastw[k] = idx
            self.readers[k] = []
        for k in r:
            lst = self.readers.setdefault(k, [])
            if dma is None:
                lst[:] = [q for q in lst if not (self.ops[q].dma is None and self.ops[q].eng == eng)]
            lst.append(idx)
        return idx

    def pe(self, fn, r=(), w=()):
        return self.op("pe", fn, r, w)

    def act(self, fn, r=(), w=()):
        return self.op("act", fn, r, w)

    def dve(self, fn, r=(), w=()):
        return self.op("dve", fn, r, w)

    def pool(self, fn, r=(), w=()):
        return self.op("pool", fn, r, w)

    def I(self, eng, meth, *args, r=(), w=(), **kw):
        return self.op(eng, lambda e: getattr(e, meth)(*args, **kw), r, w)

    def dma(self, q, key, out, in_, r=(), w=(), **kw):
        return self.op(q, lambda e: e.dma_start(out=out, in_=in_, **kw), r, w, dma=key)

    def barrier(self):
        deps = set(i for i in self.last_eng.values() if i >= self.flushed) | set(self.dma_prev.values())
        for e in self.ENGS:
            self.op(e, lambda eng: eng.nop(), deps=deps)
        self.lastw.clear()
        self.readers.clear()

    def flush(self):
        nc = self.nc
        ops = self.ops
        batch = ops[self.flushed:]
        for o in batch:
            for d in o.deps:
                p = ops[d]
                if p.dma is None:
                    if p.eng == "pe" and o.eng == "pe" and o.dma is None:
                        continue
                    assert d >= self.flushed, "dependency on already-flushed compute op"
                    p.signals = True
        for o in batch:
            if o.dma is None and o.signals:
                self.cnt[o.eng] += 1
                o.sig = self.cnt[o.eng]
            if o.dma is not None and o.dma not in self.dsem:
                self.dsem[o.dma] = self.es.enter_context(nc.semaphore(self._nm("d")))
        esem, dsem = self.esem, self.dsem
        per = {e: [o for o in batch if o.eng == e] for e in self.ENGS}

        def run(ename, eng):
            waited = self.waited[ename]
            for o in per[ename]:
                need = {}
                for d in o.deps:
                    p = ops[d]
                    if p.dma is not None:
                        s, v = dsem[p.dma], 16 * p.dseq
                    else:
                        if p.eng == "pe" and ename == "pe" and o.dma is None:
                            continue
                        s, v = esem[p.eng], p.sig
                    if waited.get(s.num, 0) >= v:
                        continue
                    if need.get(s.num, (None, 0))[1] < v:
                        need[s.num] = (s, v)
                for s, v in need.values():
                    eng.wait_ge(s, v)
                    waited[s.num] = v
                ins = o.fn(eng)
                if o.dma is not None:
                    ins.then_inc(dsem[o.dma], 16)
                elif o.signals:
                    ins.then_inc(esem[ename], 1)

        with nc.Block() as block:
            @block.tensor
            def _(e):
                run("pe", e)

            @block.scalar
            def _(e):
                run("act", e)

            @block.vector
            def _(e):
                run("dve", e)

            @block.gpsimd
            def _(e):
                run("pool", e)

            @block.sync
            def _(e):
                run("sp", e)
        self.flushed = len(ops)

    def finish(self):
        if self.flushed < len(self.ops):
            self.barrier()
            self.flush()
        self.es.close()

import math
import ml_dtypes
from concourse.bass_utils import run_bass_kernel_spmd
import math
import numpy as np
import ml_dtypes

D = 1024
EPS = 1e-6
THETA = 500000.0
NBIS = 15
TOPK = 256
PI = math.pi
NEG = -1.0e30


def fm_blocks():
    b = []
    for h in range(8):
        b.append(("qa", h, 128, 32))
    for g in range(2):
        b.append(("ka", g, 128, 32))
    for h in range(8):
        b.append(("qi", h, 64, 16))
    b.append(("ki", 0, 64, 16))
    for i in range(24):
        b.append(("gd", i, 128, 0))
    return b


FM = fm_blocks()
FM_GROUPS = [FM[0:19], FM[19:35], FM[35:43]]
NFM = sum(b[2] + b[3] for b in FM)
TM = ([("v", 1280, 256), ("za", 1536, 512), ("za", 2048, 512), ("wi", 3136, 8)] +
      [("gt", 7256 + 512 * i, 512) for i in range(4)] +
      [("zb", 6216, 512), ("zb", 6728, 512), ("ba", 7240, 16)])
TM_GROUPS = [TM[0:6], TM[6:11]]
NTM = sum(c[2] for c in TM)
WG_MAX = 2560


def host_w_fm(w_in, l):
    W = w_in[l]
    cols = []
    for kind, i, nm, ns in FM:
        if kind == "qa":
            c0 = i * 128
            cols += [W[:, c0:c0 + 128], W[:, c0 + 16:c0 + 32], W[:, c0:c0 + 16]]
        elif kind == "ka":
            c0 = 1024 + i * 128
            cols += [W[:, c0:c0 + 128], W[:, c0 + 16:c0 + 32], W[:, c0:c0 + 16]]
        elif kind == "qi":
            c0 = 2560 + i * 64
            cols += [W[:, c0:c0 + 64], W[:, c0 + 8:c0 + 16], W[:, c0:c0 + 8]]
        elif kind == "ki":
            c0 = 3072
            cols += [W[:, c0:c0 + 64], W[:, c0 + 8:c0 + 16], W[:, c0:c0 + 8]]
        else:
            c0 = 3144 + i * 128
            cols += [W[:, c0:c0 + 128]]
    return np.ascontiguousarray(np.concatenate(cols, axis=1))


def host_w_tm(w_in, l):
    W = w_in[l]
    return np.ascontiguousarray(np.concatenate([W[:, c0:c0 + n] for _, c0, n in TM], axis=1))


def host_consts():
    c = {}
    c["ident_bf"] = np.eye(128, dtype=ml_dtypes.bfloat16)
    c["ident_f"] = np.eye(128, dtype=np.float32)
    c["i4"] = np.concatenate([np.eye(128)] * 4, axis=1).astype(ml_dtypes.bfloat16)
    q = np.arange(128)[:, None]
    k = np.arange(128)[None, :]
    c["tri_bias"] = np.where(k <= q, 0.0, NEG).astype(np.float32)
    j = np.arange(64)[:, None]
    i = np.arange(64)[None, :]
    m = np.zeros((64, 5, 64), np.float32)
    m[:, 0, :] = (j <= i)
    m[:, 1, :] = (i < j)
    m[:, 2, :] = np.where(i >= j, 0.0, NEG)
    m[:, 3, :] = np.where(i < j, 0.0, NEG)
    m[:, 4, :] = np.eye(64)
    c["m64"] = m
    cs = np.zeros((128, 8), np.float32)
    r = np.arange(32)
    cs[:32, 0] = THETA ** (-(2.0 * (r % 16)) / 32.0)
    r16 = np.arange(16)
    cs[:16, 1] = THETA ** (-(2.0 * (r16 % 8)) / 16.0)
    cs[:32, 2] = np.where(r < 16, -1.0, 1.0)
    cs[:16, 3] = np.where(r16 < 8, -1.0, 1.0)
    cs[:, 4] = EPS
    cs[:, 5] = 1.0
    c["cst"] = cs
    return c


class K:
    pass


def build(SEQ, NL, dbg=(), phases=('tab', 'A1', 'A2', 'gdn', 'attn', 'C')):
    nc = bass.Bass("TRN2", target_bir_lowering=False)
    k = K()
    k.nc, k.SEQ, k.NL = nc, SEQ, NL
    NT = SEQ // 128
    NB = SEQ // 512
    k.NT, k.NB = NT, NB

    def inp(name, shape, dt=F32):
        return nc.dram_tensor(name, list(shape), dt, kind="ExternalInput").ap()

    def scr(name, shape, dt=F32):
        kind = "ExternalOutput" if name in dbg else "Internal"
        return nc.dram_tensor(name, list(shape), dt, kind=kind).ap()

    k.x = inp("x", [SEQ, D])
    k.pos = inp("pos", [1, SEQ], I32)
    k.fgain = inp("fgain", [1, D])
    k.wfm = [inp("wfm%d" % l, [D, NFM]) for l in range(NL)]
    k.wtm = [inp("wtm%d" % l, [D, NTM]) for l in range(NL)]
    k.wout = [inp("wout%d" % l, [D, D]) for l in range(NL)]
    k.ng = [inp("ng%d" % l, [1, D]) for l in range(NL)]
    k.convw = [inp("convw%d" % l, [128, 24, 4]) for l in range(NL)]
    k.gdng = [inp("gdng%d" % l, [1, 128]) for l in range(NL)]
    k.alog = [inp("alog%d" % l, [1, 8]) for l in range(NL)]
    k.dtb = [inp("dtb%d" % l, [1, 8]) for l in range(NL)]
    k.ikg = [inp("ikg%d" % l, [64, 2]) for l in range(NL)]
    k.c_ident_bf = inp("ident_bf", [128, 128], BF16)
    k.c_ident_f = inp("ident_f", [128, 128])
    k.c_i4 = inp("i4", [128, 512], BF16)
    k.c_tri = inp("tri_bias", [128, 128])
    k.c_m64 = inp("m64", [64, 5, 64])
    k.c_cst = inp("cst", [128, 8])
    k.out = nc.dram_tensor("out", [SEQ, D], F32, kind="ExternalOutput").ap()

    k.tab = scr("tab", [4, 32, SEQ])
    k.hT = scr("hT", [128, 8, SEQ], BF16)
    k.qT = scr("qT", [8, 128, SEQ], BF16)
    k.kT = scr("kT", [2, 128, SEQ], BF16)
    k.qiT = scr("qiT", [8, 64, SEQ], BF16)
    k.kiT = scr("kiT", [64, SEQ], BF16)
    k.gT = scr("gT", [24, 128, SEQ])
    k.v = scr("v", [SEQ, 256], BF16)
    k.za = scr("za", [SEQ, D])
    k.wi = scr("wi", [SEQ, 8])
    k.gt = scr("gt", [SEQ, 2 * D])
    k.zb = scr("zb", [SEQ, D])
    k.ba = scr("ba", [SEQ, 16])
    k.yb = scr("yb", [SEQ, D])
    k.oa = scr("oa", [SEQ, D])
    k.x1 = scr("x1", [SEQ, D])

    P = Prog(nc)
    k.P = P
    k.banks = [P.ps("bank", [128, 512], F32) for _ in range(8)]
    k.bi = 0

    k.ident_bf = P.sb("ident_bf", [128, 128], BF16, persist=True)
    k.ident_f = P.sb("ident_f", [128, 128], F32, persist=True)
    k.i4 = P.sb("i4", [128, 512], BF16, persist=True)
    k.tri = P.sb("tri", [128, 128], F32, persist=True)
    k.m64 = P.sb("m64", [64, 5, 64], F32, persist=True)
    k.cst = P.sb("cst", [128, 8], F32, persist=True)
    k.ones_f = P.sb("ones_f", [128, 128], F32, persist=True)
    k.ones_bf = P.sb("ones_bf", [128, 8], BF16, persist=True)
    k.zeros_bf = P.sb("zeros_bf", [128, 512], BF16, persist=True)
    k.ones_b = P.sb("ones_b", [128, 128], BF16, persist=True)
    k.su_b = P.sb("su_b", [64, 64], BF16, persist=True)
    P.begin_phase()
    for t, src in ((k.ident_bf, k.c_ident_bf), (k.ident_f, k.c_ident_f), (k.i4, k.c_i4), (k.tri, k.c_tri),
                   (k.m64, k.c_m64), (k.cst, k.c_cst)):
        P.dma("sp", ("cload", id(t)), t[:], src, w=[("cst", "m64")] if t is k.m64 else [])
    P.pool(lambda e: e.memset(k.ones_f[:], 1.0))
    P.pool(lambda e: e.memset(k.ones_bf[:], 1.0))
    P.pool(lambda e: e.memset(k.zeros_bf[:], 0.0))
    P.pool(lambda e: e.memset(k.ones_b[:], 1.0))
    P.I("dve", "tensor_copy", out=k.su_b[:], in_=k.m64[:, 1, :], r=[("cst", "m64")], w=["su_b"])
    P.end_phase()

    if 'tab' in phases:
        phase_tables(k)
    for l in range(NL):
        src = k.x if l == 0 else k.x1
        if 'A1' in phases:
            phase_A1(k, l, src)
        if 'A2' in phases:
            phase_A2(k, l)
        if 'gdn' in phases:
            phase_gdn(k, l)
        if 'attn' in phases:
            phase_attn(k, l)
        if 'C' in phases:
            phase_C(k, l, src, last=(l == NL - 1))
    P.finish()
    return nc


def next_bank(k):
    i = k.bi % 8
    k.bi += 1
    return i, k.banks[i]


def phase_tables(k):
    P, SEQ = k.P, k.SEQ
    P.begin_phase()
    CH = min(SEQ, 2048)
    posi = P.sb("posi", [32, CH], I32)
    posf = P.sb("posf", [32, CH], F32)
    ang = P.sb("ang", [32, CH], F32)
    u = P.sb("u", [32, CH], F32)
    ki = P.sb("ki", [32, CH], I32)
    kf = P.sb("kf", [32, CH], F32)
    r = P.sb("r", [32, CH], F32)
    w = P.sb("w", [32, CH], F32)
    res = P.sb("res", [32, CH], F32)
    C1 = float(np.float32(2 * PI))
    C2 = float(2 * PI - C1)
    LIM = 3.14159
    for c0 in range(0, SEQ, CH):
        P.dma("sp", "posi", posi[:], k.pos[:, c0:c0 + CH].partition_broadcast(32), w=["posi"])
        P.dve(lambda e: e.tensor_copy(out=posf[:], in_=posi[:]), r=["posi"], w=["posf"])
        for ti, (rows, fcol, scol, shift) in enumerate(((32, 0, None, PI / 2), (32, 0, 2, 0.0),
                                                        (16, 1, None, PI / 2), (16, 1, 3, 0.0))):
            R = slice(0, rows)
            P.dve(lambda e, R=R, fcol=fcol, shift=shift: e.tensor_scalar(
                out=ang[R, :], in0=posf[R, :], scalar1=k.cst[R, fcol:fcol + 1], scalar2=shift,
                op0=ALU.mult, op1=ALU.add), r=["posf"], w=["ang"])
            P.dve(lambda e, R=R: e.tensor_scalar(out=u[R, :], in0=ang[R, :], scalar1=1.0 / (2 * PI), scalar2=None,
                                                 op0=ALU.mult), r=["ang"], w=["u"])
            P.dve(lambda e, R=R: e.tensor_copy(out=ki[R, :], in_=u[R, :]), r=["u"], w=["ki"])
            P.dve(lambda e, R=R: e.tensor_copy(out=kf[R, :], in_=ki[R, :]), r=["ki"], w=["kf"])
            P.dve(lambda e, R=R: e.scalar_tensor_tensor(out=r[R, :], in0=kf[R, :], scalar=-C1, in1=ang[R, :],
                                                        op0=ALU.mult, op1=ALU.add), r=["kf", "ang"], w=["r"])
            P.dve(lambda e, R=R: e.scalar_tensor_tensor(out=r[R, :], in0=kf[R, :], scalar=-C2, in1=r[R, :],
                                                        op0=ALU.mult, op1=ALU.add), r=["kf", "r"], w=["r"])
            P.dve(lambda e, R=R: e.tensor_scalar(out=w[R, :], in0=r[R, :], scalar1=PI, scalar2=-2 * PI,
                                                 op0=ALU.is_gt, op1=ALU.mult), r=["r"], w=["w"])
            P.dve(lambda e, R=R: e.tensor_tensor(out=r[R, :], in0=r[R, :], in1=w[R, :], op=ALU.add),
                  r=["r", "w"], w=["r"])
            P.dve(lambda e, R=R: e.tensor_scalar(out=w[R, :], in0=r[R, :], scalar1=-PI, scalar2=2 * PI,
                                                 op0=ALU.is_lt, op1=ALU.mult), r=["r"], w=["w"])
            P.dve(lambda e, R=R: e.tensor_tensor(out=r[R, :], in0=r[R, :], in1=w[R, :], op=ALU.add),
                  r=["r", "w"], w=["r"])
            P.dve(lambda e, R=R: e.tensor_scalar(out=r[R, :], in0=r[R, :], scalar1=-LIM, scalar2=LIM,
                                                 op0=ALU.max, op1=ALU.min), r=["r"], w=["r"])
            if scol is None:
                P.act(lambda e, R=R: e.activation(out=res[R, :], in_=r[R, :], func=AF.Sin), r=["r"], w=["res"])
            else:
                P.act(lambda e, R=R, scol=scol: e.activation(out=res[R, :], in_=r[R, :], func=AF.Sin,
                                                             scale=k.cst[R, scol:scol + 1]), r=["r"], w=["res"])
            P.dma("sp", "tabst", k.tab[ti, 0:rows, c0:c0 + CH], res[R, :], r=["res"])
    P.end_phase()


def phase_A1(k, l, src):
    P, NT = k.P, k.NT
    P.begin_phase()
    gain_bc = P.sb("gain_bc", [128, D], F32)
    P.dma("sp", "gain", gain_bc[:], k.ng[l].partition_broadcast(128), w=["gain"])
    xt = [P.sb("xt", [128, D], F32) for _ in range(2)]
    junk = P.sb("junk", [128, D], BF16)
    ss = P.sb("ss", [128, NT], F32)
    rstd = P.sb("rstd", [128, NT], F32)
    hb = [P.sb("hb", [128, D], BF16) for _ in range(2)]
    ht = [P.sb("ht", [128, 8, 128], BF16) for _ in range(2)]
    for i in range(NT):
        s = i % 2
        P.dma("sp", ("xt", s), xt[s][:], src[i * 128:(i + 1) * 128, :], w=[("xt", s)])
        P.act(lambda e, s=s, i=i: e.activation(out=junk[:], in_=xt[s][:], func=AF.Square,
                                               accum_out=ss[:, i:i + 1]),
              r=[("xt", s)], w=["junk", ("ss", i)])
        P.act(lambda e, i=i: e.activation(out=rstd[:, i:i + 1], in_=ss[:, i:i + 1], func=AF.Sqrt,
                                          bias=k.cst[:, 4:5], scale=1.0 / D),
              r=[("ss", i)], w=[("rstd", i)])
        P.dve(lambda e, i=i: e.reciprocal(out=rstd[:, i:i + 1], in_=rstd[:, i:i + 1]),
              r=[("rstd", i)], w=[("rstd", i)])
        P.dve(lambda e, s=s, i=i: e.scalar_tensor_tensor(out=hb[s][:], in0=xt[s][:], scalar=rstd[:, i:i + 1],
                                                         in1=gain_bc[:], op0=ALU.mult, op1=ALU.mult),
              r=[("xt", s), ("rstd", i), "gain"], w=[("hb", s)])
        bi, bank = next_bank(k)
        pT = bank[:].bitcast(BF16).rearrange("p (c t) -> p c t", t=128)
        for kc in range(8):
            P.pe(lambda e, s=s, kc=kc, pT=pT: e.transpose(out=pT[:, kc, :], in_=hb[s][:, kc * 128:(kc + 1) * 128],
                                                          identity=k.ident_bf[:]),
                 r=[("hb", s)], w=[("pb", bi)])
        P.act(lambda e, s=s, pT=pT: e.copy(out=ht[s][:], in_=pT), r=[("pb", bi)], w=[("ht", s)])
        P.dma("pool", ("hts", s), k.hT[:, :, i * 128:(i + 1) * 128], ht[s][:], r=[("ht", s)])
    P.end_phase()


def phase_A2(k, l):
    P, NB, SEQ = k.P, k.NB, k.SEQ
    IDX_SCALE = (8 ** -0.5) * (64 ** -0.5)
    P.begin_phase()
    wb = [P.sb("wb", [128, 8, WG_MAX], BF16) for _ in range(2)]
    wf = [P.sb("wf", [128, WG_MAX], F32) for _ in range(2)]
    hb = [P.sb("hblk", [128, 8, 512], BF16) for _ in range(2)]
    ca = [P.sb("ca", [32, 512], F32) for _ in range(2)]
    sa = [P.sb("sa", [32, 512], F32) for _ in range(2)]
    cas = [P.sb("cas", [32, 512], F32) for _ in range(2)]
    sas = [P.sb("sas", [32, 512], F32) for _ in range(2)]
    ci = [P.sb("ci", [16, 512], F32) for _ in range(2)]
    si = [P.sb("si", [16, 512], F32) for _ in range(2)]
    t1 = P.sb("t1", [32, 512], F32)
    t2 = P.sb("t2", [32, 512], F32)
    obf = [P.sb("obf", [128, 512], BF16) for _ in range(3)]
    off = [P.sb("off", [128, 512], F32) for _ in range(3)]
    sq = P.sb("sq", [64, 512], F32)
    rinv = P.sb("rinv", [64, 512], F32)
    kn = P.sb("kn", [64, 512], F32)
    ksw = P.sb("ksw", [16, 512], F32)
    ikg = P.sb("ikg", [64, 2], F32)
    P.dma("sp", "ikg", ikg[:], k.ikg[l], w=["ikg"])
    wfv = k.wfm[l].rearrange("(kc p) c -> p kc c", p=128)
    wtv = k.wtm[l].rearrange("(kc p) c -> p kc c", p=128)
    st = {"w": 0, "h": 0, "o": 0, "q": 0}

    def load_w(view, c0, n):
        s = st["w"] % 2
        st["w"] += 1
        for kc in range(8):
            f = kc % 2
            P.dma("sp", ("wf", f), wf[f][:, :n], view[:, kc, c0:c0 + n], w=[("wf", f)])
            P.I("pool", "tensor_copy", out=wb[s][:, kc, :n], in_=wf[f][:, :n], r=[("wf", f)], w=[("wb", s, kc)])
        return s

    def load_h(tb, need_tab):
        s = st["h"] % 2
        st["h"] += 1
        T = slice(tb * 512, (tb + 1) * 512)
        P.dma("sp", ("hblk", s), hb[s][:], k.hT[:, :, T], w=[("hblk", s)])
        if need_tab:
            P.dma("sp", ("ca", s), ca[s][:], k.tab[0, :, T], w=[("ca", s)])
            P.dma("sp", ("sa", s), sa[s][:], k.tab[1, :, T], w=[("sa", s)])
            P.dma("sp", ("ci", s), ci[s][:], k.tab[2, 0:16, T], w=[("ci", s)])
            P.dma("sp", ("si", s), si[s][:], k.tab[3, 0:16, T], w=[("si", s)])
            P.I("dve", "tensor_scalar", out=cas[s][:], in0=ca[s][:], scalar1=128 ** -0.5, scalar2=None, op0=ALU.mult,
                r=[("ca", s)], w=[("cas", s)])
            P.I("dve", "tensor_scalar", out=sas[s][:], in0=sa[s][:], scalar1=128 ** -0.5, scalar2=None, op0=ALU.mult,
                r=[("sa", s)], w=[("sas", s)])
        return s

    def qname():
        st["q"] += 1
        return "sp" if st["q"] % 2 == 0 else "pool"

    def rope(nrot, bankm, bm, banksw, bs, cos_t, ckey, sin_t, skey, scale, o):
        P.I("dve", "tensor_tensor", out=t1[0:nrot, :], in0=bankm[0:nrot, :], in1=cos_t[0:nrot, :], op=ALU.mult,
            r=[("pb", bm), ckey], w=["t1"])
        P.I("dve", "tensor_tensor", out=t2[0:nrot, :], in0=banksw[0:nrot, :], in1=sin_t[0:nrot, :], op=ALU.mult,
            r=[("pb", bs), skey], w=["t2"])
        P.I("dve", "tensor_tensor", out=obf[o][0:nrot, :], in0=t1[0:nrot, :], in1=t2[0:nrot, :], op=ALU.add,
            r=["t1", "t2"], w=[("obf", o)])

    import os
    for gi, grp in enumerate(FM_GROUPS):
        gc0 = sum(b[2] + b[3] for g in FM_GROUPS[:gi] for b in g)
        gn = sum(b[2] + b[3] for b in grp)
        ws = load_w(wfv, gc0, gn)
        wk = [("wb", ws, kc) for kc in range(8)]
        for tb in range(NB):
            hs = load_h(tb, gi == 0)
            T = slice(tb * 512, (tb + 1) * 512)
            c0 = 0
            for (kind, idx, nm, ns) in grp:
                bm, bankm = next_bank(k)
                for kc in range(8):
                    P.pe(lambda e, ws=ws, hs=hs, kc=kc, c0=c0, nm=nm, bankm=bankm: e.matmul(
                        bankm[0:nm, :], lhsT=wb[ws][:, kc, c0:c0 + nm], rhs=hb[hs][:, kc, :],
                        start=(kc == 0), stop=(kc == 7)), r=[wk[kc], ("hblk", hs)], w=[("pb", bm)])
                bs, banksw = None, None
                if ns and os.environ.get("MK_QA", "full") != "copy":
                    bs, banksw = next_bank(k)
                    for kc in range(8):
                        P.pe(lambda e, ws=ws, hs=hs, kc=kc, c0=c0, nm=nm, ns=ns, banksw=banksw: e.matmul(
                            banksw[0:ns, :], lhsT=wb[ws][:, kc, c0 + nm:c0 + nm + ns], rhs=hb[hs][:, kc, :],
                            start=(kc == 0), stop=(kc == 7)), r=[wk[kc], ("hblk", hs)], w=[("pb", bs)])
                c0 += nm + ns
                o = st["o"] % 3
                st["o"] += 1
                if kind in ("qa", "ka"):
                    scale = 128 ** -0.5 if kind == "qa" else 1.0
                    P.act(lambda e, o=o, bankm=bankm, scale=scale: e.activation(
                        out=obf[o][:, :], in_=bankm[:, :], func=AF.Copy, scale=scale),
                        r=[("pb", bm)], w=[("obf", o)])
                    if os.environ.get("MK_QA", "full") == "full":
                        if kind == "qa":
                            rope(32, bankm, bm, banksw, bs, cas[hs], ("cas", hs), sas[hs], ("sas", hs), scale, o)
                        else:
                            rope(32, bankm, bm, banksw, bs, ca[hs], ("ca", hs), sa[hs], ("sa", hs), scale, o)
                    dst = (k.qT if kind == "qa" else k.kT)[idx, :, T]
                    P.dma(qname(), ("obf", o), dst, obf[o][:, :], r=[("obf", o)])
                elif kind == "qi":
                    P.act(lambda e, o=o, bankm=bankm: e.copy(out=obf[o][0:64, :], in_=bankm[0:64, :]),
                          r=[("pb", bm)], w=[("obf", o)])
                    rope(16, bankm, bm, banksw, bs, ci[hs], ("ci", hs), si[hs], ("si", hs), 1.0, o)
                    P.dma(qname(), ("obf", o), k.qiT[idx, :, T], obf[o][0:64, :], r=[("obf", o)])
                elif kind == "ki":
                    P.act(lambda e, bankm=bankm: e.activation(out=sq[:, :], in_=bankm[0:64, :], func=AF.Square),
                          r=[("pb", bm)], w=["sq"])
                    bq, bankq = next_bank(k)
                    P.pe(lambda e, bankq=bankq: e.matmul(bankq[0:64, :], lhsT=k.ones_f[0:64, 0:64], rhs=sq[:, :],
                                                         start=True, stop=True), r=["sq"], w=[("pb", bq)])
                    P.act(lambda e, bankq=bankq: e.activation(out=rinv[:, :], in_=bankq[0:64, :], func=AF.Sqrt,
                                                              bias=k.cst[0:64, 4:5], scale=1.0 / 64),
                          r=[("pb", bq)], w=["rinv"])
                    P.dve(lambda e: e.reciprocal(out=rinv[:, :], in_=rinv[:, :]), r=["rinv"], w=["rinv"])
                    P.dve(lambda e, bankm=bankm: e.scalar_tensor_tensor(
                        out=kn[:, :], in0=bankm[0:64, :], scalar=ikg[:, 0:1], in1=rinv[:, :],
                        op0=ALU.mult, op1=ALU.mult), r=[("pb", bm), "rinv", "ikg"], w=["kn"])
                    P.dve(lambda e, banksw=banksw: e.scalar_tensor_tensor(
                        out=ksw[:, :], in0=banksw[0:16, :], scalar=ikg[0:16, 1:2], in1=rinv[0:16, :],
                        op0=ALU.mult, op1=ALU.mult), r=[("pb", bs), "rinv", "ikg"], w=["ksw"])
                    P.act(lambda e, o=o: e.copy(out=obf[o][0:64, :], in_=kn[:, :]), r=["kn"], w=[("obf", o)])
                    P.dve(lambda e, hs=hs: e.tensor_tensor(out=t1[0:16, :], in0=kn[0:16, :], in1=ci[hs][:, :],
                                                           op=ALU.mult), r=["kn", ("ci", hs)], w=["t1"])
                    P.dve(lambda e, hs=hs: e.tensor_tensor(out=t2[0:16, :], in0=ksw[:, :], in1=si[hs][:, :],
                                                           op=ALU.mult), r=["ksw", ("si", hs)], w=["t2"])
                    P.dve(lambda e, o=o: e.tensor_tensor(out=obf[o][0:16, :], in0=t1[0:16, :], in1=t2[0:16, :],
                                                         op=ALU.add), r=["t1", "t2"], w=[("obf", o)])
                    P.dma(qname(), ("obf", o), k.kiT[:, T], obf[o][0:64, :], r=[("obf", o)])
                else:
                    if o % 2 == 0:
                        P.act(lambda e, o=o, bankm=bankm: e.copy(out=off[o][:, :], in_=bankm[:, :]),
                              r=[("pb", bm)], w=[("off", o)])
                    else:
                        P.dve(lambda e, o=o, bankm=bankm: e.tensor_copy(out=off[o][:, :], in_=bankm[:, :]),
                              r=[("pb", bm)], w=[("off", o)])
                    P.dma(qname(), ("off", o), k.gT[idx, :, T], off[o][:, :], r=[("off", o)])

    for gi, grp in enumerate(TM_GROUPS):
        gc0 = sum(c[2] for g in TM_GROUPS[:gi] for c in g)
        gn = sum(c[2] for c in grp)
        ws = load_w(wtv, gc0, gn)
        wk = [("wb", ws, kc) for kc in range(8)]
        for tb in range(NB):
            hs = load_h(tb, False)
            for tt in range(4):
                R = slice(tb * 512 + tt * 128, tb * 512 + (tt + 1) * 128)
                c0 = 0
                for (kind, csrc, n) in grp:
                    bm, bankm = next_bank(k)
                    for kc in range(8):
                        P.pe(lambda e, ws=ws, hs=hs, kc=kc, c0=c0, n=n, tt=tt, bankm=bankm: e.matmul(
                            bankm[:, 0:n], lhsT=hb[hs][:, kc, tt * 128:(tt + 1) * 128], rhs=wb[ws][:, kc, c0:c0 + n],
                            start=(kc == 0), stop=(kc == 7)), r=[wk[kc], ("hblk", hs)], w=[("pb", bm)])
                    c0 += n
                    o = st["o"] % 3
                    st["o"] += 1
                    if kind == "v":
                        P.act(lambda e, o=o, bankm=bankm, n=n: e.copy(out=obf[o][:, 0:n], in_=bankm[:, 0:n]),
                              r=[("pb", bm)], w=[("obf", o)])
                        P.dma(qname(), ("obf", o), k.v[R, :], obf[o][:, 0:n], r=[("obf", o)])
                        continue
                    if kind in ("za", "zb"):
                        P.act(lambda e, o=o, bankm=bankm, n=n: e.activation(out=off[o][:, 0:n], in_=bankm[:, 0:n],
                                                                            func=AF.Silu),
                              r=[("pb", bm)], w=[("off", o)])
                        dst = (k.za[R, csrc - 1536:csrc - 1536 + n] if kind == "za"
                               else k.zb[R, csrc - 6216:csrc - 6216 + n])
                    elif kind == "gt":
                        P.act(lambda e, o=o, bankm=bankm, n=n: e.activation(out=off[o][:, 0:n], in_=bankm[:, 0:n],
                                                                            func=AF.Sigmoid),
                              r=[("pb", bm)], w=[("off", o)])
                        dst = k.gt[R, csrc - 7256:csrc - 7256 + n]
                    elif kind == "wi":
                        P.dve(lambda e, o=o, bankm=bankm, n=n: e.tensor_scalar(
                            out=off[o][:, 0:n], in0=bankm[:, 0:n], scalar1=IDX_SCALE, scalar2=None, op0=ALU.mult),
                            r=[("pb", bm)], w=[("off", o)])
                        dst = k.wi[R, :]
                    else:
                        P.dve(lambda e, o=o, bankm=bankm, n=n: e.tensor_copy(out=off[o][:, 0:n], in_=bankm[:, 0:n]),
                              r=[("pb", bm)], w=[("off", o)])
                        dst = k.ba[R, :]
                    P.dma(qname(), ("off", o), dst, off[o][:, 0:n], r=[("off", o)])
    P.end_phase()


def phase_gdn(k, l):
    P, SEQ = k.P, k.SEQ
    NS = SEQ // 512
    DKS = 128 ** -0.5
    P.begin_phase()
    U, SU, MBu, MBl, I64 = (k.m64[:, i, :] for i in range(5))

    def bcn(a):
        return a.unsqueeze(1).to_broadcast([64, 8, 64])

    def bci(a, n=64):
        return a.unsqueeze(2).to_broadcast([64, a.shape[1], n])

    def v3(a, n):
        return a.rearrange("p (c n) -> p c n", n=n)

    rb = {"i": 0}

    def nb4():
        i = rb["i"] % 4
        rb["i"] += 1
        return i, k.banks[i]

    def T(name, shape, dt=F32):
        return P.sb(name, shape, dt)

    convw = T("convw", [128, 24, 4])
    gg = T("gg", [64, 128])
    alog = T("alog", [64, 8])
    dtb = T("dtb", [64, 8])
    negA = T("negA", [64, 8])
    P.dma("sp", "g_convw", convw[:], k.convw[l], w=["convw"])
    P.dma("sp", "g_gg", gg[:], k.gdng[l].partition_broadcast(64), w=["gg"])
    P.dma("sp", "g_alog", alog[:], k.alog[l].partition_broadcast(64), w=["alog"])
    P.dma("sp", "g_dtb", dtb[:], k.dtb[l].partition_broadcast(64), w=["dtb"])
    P.I("act", "activation", out=negA[:], in_=alog[:], func=AF.Exp, r=["alog"], w=["negA"])
    P.I("dve", "tensor_scalar", out=negA[:], in0=negA[:], scalar1=-1.0, scalar2=None, op0=ALU.mult,
        r=["negA"], w=["negA"])
    S = [[T("S", [128, 128]) for _ in range(2)] for _ in range(8)]
    Sb = [[T("Sb", [128, 128], BF16) for _ in range(2)] for _ in range(8)]
    for h in range(8):
        P.I("pool", "memset", S[h][0][:], 0.0, w=[("S", h, 0)])
        P.I("pool", "memset", Sb[h][0][:], 0.0, w=[("Sb", h, 0)])
    bat = T("bat", [64, 8, 16])
    beta = T("beta", [64, 8, 8])
    xa = T("xa", [64, 8, 8])
    gall = T("gall", [64, 8, 8])
    eps64 = k.cst[0:64, 4:5]
    one64 = k.cst[0:64, 5:6]

    def make_set(sid, b6i, b7i):
        B6, B7 = k.banks[b6i], k.banks[b7i]
        K6, K7 = ("pb", b6i), ("pb", b7i)

        def N(x):
            return (sid, x)

        xin = [T("xin", [128, 515]) for _ in range(3)]
        y = [T("y", [128, 512]) for _ in range(3)]
        sqt = T("sqt", [128, 512], BF16)
        yvb = T("yvb", [128, 512], BF16)
        ctmp = T("ctmp", [128, 512])
        rin = [T("rin", [128, 512]) for _ in range(2)]
        qn = T("qn", [128, 512], BF16)
        kn = T("kn", [128, 512], BF16)
        egbc = ctmp
        qd = T("qd", [128, 512], BF16)
        gh = T("gh", [64, 8])
        bh = T("bh", [64, 8])
        nbh = T("nbh", [64, 8])
        gcs = T("gcs", [64, 8])
        dgl = T("dgl", [64, 8])
        egl = T("egl", [64, 8])
        eg = T("eg", [64, 8])
        beg = T("beg", [64, 8])
        gtot = T("gtot", [128, 8])
        Gm = T("Gm", [64, 8, 64])
        d3 = T("d3", [64, 8, 64])
        DTu = T("DTu", [64, 8, 64])
        DTl = T("DTl", [64, 8, 64])
        Bm = T("Bm", [64, 8, 64], BF16)
        Mfac = T("Mfac", [64, 8, 64])
        Nm = [T("Nm", [64, 8, 64], BF16) for _ in range(2)]
        NTm = [T("NTm", [64, 8, 64], BF16) for _ in range(2)]
        Rm = [T("Rm", [64, 8, 64], BF16) for _ in range(2)]
        qkT = T("qkT", [64, 8, 64], BF16)
        kbg = T("kbg", [64, 8, 128], BF16)
        kdec = T("kdec", [64, 8, 128], BF16)
        vb = T("vb", [64, 8, 128], BF16)
        us = T("us", [64, 8, 128])
        os_ = T("os", [64, 8, 128])
        sq3 = T("sq3", [64, 8, 128])
        zbt = T("zbt", [64, 8, 128])
        y1 = T("y1", [64, 8, 128])
        wTs = T("wTs", [128, 512], BF16)
        vnew = T("vnew", [64, 128], BF16)
        ssq = T("ssq", [64, 8])
        rinv8 = T("rinv8", [64, 8])

        def unit(s, h):
            t0 = s * 512
            for i, blk in enumerate((h, 8 + h, 16 + h)):
                if s == 0:
                    P.I("pool", "memset", xin[i][:, 0:3], 0.0, w=[N(("xin", i))])
                    P.dma("sp", (sid, "g_xin", i), xin[i][:, 3:515], k.gT[blk, :, 0:512], w=[N(("xin", i))])
                else:
                    P.dma("sp", (sid, "g_xin", i), xin[i][:, :], k.gT[blk, :, t0 - 3:t0 + 512], w=[N(("xin", i))])
                P.I("act", "activation", out=y[i][:], in_=xin[i][:, 0:512], func=AF.Copy, scale=convw[:, blk, 0:1],
                    r=[N(("xin", i)), "convw"], w=[N(("y", i))])
                for j in range(1, 4):
                    P.I("dve", "scalar_tensor_tensor", out=y[i][:], in0=xin[i][:, j:j + 512],
                        scalar=convw[:, blk, j:j + 1], in1=y[i][:], op0=ALU.mult, op1=ALU.add,
                        r=[N(("xin", i)), "convw", N(("y", i))], w=[N(("y", i))])
                if i < 2:
                    P.I("act", "activation", out=y[i][:], in_=y[i][:], func=AF.Silu, r=[N(("y", i))],
                        w=[N(("y", i))])
                else:
                    P.I("act", "activation", out=yvb[:], in_=y[i][:], func=AF.Silu, r=[N(("y", i))], w=[N("yvb")])
                yield
            for i in range(2):
                P.I("act", "activation", out=sqt[:], in_=y[i][:], func=AF.Square, r=[N(("y", i))], w=[N("sqt")])
                bi, bk = nb4()
                P.I("pe", "matmul", bk[:, :], lhsT=k.ones_b[:, :], rhs=sqt[:], start=True, stop=True,
                    r=[N("sqt")], w=[("pb", bi)])
                P.I("act", "activation", out=rin[i][:], in_=bk[:, :], func=AF.Sqrt, bias=k.cst[:, 4:5],
                    r=[("pb", bi)], w=[N(("rin", i))])
                P.I("dve", "reciprocal", out=rin[i][:], in_=rin[i][:], r=[N(("rin", i))], w=[N(("rin", i))])
            P.I("dve", "scalar_tensor_tensor", out=qn[:], in0=y[0][:], scalar=DKS, in1=rin[0][:], op0=ALU.mult,
                op1=ALU.mult, r=[N(("y", 0)), N(("rin", 0))], w=[N("qn")])
            P.I("dve", "tensor_tensor", out=kn[:], in0=y[1][:], in1=rin[1][:], op=ALU.mult,
                r=[N(("y", 1)), N(("rin", 1))], w=[N("kn")])
            yield
            P.I("dve", "tensor_copy", out=gh[:], in_=gall[:, :, h], r=["gall"], w=[N("gh")])
            P.I("dve", "tensor_copy", out=bh[:], in_=beta[:, :, h], r=["beta"], w=[N("bh")])
            P.I("dve", "tensor_scalar", out=nbh[:], in0=beta[:, :, h], scalar1=-1.0, scalar2=None, op0=ALU.mult,
                r=["beta"], w=[N("nbh")])
            P.I("pe", "matmul", B6[0:64, 0:8], lhsT=U, rhs=gh[:], start=True, stop=True, r=[N("gh")], w=[K6])
            P.I("pe", "matmul", B6[:, 8:16], lhsT=k.ones_f[0:64, :], rhs=gh[:], start=True, stop=True,
                r=[N("gh")], w=[K6])
            P.I("dve", "tensor_copy", out=gcs[:], in_=B6[0:64, 0:8], r=[K6], w=[N("gcs")])
            P.I("dve", "tensor_tensor", out=dgl[:], in0=B6[0:64, 8:16], in1=gcs[:], op=ALU.subtract,
                r=[K6, N("gcs")], w=[N("dgl")])
            P.I("act", "activation", out=egl[:], in_=dgl[:], func=AF.Exp, r=[N("dgl")], w=[N("egl")])
            P.I("act", "activation", out=eg[:], in_=gcs[:], func=AF.Exp, r=[N("gcs")], w=[N("eg")])
            P.I("act", "activation", out=gtot[:], in_=B6[:, 8:16], func=AF.Exp, r=[K6], w=[N("gtot")])
            P.I("dve", "tensor_tensor", out=beg[:], in0=bh[:], in1=eg[:], op=ALU.mult, r=[N("bh"), N("eg")],
                w=[N("beg")])
            P.I("dve", "tensor_tensor", out=Gm[:], in0=bcn(U), in1=bci(gh[:]), op=ALU.mult, r=[N("gh")], w=[N("Gm")])
            bB, bkB = nb4()
            P.I("pe", "matmul", bkB[:, :], lhsT=k.ones_f[0:64, :], rhs=Gm[:].rearrange("p c n -> p (c n)"),
                start=True, stop=True, r=[N("Gm")], w=[("pb", bB)])
            P.I("act", "activation", out=egbc[:], in_=bkB[:, :], func=AF.Exp, r=[("pb", bB)], w=[N("ctmp")])
            P.I("dve", "tensor_tensor", out=qd[:], in0=qn[:], in1=egbc[:], op=ALU.mult, r=[N("qn"), N("ctmp")],
                w=[N("qd")])
            P.I("dve", "tensor_tensor", out=d3[:], in0=v3(bkB[0:64, :], 64), in1=bci(gcs[:]), op=ALU.subtract,
                r=[("pb", bB), N("gcs")], w=[N("d3")])
            yield
            P.I("dve", "tensor_tensor", out=DTu[:], in0=d3[:], in1=bcn(MBu), op=ALU.add, r=[N("d3")], w=[N("DTu")])
            P.I("act", "activation", out=DTu[:], in_=DTu[:], func=AF.Exp, r=[N("DTu")], w=[N("DTu")])
            P.I("dve", "scalar_tensor_tensor", out=DTl[:], in0=d3[:], scalar=-1.0, in1=bcn(MBl), op0=ALU.mult,
                op1=ALU.add, r=[N("d3")], w=[N("DTl")])
            P.I("act", "activation", out=DTl[:], in_=DTl[:], func=AF.Exp, r=[N("DTl")], w=[N("DTl")])
            P.I("dve", "tensor_tensor", out=DTl[:], in0=DTl[:], in1=bci(nbh[:]), op=ALU.mult,
                r=[N("DTl"), N("nbh")], w=[N("DTl")])
            P.I("dve", "tensor_tensor", out=Bm[:], in0=bcn(I64), in1=bci(nbh[:]), op=ALU.mult, r=[N("nbh")],
                w=[N("Bm")])
            bN, bkN = nb4()
            P.I("pe", "matmul", bkN[0:64, :], lhsT=k.su_b[:, :], rhs=Bm[:].rearrange("p c n -> p (c n)"), start=True,
                stop=True, r=[N("Bm")], w=[("pb", bN)])
            P.I("dve", "tensor_tensor", out=Mfac[:], in0=DTu[:], in1=v3(bkN[0:64, :], 64), op=ALU.mult,
                r=[N("DTu"), ("pb", bN)], w=[N("Mfac")])
            yield
            for half in range(2):
                bi, bk = nb4()
                bkb = bk[:].bitcast(BF16)
                for m in range(4):
                    c = half * 4 + m
                    P.I("pe", "transpose", out=bkb[0:64, m * 128:(m + 1) * 128], in_=kn[:, c * 64:(c + 1) * 64],
                        identity=k.ident_bf[:, :], r=[N("kn")], w=[("pb", bi)])
                hs = slice(half * 4, half * 4 + 4)
                P.I("dve", "tensor_tensor", out=kbg[:, hs, :], in0=v3(bkb[0:64, 0:512], 128),
                    in1=beg[:, hs].unsqueeze(2).to_broadcast([64, 4, 128]), op=ALU.mult,
                    r=[("pb", bi), N("beg")], w=[N(("kbg", half))])
                P.I("dve", "tensor_tensor", out=kdec[:, hs, :], in0=v3(bkb[0:64, 0:512], 128),
                    in1=egl[:, hs].unsqueeze(2).to_broadcast([64, 4, 128]), op=ALU.mult,
                    r=[("pb", bi), N("egl")], w=[N(("kdec", half))])
                bi, bk = nb4()
                bkb = bk[:].bitcast(BF16)
                for m in range(4):
                    c = half * 4 + m
                    P.I("pe", "transpose", out=bkb[0:64, m * 128:(m + 1) * 128], in_=yvb[:, c * 64:(c + 1) * 64],
                        identity=k.ident_bf[:, :], r=[N("yvb")], w=[("pb", bi)])
                P.I("dve", "tensor_tensor", out=vb[:, hs, :], in0=v3(bkb[0:64, 0:512], 128),
                    in1=bh[:, hs].unsqueeze(2).to_broadcast([64, 4, 128]), op=ALU.mult,
                    r=[("pb", bi), N("bh")], w=[N(("vb", half))])
                yield
            bKK, bkKK = nb4()
            for m in range(8):
                cs = slice(m * 64, (m + 1) * 64)
                P.I("pe", "matmul", bkKK[0:64, cs], lhsT=kn[:, cs], rhs=kn[:, cs], start=True, stop=True,
                    r=[N("kn")], w=[("pb", bKK)])
            bKQ, bkKQ = nb4()
            for m in range(8):
                cs = slice(m * 64, (m + 1) * 64)
                P.I("pe", "matmul", bkKQ[0:64, cs], lhsT=kn[:, cs], rhs=qn[:, cs], start=True, stop=True,
                    r=[N("kn"), N("qn")], w=[("pb", bKQ)])
            P.I("dve", "tensor_tensor", out=Nm[0][:], in0=v3(bkKK[0:64, :], 64), in1=Mfac[:], op=ALU.mult,
                r=[("pb", bKK), N("Mfac")], w=[N(("Nm", 0))])
            P.I("dve", "tensor_tensor", out=NTm[0][:], in0=v3(bkKK[0:64, :], 64), in1=DTl[:], op=ALU.mult,
                r=[("pb", bKK), N("DTl")], w=[N(("NTm", 0))])
            P.I("dve", "tensor_tensor", out=qkT[:], in0=v3(bkKQ[0:64, :], 64), in1=DTu[:], op=ALU.mult,
                r=[("pb", bKQ), N("DTu")], w=[N("qkT")])
            P.I("dve", "tensor_tensor", out=Rm[0][:], in0=Nm[0][:], in1=bcn(I64), op=ALU.add,
                r=[N(("Nm", 0))], w=[N(("Rm", 0))])
            yield
            cn, cr = 0, 0
            for it in range(5):
                nn = 1 - cn
                if it < 4:
                    bA, bkA = nb4()
                    for m in range(8):
                        cs = slice(m * 64, (m + 1) * 64)
                        P.I("pe", "matmul", bkA[0:64, cs], lhsT=NTm[cn][:, m, :], rhs=Nm[cn][:, m, :], start=True,
                            stop=True, r=[N(("NTm", cn)), N(("Nm", cn))], w=[("pb", bA)])
                bBt, bkBt = nb4()
                for m in range(8):
                    cs = slice(m * 64, (m + 1) * 64)
                    P.I("pe", "matmul", bkBt[0:64, cs], lhsT=Nm[cn][:, m, :], rhs=NTm[cn][:, m, :], start=True,
                        stop=True, r=[N(("NTm", cn)), N(("Nm", cn))], w=[("pb", bBt)])
                P.I("act", "copy", out=NTm[nn][:], in_=v3(bkBt[0:64, :], 64), r=[("pb", bBt)], w=[N(("NTm", nn))])
                bC, bkC = nb4()
                for m in range(8):
                    cs = slice(m * 64, (m + 1) * 64)
                    P.I("pe", "matmul", bkC[0:64, cs], lhsT=NTm[nn][:, m, :], rhs=Rm[cr][:, m, :], start=True,
                        stop=True, r=[N(("NTm", nn)), N(("Rm", cr))], w=[("pb", bC)])
                P.I("dve", "tensor_tensor", out=Rm[1 - cr][:], in0=Rm[cr][:], in1=v3(bkC[0:64, :], 64), op=ALU.add,
                    r=[N(("Rm", cr)), ("pb", bC)], w=[N(("Rm", 1 - cr))])
                cr = 1 - cr
                if it < 4:
                    P.I("act", "copy", out=Nm[nn][:], in_=v3(bkA[0:64, :], 64), r=[("pb", bA)], w=[N(("Nm", nn))])
                cn = nn
                yield
            R = Rm[cr]
            rk = N(("Rm", cr))
            for half in range(2):
                bi, bk = nb4()
                for m in range(4):
                    c = half * 4 + m
                    P.I("pe", "matmul", bk[0:64, m * 128:(m + 1) * 128], lhsT=R[:, c, :], rhs=vb[:, c, :], start=True,
                        stop=True, r=[rk, N(("vb", half))], w=[("pb", bi)])
                P.I("act", "copy", out=us[:, half * 4:half * 4 + 4, :], in_=v3(bk[0:64, :], 128),
                    r=[("pb", bi)], w=[N(("us", half))])
            bW, bkW = nb4()
            for m in range(8):
                P.I("pe", "matmul", bkW[:, m * 64:(m + 1) * 64], lhsT=kbg[:, m, :], rhs=R[:, m, :], start=True,
                    stop=True, r=[rk, N(("kbg", m // 4))], w=[("pb", bW)])
            P.I("dve", "tensor_copy", out=wTs[:], in_=bkW[:, :], r=[("pb", bW)], w=[N("wTs")])
            yield
            for m in range(8):
                cur = (s * 8 + m) % 2
                Sc, Sn = S[h][cur], S[h][1 - cur]
                Sbc, Sbn = Sb[h][cur], Sb[h][1 - cur]
                cs = slice(m * 64, (m + 1) * 64)
                P.I("pe", "matmul", B6[0:64, 128:256], lhsT=wTs[:, cs], rhs=Sbc[:], start=True, stop=True,
                    r=[N("wTs"), ("Sb", h, cur)], w=[K6])
                P.I("dve", "tensor_tensor", out=vnew[:], in0=us[:, m, :], in1=B6[0:64, 128:256], op=ALU.subtract,
                    r=[N(("us", m // 4)), K6], w=[N("vnew")])
                oc = slice((m % 4) * 128, (m % 4 + 1) * 128)
                P.I("pe", "matmul", B7[0:64, oc], lhsT=qd[:, cs], rhs=Sbc[:], start=True, stop=False,
                    r=[N("qd"), ("Sb", h, cur)], w=[K7])
                P.I("pe", "matmul", B7[0:64, oc], lhsT=qkT[:, m, :], rhs=vnew[:], start=False, stop=True,
                    r=[N("qkT"), N("vnew")], w=[K7])
                P.I("pe", "matmul", B6[:, 256:384], lhsT=kdec[:, m, :], rhs=vnew[:], start=True, stop=True,
                    r=[N(("kdec", m // 4)), N("vnew")], w=[K6])
                P.I("dve", "scalar_tensor_tensor", out=Sbn[:], in0=Sc[:], scalar=gtot[:, m:m + 1],
                    in1=B6[:, 256:384], op0=ALU.mult, op1=ALU.add, r=[("S", h, cur), N("gtot"), K6],
                    w=[("Sb", h, 1 - cur)])
                P.I("dve", "scalar_tensor_tensor", out=Sn[:], in0=Sc[:], scalar=gtot[:, m:m + 1], in1=B6[:, 256:384],
                    op0=ALU.mult, op1=ALU.add, r=[("S", h, cur), N("gtot"), K6], w=[("S", h, 1 - cur)])
                if m % 4 == 3:
                    P.I("act", "copy", out=os_[:, m - 3:m + 1, :], in_=v3(B7[0:64, :], 128), r=[K7],
                        w=[N(("os", m // 4))])
                yield
            okeys = [N(("os", 0)), N(("os", 1))]
            kb2 = [N(("kbg", 0)), N(("kbg", 1))]
            vb2 = [N(("vb", 0)), N(("vb", 1))]
            kd2 = [N(("kdec", 0)), N(("kdec", 1))]
            kb2, vb2, kd2 = [N("sq3")], [N("zbt")], [N("y1")]
            P.I("dve", "tensor_tensor", out=sq3[:], in0=os_[:], in1=os_[:], op=ALU.mult, r=okeys, w=kb2)
            P.I("dve", "tensor_reduce", out=ssq[:], in_=sq3[:], axis=AX.X, op=ALU.add, r=kb2, w=[N("ssq")])
            P.I("act", "activation", out=rinv8[:], in_=ssq[:], func=AF.Sqrt, bias=eps64, scale=1.0 / 128,
                r=[N("ssq")], w=[N("rinv8")])
            P.I("dve", "reciprocal", out=rinv8[:], in_=rinv8[:], r=[N("rinv8")], w=[N("rinv8")])
            P.dma("sp", (sid, "g_zbt"), zbt[:],
                  k.zb[t0:t0 + 512, h * 128:(h + 1) * 128].rearrange("(n p) c -> p n c", p=64), w=vb2)
            P.I("dve", "tensor_tensor", out=y1[:], in0=os_[:], in1=bci(rinv8[:], 128), op=ALU.mult,
                r=okeys + [N("rinv8")], w=kd2)
            P.I("pool", "tensor_tensor", out=y1[:], in0=y1[:], in1=gg[:].unsqueeze(1).to_broadcast([64, 8, 128]),
                op=ALU.mult, r=kd2 + ["gg"], w=kd2)
            P.I("pool", "tensor_tensor", out=y1[:], in0=y1[:], in1=zbt[:], op=ALU.mult, r=kd2 + vb2, w=kd2)
            P.dma("pool", (sid, "g_yb"), k.yb[t0:t0 + 512, h * 128:(h + 1) * 128].rearrange("(n p) c -> p n c", p=64),
                  y1[:], r=kd2)
            yield
        return unit

    units = [make_set(0, 6, 7), make_set(1, 4, 5)]
    for s in range(NS):
        t0 = s * 512
        P.dma("sp", "g_bat", bat[:], k.ba[t0:t0 + 512, :].rearrange("(n p) c -> p n c", p=64), w=["bat"])
        P.I("act", "activation", out=beta[:], in_=bat[:, :, 0:8], func=AF.Sigmoid, r=["bat"], w=["beta"])
        P.I("dve", "tensor_tensor", out=xa[:], in0=bat[:, :, 8:16], in1=dtb[:].unsqueeze(1).to_broadcast([64, 8, 8]),
            op=ALU.add, r=["bat", "dtb"], w=["xa"])
        P.I("act", "activation", out=xa[:], in_=xa[:], func=AF.Exp, r=["xa"], w=["xa"])
        P.I("act", "activation", out=xa[:], in_=xa[:], func=AF.Ln, bias=one64, r=["xa"], w=["xa"])
        P.I("dve", "tensor_tensor", out=gall[:], in0=xa[:], in1=negA[:].unsqueeze(1).to_broadcast([64, 8, 8]),
            op=ALU.mult, r=["xa", "negA"], w=["gall"])
        for h0 in range(0, 8, 2):
            gens = [units[0](s, h0), units[1](s, h0 + 1)]
            while gens:
                for g_ in list(gens):
                    try:
                        next(g_)
                    except StopIteration:
                        gens.remove(g_)
    P.end_phase()


def phase_attn(k, l):
    P, SEQ, NT = k.P, k.SEQ, k.NT
    H2 = SEQ // 2
    P.begin_phase()

    def T(name, shape, dt=F32):
        return P.sb(name, shape, dt)

    kT = T("kT", [128, 2, SEQ], BF16)
    V = T("V", [128, NT, 256], BF16)
    kiT = T("kiT", [128, H2], BF16)
    Isc = [T("Isc", [128, SEQ]) for _ in range(2)]
    mb = [T("mb", [128, SEQ], BF16) for _ in range(2)]
    qt = [T("qt", [128, 8, 128], BF16) for _ in range(2)]
    qit = [T("qit", [128, 8, 128], BF16) for _ in range(2)]
    wit = [T("wit", [128, 8]) for _ in range(2)]
    rl = [T("rl", [128, 512]) for _ in range(2)]
    pt = [T("pt", [128, 512], BF16) for _ in range(3)]
    osb = T("osb", [128, 8, 128])
    st8 = [T("st8", [128, 8]) for _ in range(2)]
    rden = T("rden", [128, 8])
    for g in range(2):
        P.dma("sp", ("a_kT", g), kT[:, g, :], k.kT[g, :, :], w=["kT"])
    vv = k.v.rearrange("(t p) c -> p t c", p=128)
    for t0 in range(0, NT, 8):
        P.dma("sp", ("a_V", t0), V[:, t0:t0 + 8, :], vv[:, t0:t0 + 8, :], w=[("V", t0)])
    P.dma("sp", "a_kiT0", kiT[0:64, :], k.kiT[:, 0:H2], w=["kiT"])
    P.dma("sp", "a_kiT1", kiT[64:128, :], k.kiT[:, H2:SEQ], w=["kiT"])
    BI = [k.banks[0], k.banks[1]]
    BS = [k.banks[2], k.banks[3]]
    BO = [k.banks[4], k.banks[5]]
    BD = k.banks[6]
    ci = {"i": 0, "s": 0, "r": 0, "p": 0}
    topk = float(min(TOPK, SEQ // 4))

    def PRE(j):
        s = j % 2
        L = (j + 1) * 128
        Tq = slice(j * 128, (j + 1) * 128)
        I_, M_, S8 = Isc[s], mb[s], st8[s]
        IK, MK = ("I", s), ("mb", s)
        P.dma("sp", ("a_qt", s), qt[s][:], k.qT[:, :, Tq].rearrange("h d t -> d h t"), w=[("qt", s)])
        P.dma("sp", ("a_qit0", s), qit[s][0:64], k.qiT[:, :, Tq].rearrange("h d t -> d h t"), w=[("qit", s)])
        P.dma("sp", ("a_qit1", s), qit[s][64:128], k.qiT[:, :, Tq].rearrange("h d t -> d h t"), w=[("qit", s)])
        P.dma("sp", ("a_wit", s), wit[s][:], k.wi[Tq, :], w=[("wit", s)])
        nblk = (L + 511) // 512
        nk_ = j + 1
        Ld = max(1, int(0.45 * nk_ + 0.5)) * 128 if nk_ >= 2 else L
        n2 = L - Ld
        for kb in range(nblk):
            if kb:
                yield
            w_ = min(512, L - kb * 512)
            cs = slice(kb * 512, kb * 512 + w_)
            if kb * 512 < H2:
                pr, kc = slice(0, 64), slice(kb * 512, kb * 512 + w_)
            else:
                pr, kc = slice(64, 128), slice(kb * 512 - H2, kb * 512 - H2 + w_)
            for h in range(8):
                b = ci["i"] % 2
                ci["i"] += 1
                P.I("pe", "matmul", BI[b][:, 0:w_], lhsT=qit[s][pr, h, :], rhs=kiT[pr, kc], start=True, stop=True,
                    r=[("qit", s), "kiT"], w=[("bi", b)])
                r_ = ci["r"] % 2
                ci["r"] += 1
                P.I("act", "activation", out=rl[r_][:, 0:w_], in_=BI[b][:, 0:w_], func=AF.Relu,
                    r=[("bi", b)], w=[("rl", r_)])
                if h == 0:
                    P.I("dve", "tensor_scalar", out=I_[:, cs], in0=rl[r_][:, 0:w_], scalar1=wit[s][:, 0:1],
                        scalar2=None, op0=ALU.mult, r=[("rl", r_), ("wit", s)], w=[IK])
                else:
                    P.I("dve", "scalar_tensor_tensor", out=I_[:, cs], in0=rl[r_][:, 0:w_], scalar=wit[s][:, h:h + 1],
                        in1=I_[:, cs], op0=ALU.mult, op1=ALU.add, r=[("rl", r_), ("wit", s), IK], w=[IK])
        sk = lambda n: ("st", s, n)
        P.I("dve", "tensor_reduce", out=S8[:, 0:1], in_=I_[:, 0:L], axis=AX.X, op=ALU.max, r=[IK], w=[sk(0)])
        P.I("dve", "tensor_reduce", out=S8[:, 1:2], in_=I_[:, 0:L], axis=AX.X, op=ALU.min, r=[IK], w=[sk(1)])
        P.I("dve", "tensor_tensor", out=S8[:, 2:3], in0=S8[:, 0:1], in1=S8[:, 1:2], op=ALU.subtract,
            r=[sk(0), sk(1)], w=[sk(2)])
        P.I("dve", "tensor_scalar", out=S8[:, 2:3], in0=S8[:, 2:3], scalar1=1e-20, scalar2=None, op0=ALU.add,
            r=[sk(2)], w=[sk(2)])
        P.I("dve", "reciprocal", out=S8[:, 2:3], in_=S8[:, 2:3], r=[sk(2)], w=[sk(2)])
        P.I("dve", "tensor_scalar", out=I_[:, 0:L], in0=I_[:, 0:L], scalar1=S8[:, 1:2], scalar2=S8[:, 2:3],
            op0=ALU.subtract, op1=ALU.mult, r=[IK, sk(1), sk(2)], w=[IK])
        P.I("dve", "tensor_tensor", out=I_[:, L - 128:L], in0=I_[:, L - 128:L], in1=k.tri[:, :], op=ALU.add,
            r=[IK], w=[IK])
        P.I("dve", "memset", S8[:, 3:4], 0.5, w=[sk(3)])
        yield
        for it in range(NBIS):
            wk = 2.0 ** -(it + 2)
            P.I("dve", "tensor_scalar", out=Isc[1 - s][:, 0:Ld], in0=I_[:, 0:Ld], scalar1=S8[:, 3:4], scalar2=0.0,
                op0=ALU.is_ge, op1=ALU.add, accum_out=S8[:, 4:5], r=[IK, sk(3)], w=[("I", 1 - s), sk(4)])
            if n2:
                P.I("act", "activation", out=M_[:, Ld:L], in_=I_[:, Ld:L], func=AF.Sign, scale=-1.0,
                    bias=S8[:, 3:4], accum_out=S8[:, 7:8], r=[IK, sk(3)], w=[MK, sk(7)])
                P.I("dve", "scalar_tensor_tensor", out=S8[:, 5:6], in0=S8[:, 4:5], scalar=2.0, in1=S8[:, 7:8],
                    op0=ALU.mult, op1=ALU.subtract, r=[sk(4), sk(7)], w=[sk(5)])
                P.I("dve", "tensor_scalar", out=S8[:, 5:6], in0=S8[:, 5:6], scalar1=2.0 * topk - n2, scalar2=2.0 * wk,
                    op0=ALU.is_ge, op1=ALU.mult, r=[sk(5)], w=[sk(5)])
            else:
                P.I("dve", "tensor_scalar", out=S8[:, 5:6], in0=S8[:, 4:5], scalar1=topk, scalar2=2.0 * wk,
                    op0=ALU.is_ge, op1=ALU.mult, r=[sk(4)], w=[sk(5)])
            P.I("dve", "scalar_tensor_tensor", out=S8[:, 3:4], in0=S8[:, 5:6], scalar=-wk, in1=S8[:, 3:4],
                op0=ALU.add, op1=ALU.add, r=[sk(5), sk(3)], w=[sk(3)])
            yield
        P.I("dve", "tensor_scalar", out=S8[:, 6:7], in0=S8[:, 3:4], scalar1=-(2.0 ** -(NBIS + 1)), scalar2=None,
            op0=ALU.add, r=[sk(3)], w=[sk(6)])
        P.I("dve", "tensor_scalar", out=M_[:, 0:L], in0=I_[:, 0:L], scalar1=S8[:, 6:7], scalar2=-30000.0,
            op0=ALU.is_lt, op1=ALU.mult, r=[IK, sk(6)], w=[MK])
        yield

    def ATT(j):
        s = j % 2
        nk = j + 1
        Tq = slice(j * 128, (j + 1) * 128)
        M_, MK = mb[s], ("mb", s)
        for g in range(2):
            P.I("pe", "matmul", BO[g][:, :], lhsT=k.zeros_bf[:, 0:128], rhs=k.zeros_bf[:, :], start=True, stop=False,
                w=[("bo", g)])
        P.I("pe", "matmul", BD[:, 0:8], lhsT=k.zeros_bf[:, 0:128], rhs=k.zeros_bf[:, 0:8], start=True, stop=False,
            w=["bd"])
        tiles = [(kt, g) for kt in range(nk) for g in range(2)]
        bsel = {}

        def emit_S(i):
            kt, g = tiles[i]
            ks = slice(kt * 128, (kt + 1) * 128)
            b = ci["s"] % 2
            ci["s"] += 1
            bsel[i] = b
            P.I("pe", "matmul", BS[b][:, :], lhsT=kT[:, g, ks],
                rhs=qt[s][:, 4 * g:4 * g + 4, :].rearrange("p h t -> p (h t)"), start=True, stop=False,
                r=["kT", ("qt", s)], w=[("bs", b)])
            P.I("pe", "matmul", BS[b][:, :], lhsT=M_[:, ks], rhs=k.i4[:, :], start=False, stop=True,
                r=[MK], w=[("bs", b)])

        emit_S(0)
        for i, (kt, g) in enumerate(tiles):
            if i + 1 < len(tiles):
                emit_S(i + 1)
            b = bsel[i]
            last = (kt == nk - 1)
            p_ = ci["p"] % 3
            ci["p"] += 1
            P.I("act", "activation", out=pt[p_][:, :], in_=BS[b][:, :], func=AF.Exp, r=[("bs", b)], w=[("pt", p_)])
            for hh in range(4):
                P.I("pe", "matmul", BO[g][:, hh * 128:(hh + 1) * 128], lhsT=pt[p_][:, hh * 128:(hh + 1) * 128],
                    rhs=V[:, kt, g * 128:(g + 1) * 128], start=False, stop=last,
                    r=[("pt", p_), ("V", (kt // 8) * 8)], w=[("bo", g)])
                P.I("pe", "matmul", BD[:, g * 4 + hh:g * 4 + hh + 1], lhsT=pt[p_][:, hh * 128:(hh + 1) * 128],
                    rhs=k.ones_bf[:, 0:1], start=False, stop=last, r=[("pt", p_)], w=["bd"])
            yield
        P.I("dve", "reciprocal", out=rden[:, :], in_=BD[:, 0:8], r=["bd"], w=["rden"])
        for g in range(2):
            P.I("dve", "tensor_tensor", out=osb[:, 4 * g:4 * g + 4, :],
                in0=BO[g][:, :].rearrange("p (h d) -> p h d", d=128),
                in1=rden[:, 4 * g:4 * g + 4].unsqueeze(2).to_broadcast([128, 4, 128]), op=ALU.mult,
                r=[("bo", g), "rden"], w=["osb"])
        P.dma("pool", "a_oa", k.oa[Tq, :], osb[:].rearrange("p h d -> p (h d)"), r=["osb"])
        yield

    def drain(g_):
        for _ in g_:
            pass

    def merge(ga, na, gb, nb_):
        n = max(na, nb_)
        da = db = 0
        for i in range(1, n + 1):
            while da * n < i * na:
                next(ga, None)
                da += 1
            while db * n < i * nb_:
                next(gb, None)
                db += 1
        drain(ga)
        drain(gb)

    drain(PRE(0))
    for j in range(NT):
        if j + 1 < NT:
            npre = (((j + 2) * 128 + 511) // 512) + NBIS + 1
            natt = 2 * (j + 1) + 1
            merge(PRE(j + 1), npre, ATT(j), natt)
        else:
            drain(ATT(j))
    P.end_phase()


def phase_C(k, l, src, last):
    P, NT = k.P, k.NT
    P.begin_phase()

    def T(name, shape, dt=F32):
        return P.sb(name, shape, dt)

    wo = T("wo", [128, 8, D], BF16)
    wov = k.wout[l].rearrange("(kc p) c -> p kc c", p=128)
    wof = [T("wof", [128, D]) for _ in range(2)]
    for kc in range(8):
        f = kc % 2
        P.dma("sp", ("c_wof", f), wof[f][:], wov[:, kc, :], w=[("wof", f)])
        P.I("pool", "tensor_copy", out=wo[:, kc, :], in_=wof[f][:], r=[("wof", f)], w=[("wo", kc)])
    fg = T("fg", [128, D])
    if last:
        P.dma("sp", "c_fg", fg[:], k.fgain.partition_broadcast(128), w=["fg"])
    oa = [T("oa", [128, D]) for _ in range(2)]
    za = [T("za", [128, D]) for _ in range(2)]
    gt = [T("gt", [128, 2 * D]) for _ in range(2)]
    yb = [T("yb", [128, D]) for _ in range(2)]
    xt = [T("xt", [128, D]) for _ in range(2)]
    mx = [T("mx", [128, D], BF16) for _ in range(2)]
    mT = [T("mT", [128, 8, 128], BF16) for _ in range(2)]
    xn = [T("xn", [128, D]) for _ in range(2)]
    junk = T("junk", [128, D], BF16)
    ss = T("ss", [128, 2])
    wk = [("wo", kc) for kc in range(8)]
    for i in range(NT):
        s = i % 2
        R = slice(i * 128, (i + 1) * 128)
        P.dma("sp", ("c_oa", s), oa[s][:], k.oa[R, :], w=[("oa", s)])
        P.dma("sp", ("c_za", s), za[s][:], k.za[R, :], w=[("za", s)])
        P.dma("sp", ("c_gt", s), gt[s][:], k.gt[R, :], w=[("gt", s)])
        P.dma("sp", ("c_yb", s), yb[s][:], k.yb[R, :], w=[("yb", s)])
        P.dma("sp", ("c_xt", s), xt[s][:], src[R, :], w=[("xt", s)])
        P.I("pool", "tensor_tensor", out=oa[s][:], in0=oa[s][:], in1=za[s][:], op=ALU.mult,
            r=[("oa", s), ("za", s)], w=[("oa", s)])
        P.I("pool", "tensor_tensor", out=oa[s][:], in0=oa[s][:], in1=gt[s][:, 0:D], op=ALU.mult,
            r=[("oa", s), ("gt", s)], w=[("oa", s)])
        P.I("dve", "tensor_tensor", out=yb[s][:], in0=yb[s][:], in1=gt[s][:, D:2 * D], op=ALU.mult,
            r=[("yb", s), ("gt", s)], w=[("yb", s)])
        P.I("dve", "tensor_tensor", out=mx[s][:], in0=oa[s][:], in1=yb[s][:], op=ALU.add,
            r=[("oa", s), ("yb", s)], w=[("mx", s)])
        bi, bank = next_bank(k)
        pT = bank[:].bitcast(BF16).rearrange("p (c t) -> p c t", t=128)
        for kc in range(8):
            P.I("pe", "transpose", out=pT[:, kc, :], in_=mx[s][:, kc * 128:(kc + 1) * 128], identity=k.ident_bf[:],
                r=[("mx", s)], w=[("pb", bi)])
        P.I("act", "copy", out=mT[s][:], in_=pT, r=[("pb", bi)], w=[("mT", s)])
        for half in range(2):
            bo, bko = next_bank(k)
            for kc in range(8):
                P.I("pe", "matmul", bko[:, :], lhsT=mT[s][:, kc, :], rhs=wo[:, kc, half * 512:(half + 1) * 512],
                    start=(kc == 0), stop=(kc == 7), r=[("mT", s), wk[kc]], w=[("pb", bo)])
            P.I("dve", "tensor_tensor", out=xn[s][:, half * 512:(half + 1) * 512], in0=xt[s][:, half * 512:(half + 1) * 512],
                in1=bko[:, :], op=ALU.add, r=[("xt", s), ("pb", bo)], w=[("xn", s, half)])
        xk = [("xn", s, 0), ("xn", s, 1)]
        if not last:
            P.dma("pool", ("c_st", s), k.x1[R, :], xn[s][:], r=xk)
        else:
            P.I("act", "activation", out=junk[:], in_=xn[s][:], func=AF.Square, accum_out=ss[:, s:s + 1],
                r=xk, w=["junk", ("ss", s)])
            P.I("act", "activation", out=ss[:, s:s + 1], in_=ss[:, s:s + 1], func=AF.Sqrt, bias=k.cst[:, 4:5],
                scale=1.0 / D, r=[("ss", s)], w=[("ss", s)])
            P.I("dve", "reciprocal", out=ss[:, s:s + 1], in_=ss[:, s:s + 1], r=[("ss", s)], w=[("ss", s)])
            P.I("dve", "scalar_tensor_tensor", out=xn[s][:], in0=xn[s][:], scalar=ss[:, s:s + 1], in1=fg[:],
                op0=ALU.mult, op1=ALU.mult, r=xk + [("ss", s), "fg"], w=xk)
            P.dma("pool", ("c_st", s), k.out[R, :], xn[s][:], r=xk)
    P.end_phase()


_CACHE = {}
SEQ_FULL = 8192
NLAYERS = 2
NCORES = 2


def _in_maps(inp, NL):
    consts = host_consts()
    per_layer = {}
    for l in range(NL):
        per_layer["wfm%d" % l] = host_w_fm(inp["w_in"], l)
        per_layer["wtm%d" % l] = host_w_tm(inp["w_in"], l)
        per_layer["wout%d" % l] = np.ascontiguousarray(inp["w_out"][l], dtype=np.float32)
        per_layer["ng%d" % l] = np.ascontiguousarray(inp["norm_gain"][l][None, :], dtype=np.float32)
        cw = np.asarray(inp["conv_w"][l], dtype=np.float32)
        per_layer["convw%d" % l] = np.ascontiguousarray(cw.reshape(4, 24, 128).transpose(2, 1, 0))
        per_layer["gdng%d" % l] = np.ascontiguousarray(inp["gdn_norm_gain"][l][None, :], dtype=np.float32)
        per_layer["alog%d" % l] = np.ascontiguousarray(inp["a_log"][l][None, :], dtype=np.float32)
        per_layer["dtb%d" % l] = np.ascontiguousarray(inp["dt_bias"][l][None, :], dtype=np.float32)
        g = np.asarray(inp["idx_k_gain"][l], dtype=np.float32)
        gs = np.zeros(64, np.float32)
        gs[:8] = g[8:16]
        gs[8:16] = g[0:8]
        per_layer["ikg%d" % l] = np.ascontiguousarray(np.stack([g, gs], 1))
    maps = []
    for c in range(NCORES):
        b = c % 2
        m = {"x": np.ascontiguousarray(inp["x"][b], dtype=np.float32),
             "pos": np.ascontiguousarray(inp["positions"][b:b + 1]).astype(np.int32),
             "fgain": np.ascontiguousarray(np.asarray(inp["final_gain"], dtype=np.float32)[None, :])}
        m.update(per_layer)
        m.update(consts)
        maps.append(m)
    return maps


def kernel(x, positions, norm_gain, w_in, conv_w, a_log, dt_bias, gdn_norm_gain, idx_k_gain, w_out, final_gain):
    inp = {"x": np.asarray(x), "positions": np.asarray(positions), "norm_gain": np.asarray(norm_gain),
           "w_in": np.asarray(w_in), "conv_w": np.asarray(conv_w), "a_log": np.asarray(a_log),
           "dt_bias": np.asarray(dt_bias), "gdn_norm_gain": np.asarray(gdn_norm_gain),
           "idx_k_gain": np.asarray(idx_k_gain), "w_out": np.asarray(w_out), "final_gain": np.asarray(final_gain)}
    B, S, _ = inp["x"].shape
    NL = inp["w_in"].shape[0]
    assert B == 2
    nc = build(S, NL)
    maps = _in_maps(inp, NL)
    res = run_bass_kernel_spmd(nc, maps, core_ids=list(range(NCORES)))
    out = np.stack([np.asarray(res.results[b]["out"], dtype=np.float32) for b in range(2)], axis=0)
    return out
```

```python
import numpy as np
from contextlib import ExitStack
import concourse.bass as bass
import concourse.mybir as mybir

F32 = mybir.dt.float32
BF16 = mybir.dt.bfloat16
I32 = mybir.dt.int32
ALU = mybir.AluOpType
AF = mybir.ActivationFunctionType
AX = mybir.AxisListType


class _Op:
    __slots__ = ("idx", "eng", "fn", "deps", "dma", "dseq", "sig", "signals")

    def __init__(self, idx, eng, fn, deps, dma):
        self.idx = idx
        self.eng = eng
        self.fn = fn
        self.deps = deps
        self.dma = dma
        self.dseq = 0
        self.sig = 0
        self.signals = False


_PSUM_T = ("pb", "bi", "bs", "bo")
_PSUM_S = ("b6", "b7", "bd")


def _is_psum_key(x):
    return (isinstance(x, tuple) and x[0] in _PSUM_T) or (isinstance(x, str) and x in _PSUM_S)


class Prog:
    ENGS = ("pe", "act", "dve", "pool", "sp")

    def __init__(self, nc):
        self.nc = nc
        self.ops = []
        self.flushed = 0
        self.lastw = {}
        self.readers = {}
        self.dma_prev = {}
        self.dma_cnt = {}
        self.last_eng = {}
        self.es = ExitStack()
        self.pes = None
        self._n = 0
        self.esem = {e: self.es.enter_context(nc.semaphore("s_" + e)) for e in self.ENGS}
        self.dsem = {}
        self.cnt = {e: 0 for e in self.ENGS}
        self.waited = {e: {} for e in self.ENGS}

    def _nm(self, name):
        self._n += 1
        return "%s_%d" % (name, self._n)

    def sb(self, name, shape, dt, persist=False):
        st = self.es if (persist or self.pes is None) else self.pes
        return st.enter_context(self.nc.sbuf_tensor(self._nm(name), list(shape), dt))

    def ps(self, name, shape, dt):
        return self.es.enter_context(self.nc.psum_tensor(self._nm(name), list(shape), dt))

    def begin_phase(self):
        self.pes = ExitStack()

    def end_phase(self):
        self.barrier()
        self.flush()
        self.pes.close()
        self.pes = None

    def op(self, eng, fn, r=(), w=(), dma=None, deps=None):
        idx = len(self.ops)
        deps = set(deps) if deps is not None else set()
        pr = [x for x in r if _is_psum_key(x)]
        if pr:
            r = [x for x in r if not _is_psum_key(x)]
            w = list(w) + pr
        for k in r:
            if k in self.lastw:
                deps.add(self.lastw[k])
        for k in w:
            if k in self.lastw:
                deps.add(self.lastw[k])
            for q in self.readers.get(k, ()):
                deps.add(q)
        if dma is not None:
            if dma in self.dma_prev:
                deps.add(self.dma_prev[dma])
            self.dma_prev[dma] = idx
        o = _Op(idx, eng, fn, deps, dma)
        if dma is not None:
            self.dma_cnt[dma] = self.dma_cnt.get(dma, 0) + 1
            o.dseq = self.dma_cnt[dma]
        else:
            self.last_eng[eng] = idx
        self.ops.append(o)
        for k in w:
            self.lastw[k] = idx
            self.readers[k] = []
        for k in r:
            lst = self.readers.setdefault(k, [])
            if dma is None:
                lst[:] = [q for q in lst if not (self.ops[q].dma is None and self.ops[q].eng == eng)]
            lst.append(idx)
        return idx

    def pe(self, fn, r=(), w=()):
        return self.op("pe", fn, r, w)

    def act(self, fn, r=(), w=()):
        return self.op("act", fn, r, w)

    def dve(self, fn, r=(), w=()):
        return self.op("dve", fn, r, w)

    def pool(self, fn, r=(), w=()):
        return self.op("pool", fn, r, w)

    def I(self, eng, meth, *args, r=(), w=(), **kw):
        return self.op(eng, lambda e: getattr(e, meth)(*args, **kw), r, w)

    def dma(self, q, key, out, in_, r=(), w=(), **kw):
        return self.op(q, lambda e: e.dma_start(out=out, in_=in_, **kw), r, w, dma=(q, key))

    def barrier(self):
        deps = set(i for i in self.last_eng.values() if i >= self.flushed) | set(self.dma_prev.values())
        for e in self.ENGS:
            self.op(e, lambda eng: eng.nop(), deps=deps)
        self.lastw.clear()
        self.readers.clear()

    def flush(self):
        nc = self.nc
        ops = self.ops
        batch = ops[self.flushed:]
        for o in batch:
            for d in o.deps:
                p = ops[d]
                if p.dma is None:
                    if p.eng == "pe" and o.eng == "pe" and o.dma is None:
                        continue
                    assert d >= self.flushed, "dependency on already-flushed compute op"
                    p.signals = True
        for o in batch:
            if o.dma is None and o.signals:
                self.cnt[o.eng] += 1
                o.sig = self.cnt[o.eng]
            if o.dma is not None and o.dma not in self.dsem:
                self.dsem[o.dma] = self.es.enter_context(nc.semaphore(self._nm("d")))
        esem, dsem = self.esem, self.dsem
        per = {e: [o for o in batch if o.eng == e] for e in self.ENGS}

        def run(ename, eng):
            waited = self.waited[ename]
            for o in per[ename]:
                need = {}
                for d in o.deps:
                    p = ops[d]
                    if p.dma is not None:
                        s, v = dsem[p.dma], 16 * p.dseq
                    else:
                        if p.eng == "pe" and ename == "pe" and o.dma is None:
                            continue
                        s, v = esem[p.eng], p.sig
                    if waited.get(s.num, 0) >= v:
                        continue
                    if need.get(s.num, (None, 0))[1] < v:
                        need[s.num] = (s, v)
                for s, v in need.values():
                    eng.wait_ge(s, v)
                    waited[s.num] = v
                ins = o.fn(eng)
                if o.dma is not None:
                    ins.then_inc(dsem[o.dma], 16)
                elif o.signals:
                    ins.then_inc(esem[ename], 1)

        with nc.Block() as block:
            @block.tensor
            def _(e):
                run("pe", e)

            @block.scalar
            def _(e):
                run("act", e)

            @block.vector
            def _(e):
                run("dve", e)

            @block.gpsimd
            def _(e):
                run("pool", e)

            @block.sync
            def _(e):
                run("sp", e)
        self.flushed = len(ops)

    def finish(self):
        if self.flushed < len(self.ops):
            self.barrier()
            self.flush()
        self.es.close()

import math
import ml_dtypes
from concourse.bass_utils import run_bass_kernel_spmd
import math
import numpy as np
import ml_dtypes

D = 1024
EPS = 1e-6
THETA = 500000.0
NBIS = 13
TOPK = 256
PI = math.pi
NEG = -1.0e30


def fm_blocks():
    b = []
    for h in range(8):
        b.append(("qa", h, 128, 32))
    for g in range(2):
        b.append(("ka", g, 128, 32))
    for h in range(8):
        b.append(("qi", h, 64, 16))
    b.append(("ki", 0, 64, 16))
    for i in range(24):
        b.append(("gd", i, 128, 0))
    return b


FM = fm_blocks()
FM_GROUPS = [FM[0:19], FM[19:35], FM[35:43]]
NFM = sum(b[2] + b[3] for b in FM)
TM = ([("v", 1280, 256), ("za", 1536, 512), ("za", 2048, 512), ("wi", 3136, 8)] +
      [("gt", 7256 + 512 * i, 512) for i in range(4)] +
      [("zb", 6216, 512), ("zb", 6728, 512), ("ba", 7240, 16)])
TM_GROUPS = [TM[0:6], TM[6:11]]
NTM = sum(c[2] for c in TM)
WG_MAX = 2560


def host_w_fm(w_in, l):
    W = w_in[l]
    cols = []
    for kind, i, nm, ns in FM:
        if kind == "qa":
            c0 = i * 128
            cols += [W[:, c0:c0 + 128], W[:, c0 + 16:c0 + 32], W[:, c0:c0 + 16]]
        elif kind == "ka":
            c0 = 1024 + i * 128
            cols += [W[:, c0:c0 + 128], W[:, c0 + 16:c0 + 32], W[:, c0:c0 + 16]]
        elif kind == "qi":
            c0 = 2560 + i * 64
            cols += [W[:, c0:c0 + 64], W[:, c0 + 8:c0 + 16], W[:, c0:c0 + 8]]
        elif kind == "ki":
            c0 = 3072
            cols += [W[:, c0:c0 + 64], W[:, c0 + 8:c0 + 16], W[:, c0:c0 + 8]]
        else:
            c0 = 3144 + i * 128
            cols += [W[:, c0:c0 + 128]]
    return np.ascontiguousarray(np.concatenate(cols, axis=1))


def host_w_tm(w_in, l):
    W = w_in[l]
    return np.ascontiguousarray(np.concatenate([W[:, c0:c0 + n] for _, c0, n in TM], axis=1))


def host_consts():
    c = {}
    c["ident_bf"] = np.eye(128, dtype=ml_dtypes.bfloat16)
    c["ident_f"] = np.eye(128, dtype=np.float32)
    c["i4"] = np.concatenate([np.eye(128)] * 4, axis=1).astype(ml_dtypes.bfloat16)
    q = np.arange(128)[:, None]
    k = np.arange(128)[None, :]
    c["tri_bias"] = np.where(k <= q, 0.0, NEG).astype(np.float32)
    j = np.arange(64)[:, None]
    i = np.arange(64)[None, :]
    m = np.zeros((64, 5, 64), np.float32)
    m[:, 0, :] = (j <= i)
    m[:, 1, :] = (i < j)
    m[:, 2, :] = np.where(i >= j, 0.0, NEG)
    m[:, 3, :] = np.where(i < j, 0.0, NEG)
    m[:, 4, :] = np.eye(64)
    c["m64"] = m
    cs = np.zeros((128, 8), np.float32)
    r = np.arange(32)
    cs[:32, 0] = THETA ** (-(2.0 * (r % 16)) / 32.0)
    r16 = np.arange(16)
    cs[:16, 1] = THETA ** (-(2.0 * (r16 % 8)) / 16.0)
    cs[:32, 2] = np.where(r < 16, -1.0, 1.0)
    cs[:16, 3] = np.where(r16 < 8, -1.0, 1.0)
    cs[:, 4] = EPS
    cs[:, 5] = 1.0
    c["cst"] = cs
    return c


class K:
    pass


def build(SEQ, NL, dbg=(), phases=('tab', 'A1', 'A2', 'gdn', 'attn', 'C')):
    nc = bass.Bass("TRN2", target_bir_lowering=False)
    k = K()
    k.nc, k.SEQ, k.NL = nc, SEQ, NL
    NT = SEQ // 128
    NB = SEQ // 512
    k.NT, k.NB = NT, NB

    def inp(name, shape, dt=F32):
        return nc.dram_tensor(name, list(shape), dt, kind="ExternalInput").ap()

    def scr(name, shape, dt=F32):
        kind = "ExternalOutput" if name in dbg else "Internal"
        return nc.dram_tensor(name, list(shape), dt, kind=kind).ap()

    k.x = inp("x", [SEQ, D])
    k.pos = inp("pos", [1, SEQ], I32)
    k.fgain = inp("fgain", [1, D])
    k.wfm = [inp("wfm%d" % l, [D, NFM]) for l in range(NL)]
    k.wtm = [inp("wtm%d" % l, [D, NTM]) for l in range(NL)]
    k.wout = [inp("wout%d" % l, [D, D]) for l in range(NL)]
    k.ng = [inp("ng%d" % l, [1, D]) for l in range(NL)]
    k.convw = [inp("convw%d" % l, [128, 24, 4]) for l in range(NL)]
    k.gdng = [inp("gdng%d" % l, [1, 128]) for l in range(NL)]
    k.alog = [inp("alog%d" % l, [1, 8]) for l in range(NL)]
    k.dtb = [inp("dtb%d" % l, [1, 8]) for l in range(NL)]
    k.ikg = [inp("ikg%d" % l, [64, 2]) for l in range(NL)]
    k.c_ident_bf = inp("ident_bf", [128, 128], BF16)
    k.c_ident_f = inp("ident_f", [128, 128])
    k.c_i4 = inp("i4", [128, 512], BF16)
    k.c_tri = inp("tri_bias", [128, 128])
    k.c_m64 = inp("m64", [64, 5, 64])
    k.c_cst = inp("cst", [128, 8])
    k.out = nc.dram_tensor("out", [SEQ, D], F32, kind="ExternalOutput").ap()

    k.tab = scr("tab", [4, 32, SEQ])
    k.hT = scr("hT", [128, 8, SEQ], BF16)
    k.qT = scr("qT", [8, 128, SEQ], BF16)
    k.kT = scr("kT", [2, 128, SEQ], BF16)
    k.qiT = scr("qiT", [8, 64, SEQ], BF16)
    k.kiT = scr("kiT", [64, SEQ], BF16)
    k.gT = scr("gT", [24, 128, SEQ])
    k.v = scr("v", [SEQ, 256], BF16)
    k.za = scr("za", [SEQ, D])
    k.wi = scr("wi", [SEQ, 8])
    k.gt = scr("gt", [SEQ, 2 * D])
    k.zb = scr("zb", [SEQ, D])
    k.ba = scr("ba", [SEQ, 16])
    k.yb = scr("yb", [SEQ, D])
    k.oa = scr("oa", [SEQ, D])
    k.x1 = scr("x1", [SEQ, D])

    P = Prog(nc)
    k.P = P
    k.banks = [P.ps("bank", [128, 512], F32) for _ in range(8)]
    k.bi = 0

    k.ident_bf = P.sb("ident_bf", [128, 128], BF16, persist=True)
    k.ident_f = P.sb("ident_f", [128, 128], F32, persist=True)
    k.i4 = P.sb("i4", [128, 512], BF16, persist=True)
    k.tri = P.sb("tri", [128, 128], F32, persist=True)
    k.m64 = P.sb("m64", [64, 5, 64], F32, persist=True)
    k.cst = P.sb("cst", [128, 8], F32, persist=True)
    k.ones_f = P.sb("ones_f", [128, 128], F32, persist=True)
    k.ones_bf = P.sb("ones_bf", [128, 8], BF16, persist=True)
    k.zeros_bf = P.sb("zeros_bf", [128, 512], BF16, persist=True)
    k.ones_b = P.sb("ones_b", [128, 128], BF16, persist=True)
    k.su_b = P.sb("su_b", [64, 64], BF16, persist=True)
    P.begin_phase()
    for t, src in ((k.ident_bf, k.c_ident_bf), (k.ident_f, k.c_ident_f), (k.i4, k.c_i4), (k.tri, k.c_tri),
                   (k.m64, k.c_m64), (k.cst, k.c_cst)):
        P.dma("sp", ("cload", id(t)), t[:], src, w=[("cst", "m64")] if t is k.m64 else [])
    P.pool(lambda e: e.memset(k.ones_f[:], 1.0))
    P.pool(lambda e: e.memset(k.ones_bf[:], 1.0))
    P.pool(lambda e: e.memset(k.zeros_bf[:], 0.0))
    P.pool(lambda e: e.memset(k.ones_b[:], 1.0))
    P.I("dve", "tensor_copy", out=k.su_b[:], in_=k.m64[:, 1, :], r=[("cst", "m64")], w=["su_b"])
    P.end_phase()

    if 'tab' in phases:
        phase_tables(k)
    for l in range(NL):
        src = k.x if l == 0 else k.x1
        if 'A1' in phases:
            phase_A1(k, l, src)
        if 'A2' in phases:
            phase_A2(k, l)
        if 'gdn' in phases:
            phase_gdn(k, l)
        if 'attn' in phases:
            phase_attn(k, l)
        if 'C' in phases:
            phase_C(k, l, src, last=(l == NL - 1))
    P.finish()
    return nc


def next_bank(k):
    i = k.bi % 8
    k.bi += 1
    return i, k.banks[i]


def phase_tables(k):
    P, SEQ = k.P, k.SEQ
    P.begin_phase()
    CH = min(SEQ, 2048)
    posi = P.sb("posi", [32, CH], I32)
    posf = P.sb("posf", [32, CH], F32)
    ang = P.sb("ang", [32, CH], F32)
    u = P.sb("u", [32, CH], F32)
    ki = P.sb("ki", [32, CH], I32)
    kf = P.sb("kf", [32, CH], F32)
    r = P.sb("r", [32, CH], F32)
    w = P.sb("w", [32, CH], F32)
    res = P.sb("res", [32, CH], F32)
    C1 = float(np.float32(2 * PI))
    C2 = float(2 * PI - C1)
    LIM = 3.14159
    for c0 in range(0, SEQ, CH):
        P.dma("sp", "posi", posi[:], k.pos[:, c0:c0 + CH].partition_broadcast(32), w=["posi"])
        P.dve(lambda e: e.tensor_copy(out=posf[:], in_=posi[:]), r=["posi"], w=["posf"])
        for ti, (rows, fcol, scol, shift) in enumerate(((32, 0, None, PI / 2), (32, 0, 2, 0.0),
                                                        (16, 1, None, PI / 2), (16, 1, 3, 0.0))):
            R = slice(0, rows)
            P.dve(lambda e, R=R, fcol=fcol, shift=shift: e.tensor_scalar(
                out=ang[R, :], in0=posf[R, :], scalar1=k.cst[R, fcol:fcol + 1], scalar2=shift,
                op0=ALU.mult, op1=ALU.add), r=["posf"], w=["ang"])
            P.dve(lambda e, R=R: e.tensor_scalar(out=u[R, :], in0=ang[R, :], scalar1=1.0 / (2 * PI), scalar2=None,
                                                 op0=ALU.mult), r=["ang"], w=["u"])
            P.dve(lambda e, R=R: e.tensor_copy(out=ki[R, :], in_=u[R, :]), r=["u"], w=["ki"])
            P.dve(lambda e, R=R: e.tensor_copy(out=kf[R, :], in_=ki[R, :]), r=["ki"], w=["kf"])
            P.dve(lambda e, R=R: e.scalar_tensor_tensor(out=r[R, :], in0=kf[R, :], scalar=-C1, in1=ang[R, :],
                                                        op0=ALU.mult, op1=ALU.add), r=["kf", "ang"], w=["r"])
            P.dve(lambda e, R=R: e.scalar_tensor_tensor(out=r[R, :], in0=kf[R, :], scalar=-C2, in1=r[R, :],
                                                        op0=ALU.mult, op1=ALU.add), r=["kf", "r"], w=["r"])
            P.dve(lambda e, R=R: e.tensor_scalar(out=w[R, :], in0=r[R, :], scalar1=PI, scalar2=-2 * PI,
                                                 op0=ALU.is_gt, op1=ALU.mult), r=["r"], w=["w"])
            P.dve(lambda e, R=R: e.tensor_tensor(out=r[R, :], in0=r[R, :], in1=w[R, :], op=ALU.add),
                  r=["r", "w"], w=["r"])
            P.dve(lambda e, R=R: e.tensor_scalar(out=w[R, :], in0=r[R, :], scalar1=-PI, scalar2=2 * PI,
                                                 op0=ALU.is_lt, op1=ALU.mult), r=["r"], w=["w"])
            P.dve(lambda e, R=R: e.tensor_tensor(out=r[R, :], in0=r[R, :], in1=w[R, :], op=ALU.add),
                  r=["r", "w"], w=["r"])
            P.dve(lambda e, R=R: e.tensor_scalar(out=r[R, :], in0=r[R, :], scalar1=-LIM, scalar2=LIM,
                                                 op0=ALU.max, op1=ALU.min), r=["r"], w=["r"])
            if scol is None:
                P.act(lambda e, R=R: e.activation(out=res[R, :], in_=r[R, :], func=AF.Sin), r=["r"], w=["res"])
            else:
                P.act(lambda e, R=R, scol=scol: e.activation(out=res[R, :], in_=r[R, :], func=AF.Sin,
                                                             scale=k.cst[R, scol:scol + 1]), r=["r"], w=["res"])
            P.dma("sp", "tabst", k.tab[ti, 0:rows, c0:c0 + CH], res[R, :], r=["res"])
    P.end_phase()


def phase_A1(k, l, src):
    P, NT = k.P, k.NT
    P.begin_phase()
    gain_bc = P.sb("gain_bc", [128, D], F32)
    P.dma("sp", "gain", gain_bc[:], k.ng[l].partition_broadcast(128), w=["gain"])
    xt = [P.sb("xt", [128, D], F32) for _ in range(2)]
    junk = P.sb("junk", [128, D], BF16)
    ss = P.sb("ss", [128, NT], F32)
    rstd = P.sb("rstd", [128, NT], F32)
    hb = [P.sb("hb", [128, D], BF16) for _ in range(2)]
    ht = [P.sb("ht", [128, 8, 128], BF16) for _ in range(2)]
    for i in range(NT):
        s = i % 2
        P.dma("sp", ("xt", s), xt[s][:], src[i * 128:(i + 1) * 128, :], w=[("xt", s)])
        P.act(lambda e, s=s, i=i: e.activation(out=junk[:], in_=xt[s][:], func=AF.Square,
                                               accum_out=ss[:, i:i + 1]),
              r=[("xt", s)], w=["junk", ("ss", i)])
        P.act(lambda e, i=i: e.activation(out=rstd[:, i:i + 1], in_=ss[:, i:i + 1], func=AF.Sqrt,
                                          bias=k.cst[:, 4:5], scale=1.0 / D),
              r=[("ss", i)], w=[("rstd", i)])
        P.dve(lambda e, i=i: e.reciprocal(out=rstd[:, i:i + 1], in_=rstd[:, i:i + 1]),
              r=[("rstd", i)], w=[("rstd", i)])
        P.dve(lambda e, s=s, i=i: e.scalar_tensor_tensor(out=hb[s][:], in0=xt[s][:], scalar=rstd[:, i:i + 1],
                                                         in1=gain_bc[:], op0=ALU.mult, op1=ALU.mult),
              r=[("xt", s), ("rstd", i), "gain"], w=[("hb", s)])
        bi, bank = next_bank(k)
        pT = bank[:].bitcast(BF16).rearrange("p (c t) -> p c t", t=128)
        for kc in range(8):
            P.pe(lambda e, s=s, kc=kc, pT=pT: e.transpose(out=pT[:, kc, :], in_=hb[s][:, kc * 128:(kc + 1) * 128],
                                                          identity=k.ident_bf[:]),
                 r=[("hb", s)], w=[("pb", bi)])
        P.act(lambda e, s=s, pT=pT: e.copy(out=ht[s][:], in_=pT), r=[("pb", bi)], w=[("ht", s)])
        P.dma("pool", ("hts", s), k.hT[:, :, i * 128:(i + 1) * 128], ht[s][:], r=[("ht", s)])
    P.end_phase()


def phase_A2(k, l):
    P, NB, SEQ = k.P, k.NB, k.SEQ
    IDX_SCALE = (8 ** -0.5) * (64 ** -0.5)
    P.begin_phase()
    wb = [P.sb("wb", [128, 8, WG_MAX], BF16) for _ in range(2)]
    wf = [P.sb("wf", [128, WG_MAX], F32) for _ in range(2)]
    hb = [P.sb("hblk", [128, 8, 512], BF16) for _ in range(2)]
    ca = [P.sb("ca", [32, 512], F32) for _ in range(2)]
    sa = [P.sb("sa", [32, 512], F32) for _ in range(2)]
    cas = [P.sb("cas", [32, 512], F32) for _ in range(2)]
    sas = [P.sb("sas", [32, 512], F32) for _ in range(2)]
    ci = [P.sb("ci", [16, 512], F32) for _ in range(2)]
    si = [P.sb("si", [16, 512], F32) for _ in range(2)]
    t1 = P.sb("t1", [32, 512], F32)
    t2 = P.sb("t2", [32, 512], F32)
    obf = [P.sb("obf", [128, 512], BF16) for _ in range(3)]
    off = [P.sb("off", [128, 512], F32) for _ in range(3)]
    sq = P.sb("sq", [64, 512], F32)
    rinv = P.sb("rinv", [64, 512], F32)
    kn = P.sb("kn", [64, 512], F32)
    ksw = P.sb("ksw", [16, 512], F32)
    ikg = P.sb("ikg", [64, 2], F32)
    P.dma("sp", "ikg", ikg[:], k.ikg[l], w=["ikg"])
    wfv = k.wfm[l].rearrange("(kc p) c -> p kc c", p=128)
    wtv = k.wtm[l].rearrange("(kc p) c -> p kc c", p=128)
    st = {"w": 0, "h": 0, "o": 0, "q": 0}

    def load_w(view, c0, n):
        s = st["w"] % 2
        st["w"] += 1
        for kc in range(8):
            f = kc % 2
            P.dma("sp", ("wf", f), wf[f][:, :n], view[:, kc, c0:c0 + n], w=[("wf", f)])
            P.I("pool", "tensor_copy", out=wb[s][:, kc, :n], in_=wf[f][:, :n], r=[("wf", f)], w=[("wb", s, kc)])
        return s

    def load_h(tb, need_tab):
        s = st["h"] % 2
        st["h"] += 1
        T = slice(tb * 512, (tb + 1) * 512)
        P.dma("sp", ("hblk", s), hb[s][:], k.hT[:, :, T], w=[("hblk", s)])
        if need_tab:
            P.dma("sp", ("ca", s), ca[s][:], k.tab[0, :, T], w=[("ca", s)])
            P.dma("sp", ("sa", s), sa[s][:], k.tab[1, :, T], w=[("sa", s)])
            P.dma("sp", ("ci", s), ci[s][:], k.tab[2, 0:16, T], w=[("ci", s)])
            P.dma("sp", ("si", s), si[s][:], k.tab[3, 0:16, T], w=[("si", s)])
            P.I("dve", "tensor_scalar", out=cas[s][:], in0=ca[s][:], scalar1=128 ** -0.5, scalar2=None, op0=ALU.mult,
                r=[("ca", s)], w=[("cas", s)])
            P.I("dve", "tensor_scalar", out=sas[s][:], in0=sa[s][:], scalar1=128 ** -0.5, scalar2=None, op0=ALU.mult,
                r=[("sa", s)], w=[("sas", s)])
        return s

    def qname():
        st["q"] += 1
        return "sp" if st["q"] % 2 == 0 else "pool"

    def rope(nrot, bankm, bm, banksw, bs, cos_t, ckey, sin_t, skey, scale, o):
        P.I("dve", "tensor_tensor", out=t1[0:nrot, :], in0=bankm[0:nrot, :], in1=cos_t[0:nrot, :], op=ALU.mult,
            r=[("pb", bm), ckey], w=["t1"])
        P.I("dve", "tensor_tensor", out=t2[0:nrot, :], in0=banksw[0:nrot, :], in1=sin_t[0:nrot, :], op=ALU.mult,
            r=[("pb", bs), skey], w=["t2"])
        P.I("dve", "tensor_tensor", out=obf[o][0:nrot, :], in0=t1[0:nrot, :], in1=t2[0:nrot, :], op=ALU.add,
            r=["t1", "t2"], w=[("obf", o)])

    import os
    for gi, grp in enumerate(FM_GROUPS):
        gc0 = sum(b[2] + b[3] for g in FM_GROUPS[:gi] for b in g)
        gn = sum(b[2] + b[3] for b in grp)
        ws = load_w(wfv, gc0, gn)
        wk = [("wb", ws, kc) for kc in range(8)]
        for tb in range(NB):
            hs = load_h(tb, gi == 0)
            T = slice(tb * 512, (tb + 1) * 512)
            c0 = 0
            for (kind, idx, nm, ns) in grp:
                bm, bankm = next_bank(k)
                for kc in range(8):
                    P.pe(lambda e, ws=ws, hs=hs, kc=kc, c0=c0, nm=nm, bankm=bankm: e.matmul(
                        bankm[0:nm, :], lhsT=wb[ws][:, kc, c0:c0 + nm], rhs=hb[hs][:, kc, :],
                        start=(kc == 0), stop=(kc == 7)), r=[wk[kc], ("hblk", hs)], w=[("pb", bm)])
                bs, banksw = None, None
                if ns and os.environ.get("MK_QA", "full") != "copy":
                    bs, banksw = next_bank(k)
                    for kc in range(8):
                        P.pe(lambda e, ws=ws, hs=hs, kc=kc, c0=c0, nm=nm, ns=ns, banksw=banksw: e.matmul(
                            banksw[0:ns, :], lhsT=wb[ws][:, kc, c0 + nm:c0 + nm + ns], rhs=hb[hs][:, kc, :],
                            start=(kc == 0), stop=(kc == 7)), r=[wk[kc], ("hblk", hs)], w=[("pb", bs)])
                c0 += nm + ns
                o = st["o"] % 3
                st["o"] += 1
                if kind in ("qa", "ka"):
                    scale = 128 ** -0.5 if kind == "qa" else 1.0
                    P.act(lambda e, o=o, bankm=bankm, scale=scale: e.activation(
                        out=obf[o][:, :], in_=bankm[:, :], func=AF.Copy, scale=scale),
                        r=[("pb", bm)], w=[("obf", o)])
                    if os.environ.get("MK_QA", "full") == "full":
                        if kind == "qa":
                            rope(32, bankm, bm, banksw, bs, cas[hs], ("cas", hs), sas[hs], ("sas", hs), scale, o)
                        else:
                            rope(32, bankm, bm, banksw, bs, ca[hs], ("ca", hs), sa[hs], ("sa", hs), scale, o)
                    dst = (k.qT if kind == "qa" else k.kT)[idx, :, T]
                    P.dma(qname(), ("obf", o), dst, obf[o][:, :], r=[("obf", o)])
                elif kind == "qi":
                    P.act(lambda e, o=o, bankm=bankm: e.copy(out=obf[o][0:64, :], in_=bankm[0:64, :]),
                          r=[("pb", bm)], w=[("obf", o)])
                    rope(16, bankm, bm, banksw, bs, ci[hs], ("ci", hs), si[hs], ("si", hs), 1.0, o)
                    P.dma(qname(), ("obf", o), k.qiT[idx, :, T], obf[o][0:64, :], r=[("obf", o)])
                elif kind == "ki":
                    P.act(lambda e, bankm=bankm: e.activation(out=sq[:, :], in_=bankm[0:64, :], func=AF.Square),
                          r=[("pb", bm)], w=["sq"])
                    bq, bankq = next_bank(k)
                    P.pe(lambda e, bankq=bankq: e.matmul(bankq[0:64, :], lhsT=k.ones_f[0:64, 0:64], rhs=sq[:, :],
                                                         start=True, stop=True), r=["sq"], w=[("pb", bq)])
                    P.act(lambda e, bankq=bankq: e.activation(out=rinv[:, :], in_=bankq[0:64, :], func=AF.Sqrt,
                                                              bias=k.cst[0:64, 4:5], scale=1.0 / 64),
                          r=[("pb", bq)], w=["rinv"])
                    P.dve(lambda e: e.reciprocal(out=rinv[:, :], in_=rinv[:, :]), r=["rinv"], w=["rinv"])
                    P.dve(lambda e, bankm=bankm: e.scalar_tensor_tensor(
                        out=kn[:, :], in0=bankm[0:64, :], scalar=ikg[:, 0:1], in1=rinv[:, :],
                        op0=ALU.mult, op1=ALU.mult), r=[("pb", bm), "rinv", "ikg"], w=["kn"])
                    P.dve(lambda e, banksw=banksw: e.scalar_tensor_tensor(
                        out=ksw[:, :], in0=banksw[0:16, :], scalar=ikg[0:16, 1:2], in1=rinv[0:16, :],
                        op0=ALU.mult, op1=ALU.mult), r=[("pb", bs), "rinv", "ikg"], w=["ksw"])
                    P.act(lambda e, o=o: e.copy(out=obf[o][0:64, :], in_=kn[:, :]), r=["kn"], w=[("obf", o)])
                    P.dve(lambda e, hs=hs: e.tensor_tensor(out=t1[0:16, :], in0=kn[0:16, :], in1=ci[hs][:, :],
                                                           op=ALU.mult), r=["kn", ("ci", hs)], w=["t1"])
                    P.dve(lambda e, hs=hs: e.tensor_tensor(out=t2[0:16, :], in0=ksw[:, :], in1=si[hs][:, :],
                                                           op=ALU.mult), r=["ksw", ("si", hs)], w=["t2"])
                    P.dve(lambda e, o=o: e.tensor_tensor(out=obf[o][0:16, :], in0=t1[0:16, :], in1=t2[0:16, :],
                                                         op=ALU.add), r=["t1", "t2"], w=[("obf", o)])
                    P.dma(qname(), ("obf", o), k.kiT[:, T], obf[o][0:64, :], r=[("obf", o)])
                else:
                    if o % 2 == 0:
                        P.act(lambda e, o=o, bankm=bankm: e.copy(out=off[o][:, :], in_=bankm[:, :]),
                              r=[("pb", bm)], w=[("off", o)])
                    else:
                        P.dve(lambda e, o=o, bankm=bankm: e.tensor_copy(out=off[o][:, :], in_=bankm[:, :]),
                              r=[("pb", bm)], w=[("off", o)])
                    P.dma(qname(), ("off", o), k.gT[idx, :, T], off[o][:, :], r=[("off", o)])

    for gi, grp in enumerate(TM_GROUPS):
        gc0 = sum(c[2] for g in TM_GROUPS[:gi] for c in g)
        gn = sum(c[2] for c in grp)
        ws = load_w(wtv, gc0, gn)
        wk = [("wb", ws, kc) for kc in range(8)]
        for tb in range(NB):
            hs = load_h(tb, False)
            for tt in range(4):
                R = slice(tb * 512 + tt * 128, tb * 512 + (tt + 1) * 128)
                c0 = 0
                for (kind, csrc, n) in grp:
                    bm, bankm = next_bank(k)
                    for kc in range(8):
                        P.pe(lambda e, ws=ws, hs=hs, kc=kc, c0=c0, n=n, tt=tt, bankm=bankm: e.matmul(
                            bankm[:, 0:n], lhsT=hb[hs][:, kc, tt * 128:(tt + 1) * 128], rhs=wb[ws][:, kc, c0:c0 + n],
                            start=(kc == 0), stop=(kc == 7)), r=[wk[kc], ("hblk", hs)], w=[("pb", bm)])
                    c0 += n
                    o = st["o"] % 3
                    st["o"] += 1
                    if kind == "v":
                        P.act(lambda e, o=o, bankm=bankm, n=n: e.copy(out=obf[o][:, 0:n], in_=bankm[:, 0:n]),
                              r=[("pb", bm)], w=[("obf", o)])
                        P.dma(qname(), ("obf", o), k.v[R, :], obf[o][:, 0:n], r=[("obf", o)])
                        continue
                    if kind in ("za", "zb"):
                        P.act(lambda e, o=o, bankm=bankm, n=n: e.activation(out=off[o][:, 0:n], in_=bankm[:, 0:n],
                                                                            func=AF.Silu),
                              r=[("pb", bm)], w=[("off", o)])
                        dst = (k.za[R, csrc - 1536:csrc - 1536 + n] if kind == "za"
                               else k.zb[R, csrc - 6216:csrc - 6216 + n])
                    elif kind == "gt":
                        P.act(lambda e, o=o, bankm=bankm, n=n: e.activation(out=off[o][:, 0:n], in_=bankm[:, 0:n],
                                                                            func=AF.Sigmoid),
                              r=[("pb", bm)], w=[("off", o)])
                        dst = k.gt[R, csrc - 7256:csrc - 7256 + n]
                    elif kind == "wi":
                        P.dve(lambda e, o=o, bankm=bankm, n=n: e.tensor_scalar(
                            out=off[o][:, 0:n], in0=bankm[:, 0:n], scalar1=IDX_SCALE, scalar2=None, op0=ALU.mult),
                            r=[("pb", bm)], w=[("off", o)])
                        dst = k.wi[R, :]
                    else:
                        P.dve(lambda e, o=o, bankm=bankm, n=n: e.tensor_copy(out=off[o][:, 0:n], in_=bankm[:, 0:n]),
                              r=[("pb", bm)], w=[("off", o)])
                        dst = k.ba[R, :]
                    P.dma(qname(), ("off", o), dst, off[o][:, 0:n], r=[("off", o)])
    P.end_phase()


def phase_gdn(k, l):
    P, SEQ = k.P, k.SEQ
    NS = SEQ // 512
    DKS = 128 ** -0.5
    P.begin_phase()
    U, SU, MBu, MBl, I64 = (k.m64[:, i, :] for i in range(5))

    def bcn(a):
        return a.unsqueeze(1).to_broadcast([64, 8, 64])

    def bci(a, n=64):
        return a.unsqueeze(2).to_broadcast([64, a.shape[1], n])

    def v3(a, n):
        return a.rearrange("p (c n) -> p c n", n=n)

    rb = {"i": 0}

    def nb4():
        i = rb["i"] % 4
        rb["i"] += 1
        return i, k.banks[i]

    def T(name, shape, dt=F32):
        return P.sb(name, shape, dt)

    convw = T("convw", [128, 24, 4])
    gg = T("gg", [64, 128])
    alog = T("alog", [64, 8])
    dtb = T("dtb", [64, 8])
    negA = T("negA", [64, 8])
    P.dma("sp", "g_convw", convw[:], k.convw[l], w=["convw"])
    P.dma("sp", "g_gg", gg[:], k.gdng[l].partition_broadcast(64), w=["gg"])
    P.dma("sp", "g_alog", alog[:], k.alog[l].partition_broadcast(64), w=["alog"])
    P.dma("sp", "g_dtb", dtb[:], k.dtb[l].partition_broadcast(64), w=["dtb"])
    P.I("act", "activation", out=negA[:], in_=alog[:], func=AF.Exp, r=["alog"], w=["negA"])
    P.I("dve", "tensor_scalar", out=negA[:], in0=negA[:], scalar1=-1.0, scalar2=None, op0=ALU.mult,
        r=["negA"], w=["negA"])
    S = [[T("S", [128, 128]) for _ in range(2)] for _ in range(8)]
    Sb = [[T("Sb", [128, 128], BF16) for _ in range(2)] for _ in range(8)]
    for h in range(8):
        P.I("pool", "memset", S[h][0][:], 0.0, w=[("S", h, 0)])
        P.I("pool", "memset", Sb[h][0][:], 0.0, w=[("Sb", h, 0)])
    bat = T("bat", [64, 8, 16])
    beta = T("beta", [64, 8, 8])
    xa = T("xa", [64, 8, 8])
    gall = T("gall", [64, 8, 8])
    eps64 = k.cst[0:64, 4:5]
    one64 = k.cst[0:64, 5:6]

    def make_set(sid, b6i, b7i):
        B6, B7 = k.banks[b6i], k.banks[b7i]
        K6, K7 = ("pb", b6i), ("pb", b7i)

        def N(x):
            return (sid, x)

        xin = [T("xin", [128, 515]) for _ in range(3)]
        y = [T("y", [128, 512]) for _ in range(3)]
        sqt = T("sqt", [128, 512], BF16)
        yvb = T("yvb", [128, 512], BF16)
        ctmp = T("ctmp", [128, 512])
        rin = [T("rin", [128, 512]) for _ in range(2)]
        qn = T("qn", [128, 512], BF16)
        kn = T("kn", [128, 512], BF16)
        egbc = ctmp
        qd = T("qd", [128, 512], BF16)
        gh = T("gh", [64, 8])
        bh = T("bh", [64, 8])
        nbh = T("nbh", [64, 8])
        gcs = T("gcs", [64, 8])
        dgl = T("dgl", [64, 8])
        egl = T("egl", [64, 8])
        eg = T("eg", [64, 8])
        beg = T("beg", [64, 8])
        gtot = T("gtot", [128, 8])
        Gm = T("Gm", [64, 8, 64])
        d3 = T("d3", [64, 8, 64])
        DTu = T("DTu", [64, 8, 64])
        DTl = T("DTl", [64, 8, 64])
        Bm = T("Bm", [64, 8, 64], BF16)
        Mfac = T("Mfac", [64, 8, 64])
        Nm = [T("Nm", [64, 8, 64], BF16) for _ in range(2)]
        NTm = [T("NTm", [64, 8, 64], BF16) for _ in range(2)]
        Rm = [T("Rm", [64, 8, 64], BF16) for _ in range(2)]
        qkT = T("qkT", [64, 8, 64], BF16)
        kbg = T("kbg", [64, 8, 128], BF16)
        kdec = T("kdec", [64, 8, 128], BF16)
        vb = T("vb", [64, 8, 128], BF16)
        us = T("us", [64, 8, 128])
        os_ = T("os", [64, 8, 128])
        sq3 = T("sq3", [64, 8, 128])
        zbt = T("zbt", [64, 8, 128])
        y1 = T("y1", [64, 8, 128])
        wTs = T("wTs", [128, 512], BF16)
        vnew = T("vnew", [64, 128], BF16)
        ssq = T("ssq", [64, 8])
        rinv8 = T("rinv8", [64, 8])

        def unit(s, h):
            t0 = s * 512
            for i, blk in enumerate((h, 8 + h, 16 + h)):
                if s == 0:
                    P.I("pool", "memset", xin[i][:, 0:3], 0.0, w=[N(("xin", i))])
                    P.dma("sp", (sid, "g_xin", i), xin[i][:, 3:515], k.gT[blk, :, 0:512], w=[N(("xin", i))])
                else:
                    P.dma("sp", (sid, "g_xin", i), xin[i][:, :], k.gT[blk, :, t0 - 3:t0 + 512], w=[N(("xin", i))])
                P.I("act", "activation", out=y[i][:], in_=xin[i][:, 0:512], func=AF.Copy, scale=convw[:, blk, 0:1],
                    r=[N(("xin", i)), "convw"], w=[N(("y", i))])
                for j in range(1, 4):
                    P.I("dve", "scalar_tensor_tensor", out=y[i][:], in0=xin[i][:, j:j + 512],
                        scalar=convw[:, blk, j:j + 1], in1=y[i][:], op0=ALU.mult, op1=ALU.add,
                        r=[N(("xin", i)), "convw", N(("y", i))], w=[N(("y", i))])
                if i < 2:
                    P.I("act", "activation", out=y[i][:], in_=y[i][:], func=AF.Silu, r=[N(("y", i))],
                        w=[N(("y", i))])
                else:
                    P.I("act", "activation", out=yvb[:], in_=y[i][:], func=AF.Silu, r=[N(("y", i))], w=[N("yvb")])
                yield
            for i in range(2):
                P.I("act", "activation", out=sqt[:], in_=y[i][:], func=AF.Square, r=[N(("y", i))], w=[N("sqt")])
                bi, bk = nb4()
                P.I("pe", "matmul", bk[:, :], lhsT=k.ones_b[:, :], rhs=sqt[:], start=True, stop=True,
                    r=[N("sqt")], w=[("pb", bi)])
                P.I("act", "activation", out=rin[i][:], in_=bk[:, :], func=AF.Sqrt, bias=k.cst[:, 4:5],
                    r=[("pb", bi)], w=[N(("rin", i))])
                P.I("dve", "reciprocal", out=rin[i][:], in_=rin[i][:], r=[N(("rin", i))], w=[N(("rin", i))])
            P.I("dve", "scalar_tensor_tensor", out=qn[:], in0=y[0][:], scalar=DKS, in1=rin[0][:], op0=ALU.mult,
                op1=ALU.mult, r=[N(("y", 0)), N(("rin", 0))], w=[N("qn")])
            P.I("dve", "tensor_tensor", out=kn[:], in0=y[1][:], in1=rin[1][:], op=ALU.mult,
                r=[N(("y", 1)), N(("rin", 1))], w=[N("kn")])
            yield
            P.I("dve", "tensor_copy", out=gh[:], in_=gall[:, :, h], r=["gall"], w=[N("gh")])
            P.I("dve", "tensor_copy", out=bh[:], in_=beta[:, :, h], r=["beta"], w=[N("bh")])
            P.I("dve", "tensor_scalar", out=nbh[:], in0=beta[:, :, h], scalar1=-1.0, scalar2=None, op0=ALU.mult,
                r=["beta"], w=[N("nbh")])
            P.I("pe", "matmul", B6[0:64, 0:8], lhsT=U, rhs=gh[:], start=True, stop=True, r=[N("gh")], w=[K6])
            P.I("pe", "matmul", B6[:, 8:16], lhsT=k.ones_f[0:64, :], rhs=gh[:], start=True, stop=True,
                r=[N("gh")], w=[K6])
            P.I("dve", "tensor_copy", out=gcs[:], in_=B6[0:64, 0:8], r=[K6], w=[N("gcs")])
            P.I("dve", "tensor_tensor", out=dgl[:], in0=B6[0:64, 8:16], in1=gcs[:], op=ALU.subtract,
                r=[K6, N("gcs")], w=[N("dgl")])
            P.I("act", "activation", out=egl[:], in_=dgl[:], func=AF.Exp, r=[N("dgl")], w=[N("egl")])
            P.I("act", "activation", out=eg[:], in_=gcs[:], func=AF.Exp, r=[N("gcs")], w=[N("eg")])
            P.I("act", "activation", out=gtot[:], in_=B6[:, 8:16], func=AF.Exp, r=[K6], w=[N("gtot")])
            P.I("dve", "tensor_tensor", out=beg[:], in0=bh[:], in1=eg[:], op=ALU.mult, r=[N("bh"), N("eg")],
                w=[N("beg")])
            P.I("dve", "tensor_tensor", out=Gm[:], in0=bcn(U), in1=bci(gh[:]), op=ALU.mult, r=[N("gh")], w=[N("Gm")])
            bB, bkB = nb4()
            P.I("pe", "matmul", bkB[:, :], lhsT=k.ones_f[0:64, :], rhs=Gm[:].rearrange("p c n -> p (c n)"),
                start=True, stop=True, r=[N("Gm")], w=[("pb", bB)])
            P.I("act", "activation", out=egbc[:], in_=bkB[:, :], func=AF.Exp, r=[("pb", bB)], w=[N("ctmp")])
            P.I("dve", "tensor_tensor", out=qd[:], in0=qn[:], in1=egbc[:], op=ALU.mult, r=[N("qn"), N("ctmp")],
                w=[N("qd")])
            P.I("dve", "tensor_tensor", out=d3[:], in0=v3(bkB[0:64, :], 64), in1=bci(gcs[:]), op=ALU.subtract,
                r=[("pb", bB), N("gcs")], w=[N("d3")])
            yield
            P.I("dve", "tensor_tensor", out=DTu[:], in0=d3[:], in1=bcn(MBu), op=ALU.add, r=[N("d3")], w=[N("DTu")])
            P.I("act", "activation", out=DTu[:], in_=DTu[:], func=AF.Exp, r=[N("DTu")], w=[N("DTu")])
            P.I("dve", "scalar_tensor_tensor", out=DTl[:], in0=d3[:], scalar=-1.0, in1=bcn(MBl), op0=ALU.mult,
                op1=ALU.add, r=[N("d3")], w=[N("DTl")])
            P.I("act", "activation", out=DTl[:], in_=DTl[:], func=AF.Exp, r=[N("DTl")], w=[N("DTl")])
            P.I("dve", "tensor_tensor", out=DTl[:], in0=DTl[:], in1=bci(nbh[:]), op=ALU.mult,
                r=[N("DTl"), N("nbh")], w=[N("DTl")])
            P.I("dve", "tensor_tensor", out=Bm[:], in0=bcn(I64), in1=bci(nbh[:]), op=ALU.mult, r=[N("nbh")],
                w=[N("Bm")])
            bN, bkN = nb4()
            P.I("pe", "matmul", bkN[0:64, :], lhsT=k.su_b[:, :], rhs=Bm[:].rearrange("p c n -> p (c n)"), start=True,
                stop=True, r=[N("Bm")], w=[("pb", bN)])
            P.I("dve", "tensor_tensor", out=Mfac[:], in0=DTu[:], in1=v3(bkN[0:64, :], 64), op=ALU.mult,
                r=[N("DTu"), ("pb", bN)], w=[N("Mfac")])
            yield
            for half in range(2):
                bi, bk = nb4()
                bkb = bk[:].bitcast(BF16)
                for m in range(4):
                    c = half * 4 + m
                    P.I("pe", "transpose", out=bkb[0:64, m * 128:(m + 1) * 128], in_=kn[:, c * 64:(c + 1) * 64],
                        identity=k.ident_bf[:, :], r=[N("kn")], w=[("pb", bi)])
                hs = slice(half * 4, half * 4 + 4)
                P.I("dve", "tensor_tensor", out=kbg[:, hs, :], in0=v3(bkb[0:64, 0:512], 128),
                    in1=beg[:, hs].unsqueeze(2).to_broadcast([64, 4, 128]), op=ALU.mult,
                    r=[("pb", bi), N("beg")], w=[N(("kbg", half))])
                P.I("dve", "tensor_tensor", out=kdec[:, hs, :], in0=v3(bkb[0:64, 0:512], 128),
                    in1=egl[:, hs].unsqueeze(2).to_broadcast([64, 4, 128]), op=ALU.mult,
                    r=[("pb", bi), N("egl")], w=[N(("kdec", half))])
                bi, bk = nb4()
                bkb = bk[:].bitcast(BF16)
                for m in range(4):
                    c = half * 4 + m
                    P.I("pe", "transpose", out=bkb[0:64, m * 128:(m + 1) * 128], in_=yvb[:, c * 64:(c + 1) * 64],
                        identity=k.ident_bf[:, :], r=[N("yvb")], w=[("pb", bi)])
                P.I("dve", "tensor_tensor", out=vb[:, hs, :], in0=v3(bkb[0:64, 0:512], 128),
                    in1=bh[:, hs].unsqueeze(2).to_broadcast([64, 4, 128]), op=ALU.mult,
                    r=[("pb", bi), N("bh")], w=[N(("vb", half))])
                yield
            bKK, bkKK = nb4()
            for m in range(8):
                cs = slice(m * 64, (m + 1) * 64)
                P.I("pe", "matmul", bkKK[0:64, cs], lhsT=kn[:, cs], rhs=kn[:, cs], start=True, stop=True,
                    r=[N("kn")], w=[("pb", bKK)])
            bKQ, bkKQ = nb4()
            for m in range(8):
                cs = slice(m * 64, (m + 1) * 64)
                P.I("pe", "matmul", bkKQ[0:64, cs], lhsT=kn[:, cs], rhs=qn[:, cs], start=True, stop=True,
                    r=[N("kn"), N("qn")], w=[("pb", bKQ)])
            P.I("dve", "tensor_tensor", out=Nm[0][:], in0=v3(bkKK[0:64, :], 64), in1=Mfac[:], op=ALU.mult,
                r=[("pb", bKK), N("Mfac")], w=[N(("Nm", 0))])
            P.I("dve", "tensor_tensor", out=NTm[0][:], in0=v3(bkKK[0:64, :], 64), in1=DTl[:], op=ALU.mult,
                r=[("pb", bKK), N("DTl")], w=[N(("NTm", 0))])
            P.I("dve", "tensor_tensor", out=qkT[:], in0=v3(bkKQ[0:64, :], 64), in1=DTu[:], op=ALU.mult,
                r=[("pb", bKQ), N("DTu")], w=[N("qkT")])
            P.I("dve", "tensor_tensor", out=Rm[0][:], in0=Nm[0][:], in1=bcn(I64), op=ALU.add,
                r=[N(("Nm", 0))], w=[N(("Rm", 0))])
            yield
            cn, cr = 0, 0
            for it in range(5):
                nn = 1 - cn
                if it < 4:
                    bA, bkA = nb4()
                    for m in range(8):
                        cs = slice(m * 64, (m + 1) * 64)
                        P.I("pe", "matmul", bkA[0:64, cs], lhsT=NTm[cn][:, m, :], rhs=Nm[cn][:, m, :], start=True,
                            stop=True, r=[N(("NTm", cn)), N(("Nm", cn))], w=[("pb", bA)])
                bBt, bkBt = nb4()
                for m in range(8):
                    cs = slice(m * 64, (m + 1) * 64)
                    P.I("pe", "matmul", bkBt[0:64, cs], lhsT=Nm[cn][:, m, :], rhs=NTm[cn][:, m, :], start=True,
                        stop=True, r=[N(("NTm", cn)), N(("Nm", cn))], w=[("pb", bBt)])
                P.I("act", "copy", out=NTm[nn][:], in_=v3(bkBt[0:64, :], 64), r=[("pb", bBt)], w=[N(("NTm", nn))])
                bC, bkC = nb4()
                for m in range(8):
                    cs = slice(m * 64, (m + 1) * 64)
                    P.I("pe", "matmul", bkC[0:64, cs], lhsT=NTm[nn][:, m, :], rhs=Rm[cr][:, m, :], start=True,
                        stop=True, r=[N(("NTm", nn)), N(("Rm", cr))], w=[("pb", bC)])
                P.I("dve", "tensor_tensor", out=Rm[1 - cr][:], in0=Rm[cr][:], in1=v3(bkC[0:64, :], 64), op=ALU.add,
                    r=[N(("Rm", cr)), ("pb", bC)], w=[N(("Rm", 1 - cr))])
                cr = 1 - cr
                if it < 4:
                    P.I("act", "copy", out=Nm[nn][:], in_=v3(bkA[0:64, :], 64), r=[("pb", bA)], w=[N(("Nm", nn))])
                cn = nn
                yield
            R = Rm[cr]
            rk = N(("Rm", cr))
            for half in range(2):
                bi, bk = nb4()
                for m in range(4):
                    c = half * 4 + m
                    P.I("pe", "matmul", bk[0:64, m * 128:(m + 1) * 128], lhsT=R[:, c, :], rhs=vb[:, c, :], start=True,
                        stop=True, r=[rk, N(("vb", half))], w=[("pb", bi)])
                P.I("act", "copy", out=us[:, half * 4:half * 4 + 4, :], in_=v3(bk[0:64, :], 128),
                    r=[("pb", bi)], w=[N(("us", half))])
            bW, bkW = nb4()
            for m in range(8):
                P.I("pe", "matmul", bkW[:, m * 64:(m + 1) * 64], lhsT=kbg[:, m, :], rhs=R[:, m, :], start=True,
                    stop=True, r=[rk, N(("kbg", m // 4))], w=[("pb", bW)])
            P.I("dve", "tensor_copy", out=wTs[:], in_=bkW[:, :], r=[("pb", bW)], w=[N("wTs")])
            yield
            for m in range(8):
                cur = (s * 8 + m) % 2
                Sc, Sn = S[h][cur], S[h][1 - cur]
                Sbc, Sbn = Sb[h][cur], Sb[h][1 - cur]
                cs = slice(m * 64, (m + 1) * 64)
                P.I("pe", "matmul", B6[0:64, 128:256], lhsT=wTs[:, cs], rhs=Sbc[:], start=True, stop=True,
                    r=[N("wTs"), ("Sb", h, cur)], w=[K6])
                P.I("dve", "tensor_tensor", out=vnew[:], in0=us[:, m, :], in1=B6[0:64, 128:256], op=ALU.subtract,
                    r=[N(("us", m // 4)), K6], w=[N("vnew")])
                oc = slice((m % 4) * 128, (m % 4 + 1) * 128)
                P.I("pe", "matmul", B7[0:64, oc], lhsT=qd[:, cs], rhs=Sbc[:], start=True, stop=False,
                    r=[N("qd"), ("Sb", h, cur)], w=[K7])
                P.I("pe", "matmul", B7[0:64, oc], lhsT=qkT[:, m, :], rhs=vnew[:], start=False, stop=True,
                    r=[N("qkT"), N("vnew")], w=[K7])
                P.I("pe", "matmul", B6[:, 256:384], lhsT=kdec[:, m, :], rhs=vnew[:], start=True, stop=True,
                    r=[N(("kdec", m // 4)), N("vnew")], w=[K6])
                P.I("dve", "scalar_tensor_tensor", out=Sbn[:], in0=Sc[:], scalar=gtot[:, m:m + 1],
                    in1=B6[:, 256:384], op0=ALU.mult, op1=ALU.add, r=[("S", h, cur), N("gtot"), K6],
                    w=[("Sb", h, 1 - cur)])
                P.I("dve", "scalar_tensor_tensor", out=Sn[:], in0=Sc[:], scalar=gtot[:, m:m + 1], in1=B6[:, 256:384],
                    op0=ALU.mult, op1=ALU.add, r=[("S", h, cur), N("gtot"), K6], w=[("S", h, 1 - cur)])
                if m % 4 == 3:
                    P.I("act", "copy", out=os_[:, m - 3:m + 1, :], in_=v3(B7[0:64, :], 128), r=[K7],
                        w=[N(("os", m // 4))])
                yield
            okeys = [N(("os", 0)), N(("os", 1))]
            kb2 = [N(("kbg", 0)), N(("kbg", 1))]
            vb2 = [N(("vb", 0)), N(("vb", 1))]
            kd2 = [N(("kdec", 0)), N(("kdec", 1))]
            kb2, vb2, kd2 = [N("sq3")], [N("zbt")], [N("y1")]
            P.I("dve", "tensor_tensor", out=sq3[:], in0=os_[:], in1=os_[:], op=ALU.mult, r=okeys, w=kb2)
            P.I("dve", "tensor_reduce", out=ssq[:], in_=sq3[:], axis=AX.X, op=ALU.add, r=kb2, w=[N("ssq")])
            P.I("act", "activation", out=rinv8[:], in_=ssq[:], func=AF.Sqrt, bias=eps64, scale=1.0 / 128,
                r=[N("ssq")], w=[N("rinv8")])
            P.I("dve", "reciprocal", out=rinv8[:], in_=rinv8[:], r=[N("rinv8")], w=[N("rinv8")])
            P.dma("sp", (sid, "g_zbt"), zbt[:],
                  k.zb[t0:t0 + 512, h * 128:(h + 1) * 128].rearrange("(n p) c -> p n c", p=64), w=vb2)
            P.I("dve", "tensor_tensor", out=y1[:], in0=os_[:], in1=bci(rinv8[:], 128), op=ALU.mult,
                r=okeys + [N("rinv8")], w=kd2)
            P.I("pool", "tensor_tensor", out=y1[:], in0=y1[:], in1=gg[:].unsqueeze(1).to_broadcast([64, 8, 128]),
                op=ALU.mult, r=kd2 + ["gg"], w=kd2)
            P.I("pool", "tensor_tensor", out=y1[:], in0=y1[:], in1=zbt[:], op=ALU.mult, r=kd2 + vb2, w=kd2)
            P.dma("pool", (sid, "g_yb"), k.yb[t0:t0 + 512, h * 128:(h + 1) * 128].rearrange("(n p) c -> p n c", p=64),
                  y1[:], r=kd2)
            yield
        return unit

    units = [make_set(0, 6, 7), make_set(1, 4, 5)]
    for s in range(NS):
        t0 = s * 512
        P.dma("sp", "g_bat", bat[:], k.ba[t0:t0 + 512, :].rearrange("(n p) c -> p n c", p=64), w=["bat"])
        P.I("act", "activation", out=beta[:], in_=bat[:, :, 0:8], func=AF.Sigmoid, r=["bat"], w=["beta"])
        P.I("dve", "tensor_tensor", out=xa[:], in0=bat[:, :, 8:16], in1=dtb[:].unsqueeze(1).to_broadcast([64, 8, 8]),
            op=ALU.add, r=["bat", "dtb"], w=["xa"])
        P.I("act", "activation", out=xa[:], in_=xa[:], func=AF.Exp, r=["xa"], w=["xa"])
        P.I("act", "activation", out=xa[:], in_=xa[:], func=AF.Ln, bias=one64, r=["xa"], w=["xa"])
        P.I("dve", "tensor_tensor", out=gall[:], in0=xa[:], in1=negA[:].unsqueeze(1).to_broadcast([64, 8, 8]),
            op=ALU.mult, r=["xa", "negA"], w=["gall"])
        for h0 in range(0, 8, 2):
            gens = [units[0](s, h0), units[1](s, h0 + 1)]
            while gens:
                for g_ in list(gens):
                    try:
                        next(g_)
                    except StopIteration:
                        gens.remove(g_)
    P.end_phase()


def phase_attn(k, l):
    P, SEQ, NT = k.P, k.SEQ, k.NT
    H2 = SEQ // 2
    P.begin_phase()

    def T(name, shape, dt=F32):
        return P.sb(name, shape, dt)

    kT = T("kT", [128, 2, SEQ], BF16)
    V = T("V", [128, NT, 256], BF16)
    kiT = T("kiT", [128, H2], BF16)
    Isc = [T("Isc", [128, SEQ]) for _ in range(2)]
    mb = [T("mb", [128, SEQ], BF16) for _ in range(2)]
    qt = [T("qt", [128, 8, 128], BF16) for _ in range(2)]
    qit = [T("qit", [128, 8, 128], BF16) for _ in range(2)]
    wit = [T("wit", [128, 8]) for _ in range(2)]
    rl = [T("rl", [128, 512]) for _ in range(2)]
    pt = [T("pt", [128, 512], BF16) for _ in range(3)]
    osb = T("osb", [128, 8, 128])
    st8 = [T("st8", [128, 8]) for _ in range(2)]
    rden = T("rden", [128, 8])
    for g in range(2):
        P.dma("sp", ("a_kT", g), kT[:, g, :], k.kT[g, :, :], w=["kT"])
    vv = k.v.rearrange("(t p) c -> p t c", p=128)
    for t0 in range(0, NT, 8):
        P.dma("sp", ("a_V", t0), V[:, t0:t0 + 8, :], vv[:, t0:t0 + 8, :], w=[("V", t0)])
    P.dma("sp", "a_kiT0", kiT[0:64, :], k.kiT[:, 0:H2], w=["kiT"])
    P.dma("sp", "a_kiT1", kiT[64:128, :], k.kiT[:, H2:SEQ], w=["kiT"])
    BI = [k.banks[0], k.banks[1]]
    BS = [k.banks[2], k.banks[3]]
    BO = [k.banks[4], k.banks[5]]
    BD = k.banks[6]
    ci = {"i": 0, "s": 0, "r": 0, "p": 0}
    topk = float(min(TOPK, SEQ // 4))

    def PRE(j):
        s = j % 2
        L = (j + 1) * 128
        Tq = slice(j * 128, (j + 1) * 128)
        I_, M_, S8 = Isc[s], mb[s], st8[s]
        IK, MK = ("I", s), ("mb", s)
        P.dma("sp", ("a_qt", s), qt[s][:], k.qT[:, :, Tq].rearrange("h d t -> d h t"), w=[("qt", s)])
        P.dma("sp", ("a_qit0", s), qit[s][0:64], k.qiT[:, :, Tq].rearrange("h d t -> d h t"), w=[("qit", s)])
        P.dma("sp", ("a_qit1", s), qit[s][64:128], k.qiT[:, :, Tq].rearrange("h d t -> d h t"), w=[("qit", s)])
        P.dma("sp", ("a_wit", s), wit[s][:], k.wi[Tq, :], w=[("wit", s)])
        nblk = (L + 511) // 512
        for kb in range(nblk):
            w_ = min(512, L - kb * 512)
            cs = slice(kb * 512, kb * 512 + w_)
            if kb * 512 < H2:
                pr, kc = slice(0, 64), slice(kb * 512, kb * 512 + w_)
            else:
                pr, kc = slice(64, 128), slice(kb * 512 - H2, kb * 512 - H2 + w_)
            for h in range(8):
                b = ci["i"] % 2
                ci["i"] += 1
                P.I("pe", "matmul", BI[b][:, 0:w_], lhsT=qit[s][pr, h, :], rhs=kiT[pr, kc], start=True, stop=True,
                    r=[("qit", s), "kiT"], w=[("bi", b)])
                r_ = ci["r"] % 2
                ci["r"] += 1
                P.I("act", "activation", out=rl[r_][:, 0:w_], in_=BI[b][:, 0:w_], func=AF.Relu,
                    r=[("bi", b)], w=[("rl", r_)])
                if h == 0:
                    P.I("dve", "tensor_scalar", out=I_[:, cs], in0=rl[r_][:, 0:w_], scalar1=wit[s][:, 0:1],
                        scalar2=None, op0=ALU.mult, r=[("rl", r_), ("wit", s)], w=[IK])
                else:
                    P.I("dve", "scalar_tensor_tensor", out=I_[:, cs], in0=rl[r_][:, 0:w_], scalar=wit[s][:, h:h + 1],
                        in1=I_[:, cs], op0=ALU.mult, op1=ALU.add, r=[("rl", r_), ("wit", s), IK], w=[IK])
        sk = lambda n: ("st", s, n)
        P.I("dve", "tensor_reduce", out=S8[:, 0:1], in_=I_[:, 0:L], axis=AX.X, op=ALU.max, r=[IK], w=[sk(0)])
        P.I("dve", "tensor_reduce", out=S8[:, 1:2], in_=I_[:, 0:L], axis=AX.X, op=ALU.min, r=[IK], w=[sk(1)])
        P.I("dve", "tensor_tensor", out=S8[:, 2:3], in0=S8[:, 0:1], in1=S8[:, 1:2], op=ALU.subtract,
            r=[sk(0), sk(1)], w=[sk(2)])
        P.I("dve", "tensor_scalar", out=S8[:, 2:3], in0=S8[:, 2:3], scalar1=1e-20, scalar2=None, op0=ALU.add,
            r=[sk(2)], w=[sk(2)])
        P.I("dve", "reciprocal", out=S8[:, 2:3], in_=S8[:, 2:3], r=[sk(2)], w=[sk(2)])
        P.I("dve", "tensor_scalar", out=I_[:, 0:L], in0=I_[:, 0:L], scalar1=S8[:, 1:2], scalar2=S8[:, 2:3],
            op0=ALU.subtract, op1=ALU.mult, r=[IK, sk(1), sk(2)], w=[IK])
        P.I("dve", "tensor_tensor", out=I_[:, L - 128:L], in0=I_[:, L - 128:L], in1=k.tri[:, :], op=ALU.add,
            r=[IK], w=[IK])
        P.I("dve", "memset", S8[:, 3:4], 0.5, w=[sk(3)])
        for it in range(NBIS):
            wk = 2.0 ** -(it + 2)
            P.I("dve", "tensor_scalar", out=Isc[1 - s][:, 0:L], in0=I_[:, 0:L], scalar1=S8[:, 3:4], scalar2=0.0,
                op0=ALU.is_ge, op1=ALU.add, accum_out=S8[:, 4:5], r=[IK, sk(3)], w=[("I", 1 - s), sk(4)])
            P.I("dve", "tensor_scalar", out=S8[:, 5:6], in0=S8[:, 4:5], scalar1=topk, scalar2=2.0 * wk,
                op0=ALU.is_ge, op1=ALU.mult, r=[sk(4)], w=[sk(5)])
            P.I("dve", "scalar_tensor_tensor", out=S8[:, 3:4], in0=S8[:, 5:6], scalar=-wk, in1=S8[:, 3:4],
                op0=ALU.add, op1=ALU.add, r=[sk(5), sk(3)], w=[sk(3)])
        P.I("dve", "tensor_scalar", out=S8[:, 6:7], in0=S8[:, 3:4], scalar1=-(2.0 ** -(NBIS + 1)), scalar2=None,
            op0=ALU.add, r=[sk(3)], w=[sk(6)])
        P.I("dve", "tensor_scalar", out=M_[:, 0:L], in0=I_[:, 0:L], scalar1=S8[:, 6:7], scalar2=-30000.0,
            op0=ALU.is_lt, op1=ALU.mult, r=[IK, sk(6)], w=[MK])

    def ATT(j):
        s = j % 2
        nk = j + 1
        Tq = slice(j * 128, (j + 1) * 128)
        M_, MK = mb[s], ("mb", s)
        for g in range(2):
            P.I("pe", "matmul", BO[g][:, :], lhsT=k.zeros_bf[:, 0:128], rhs=k.zeros_bf[:, :], start=True, stop=False,
                w=[("bo", g)])
        P.I("pe", "matmul", BD[:, 0:8], lhsT=k.zeros_bf[:, 0:128], rhs=k.zeros_bf[:, 0:8], start=True, stop=False,
            w=["bd"])
        tiles = [(kt, g) for kt in range(nk) for g in range(2)]
        bsel = {}

        def emit_S(i):
            kt, g = tiles[i]
            ks = slice(kt * 128, (kt + 1) * 128)
            b = ci["s"] % 2
            ci["s"] += 1
            bsel[i] = b
            P.I("pe", "matmul", BS[b][:, :], lhsT=kT[:, g, ks],
                rhs=qt[s][:, 4 * g:4 * g + 4, :].rearrange("p h t -> p (h t)"), start=True, stop=False,
                r=["kT", ("qt", s)], w=[("bs", b)])
            P.I("pe", "matmul", BS[b][:, :], lhsT=M_[:, ks], rhs=k.i4[:, :], start=False, stop=True,
                r=[MK], w=[("bs", b)])

        emit_S(0)
        for i, (kt, g) in enumerate(tiles):
            if i + 1 < len(tiles):
                emit_S(i + 1)
            b = bsel[i]
            last = (kt == nk - 1)
            p_ = ci["p"] % 3
            ci["p"] += 1
            P.I("act", "activation", out=pt[p_][:, :], in_=BS[b][:, :], func=AF.Exp, r=[("bs", b)], w=[("pt", p_)])
            for hh in range(4):
                P.I("pe", "matmul", BO[g][:, hh * 128:(hh + 1) * 128], lhsT=pt[p_][:, hh * 128:(hh + 1) * 128],
                    rhs=V[:, kt, g * 128:(g + 1) * 128], start=False, stop=last,
                    r=[("pt", p_), ("V", (kt // 8) * 8)], w=[("bo", g)])
                P.I("pe", "matmul", BD[:, g * 4 + hh:g * 4 + hh + 1], lhsT=pt[p_][:, hh * 128:(hh + 1) * 128],
                    rhs=k.ones_bf[:, 0:1], start=False, stop=last, r=[("pt", p_)], w=["bd"])
        P.I("dve", "reciprocal", out=rden[:, :], in_=BD[:, 0:8], r=["bd"], w=["rden"])
        for g in range(2):
            P.I("dve", "tensor_tensor", out=osb[:, 4 * g:4 * g + 4, :],
                in0=BO[g][:, :].rearrange("p (h d) -> p h d", d=128),
                in1=rden[:, 4 * g:4 * g + 4].unsqueeze(2).to_broadcast([128, 4, 128]), op=ALU.mult,
                r=[("bo", g), "rden"], w=["osb"])
        P.dma("pool", "a_oa", k.oa[Tq, :], osb[:].rearrange("p h d -> p (h d)"), r=["osb"])

    PRE(0)
    for j in range(NT):
        if j + 1 < NT:
            PRE(j + 1)
        ATT(j)
    P.end_phase()


def phase_C(k, l, src, last):
    P, NT = k.P, k.NT
    P.begin_phase()

    def T(name, shape, dt=F32):
        return P.sb(name, shape, dt)

    wo = T("wo", [128, 8, D], BF16)
    wov = k.wout[l].rearrange("(kc p) c -> p kc c", p=128)
    wof = [T("wof", [128, D]) for _ in range(2)]
    for kc in range(8):
        f = kc % 2
        P.dma("sp", ("c_wof", f), wof[f][:], wov[:, kc, :], w=[("wof", f)])
        P.I("pool", "tensor_copy", out=wo[:, kc, :], in_=wof[f][:], r=[("wof", f)], w=[("wo", kc)])
    fg = T("fg", [128, D])
    if last:
        P.dma("sp", "c_fg", fg[:], k.fgain.partition_broadcast(128), w=["fg"])
    oa = [T("oa", [128, D]) for _ in range(2)]
    za = [T("za", [128, D]) for _ in range(2)]
    gt = [T("gt", [128, 2 * D]) for _ in range(2)]
    yb = [T("yb", [128, D]) for _ in range(2)]
    xt = [T("xt", [128, D]) for _ in range(2)]
    mx = [T("mx", [128, D], BF16) for _ in range(2)]
    mT = [T("mT", [128, 8, 128], BF16) for _ in range(2)]
    xn = [T("xn", [128, D]) for _ in range(2)]
    junk = T("junk", [128, D], BF16)
    ss = T("ss", [128, 2])
    wk = [("wo", kc) for kc in range(8)]
    for i in range(NT):
        s = i % 2
        R = slice(i * 128, (i + 1) * 128)
        P.dma("sp", ("c_oa", s), oa[s][:], k.oa[R, :], w=[("oa", s)])
        P.dma("sp", ("c_za", s), za[s][:], k.za[R, :], w=[("za", s)])
        P.dma("sp", ("c_gt", s), gt[s][:], k.gt[R, :], w=[("gt", s)])
        P.dma("sp", ("c_yb", s), yb[s][:], k.yb[R, :], w=[("yb", s)])
        P.dma("sp", ("c_xt", s), xt[s][:], src[R, :], w=[("xt", s)])
        P.I("pool", "tensor_tensor", out=oa[s][:], in0=oa[s][:], in1=za[s][:], op=ALU.mult,
            r=[("oa", s), ("za", s)], w=[("oa", s)])
        P.I("pool", "tensor_tensor", out=oa[s][:], in0=oa[s][:], in1=gt[s][:, 0:D], op=ALU.mult,
            r=[("oa", s), ("gt", s)], w=[("oa", s)])
        P.I("dve", "tensor_tensor", out=yb[s][:], in0=yb[s][:], in1=gt[s][:, D:2 * D], op=ALU.mult,
            r=[("yb", s), ("gt", s)], w=[("yb", s)])
        P.I("dve", "tensor_tensor", out=mx[s][:], in0=oa[s][:], in1=yb[s][:], op=ALU.add,
            r=[("oa", s), ("yb", s)], w=[("mx", s)])
        bi, bank = next_bank(k)
        pT = bank[:].bitcast(BF16).rearrange("p (c t) -> p c t", t=128)
        for kc in range(8):
            P.I("pe", "transpose", out=pT[:, kc, :], in_=mx[s][:, kc * 128:(kc + 1) * 128], identity=k.ident_bf[:],
                r=[("mx", s)], w=[("pb", bi)])
        P.I("act", "copy", out=mT[s][:], in_=pT, r=[("pb", bi)], w=[("mT", s)])
        for half in range(2):
            bo, bko = next_bank(k)
            for kc in range(8):
                P.I("pe", "matmul", bko[:, :], lhsT=mT[s][:, kc, :], rhs=wo[:, kc, half * 512:(half + 1) * 512],
                    start=(kc == 0), stop=(kc == 7), r=[("mT", s), wk[kc]], w=[("pb", bo)])
            P.I("dve", "tensor_tensor", out=xn[s][:, half * 512:(half + 1) * 512], in0=xt[s][:, half * 512:(half + 1) * 512],
                in1=bko[:, :], op=ALU.add, r=[("xt", s), ("pb", bo)], w=[("xn", s, half)])
        xk = [("xn", s, 0), ("xn", s, 1)]
        if not last:
            P.dma("pool", ("c_st", s), k.x1[R, :], xn[s][:], r=xk)
        else:
            P.I("act", "activation", out=junk[:], in_=xn[s][:], func=AF.Square, accum_out=ss[:, s:s + 1],
                r=xk, w=["junk", ("ss", s)])
            P.I("act", "activation", out=ss[:, s:s + 1], in_=ss[:, s:s + 1], func=AF.Sqrt, bias=k.cst[:, 4:5],
                scale=1.0 / D, r=[("ss", s)], w=[("ss", s)])
            P.I("dve", "reciprocal", out=ss[:, s:s + 1], in_=ss[:, s:s + 1], r=[("ss", s)], w=[("ss", s)])
            P.I("dve", "scalar_tensor_tensor", out=xn[s][:], in0=xn[s][:], scalar=ss[:, s:s + 1], in1=fg[:],
                op0=ALU.mult, op1=ALU.mult, r=xk + [("ss", s), "fg"], w=xk)
            P.dma("pool", ("c_st", s), k.out[R, :], xn[s][:], r=xk)
    P.end_phase()


_CACHE = {}
SEQ_FULL = 8192
NLAYERS = 2
NCORES = 2


def _in_maps(inp, NL):
    consts = host_consts()
    per_layer = {}
    for l in range(NL):
        per_layer["wfm%d" % l] = host_w_fm(inp["w_in"], l)
        per_layer["wtm%d" % l] = host_w_tm(inp["w_in"], l)
        per_layer["wout%d" % l] = np.ascontiguousarray(inp["w_out"][l], dtype=np.float32)
        per_layer["ng%d" % l] = np.ascontiguousarray(inp["norm_gain"][l][None, :], dtype=np.float32)
        cw = np.asarray(inp["conv_w"][l], dtype=np.float32)
        per_layer["convw%d" % l] = np.ascontiguousarray(cw.reshape(4, 24, 128).transpose(2, 1, 0))
        per_layer["gdng%d" % l] = np.ascontiguousarray(inp["gdn_norm_gain"][l][None, :], dtype=np.float32)
        per_layer["alog%d" % l] = np.ascontiguousarray(inp["a_log"][l][None, :], dtype=np.float32)
        per_layer["dtb%d" % l] = np.ascontiguousarray(inp["dt_bias"][l][None, :], dtype=np.float32)
        g = np.asarray(inp["idx_k_gain"][l], dtype=np.float32)
        gs = np.zeros(64, np.float32)
        gs[:8] = g[8:16]
        gs[8:16] = g[0:8]
        per_layer["ikg%d" % l] = np.ascontiguousarray(np.stack([g, gs], 1))
    maps = []
    for c in range(NCORES):
        b = c % 2
        m = {"x": np.ascontiguousarray(inp["x"][b], dtype=np.float32),
             "pos": np.ascontiguousarray(inp["positions"][b:b + 1]).astype(np.int32),
             "fgain": np.ascontiguousarray(np.asarray(inp["final_gain"], dtype=np.float32)[None, :])}
        m.update(per_layer)
        m.update(consts)
        maps.append(m)
    return maps


def kernel(x, positions, norm_gain, w_in, conv_w, a_log, dt_bias, gdn_norm_gain, idx_k_gain, w_out, final_gain):
    inp = {"x": np.asarray(x), "positions": np.asarray(positions), "norm_gain": np.asarray(norm_gain),
           "w_in": np.asarray(w_in), "conv_w": np.asarray(conv_w), "a_log": np.asarray(a_log),
           "dt_bias": np.asarray(dt_bias), "gdn_norm_gain": np.asarray(gdn_norm_gain),
           "idx_k_gain": np.asarray(idx_k_gain), "w_out": np.asarray(w_out), "final_gain": np.asarray(final_gain)}
    B, S, _ = inp["x"].shape
    NL = inp["w_in"].shape[0]
    assert B == 2
    nc = build(S, NL)
    maps = _in_maps(inp, NL)
    res = run_bass_kernel_spmd(nc, maps, core_ids=list(range(NCORES)))
    out = np.stack([np.asarray(res.results[b]["out"], dtype=np.float32) for b in range(2)], axis=0)
    return out
```
